# Optimizing a Trainium2 kernel written in Bass

```python
import math
import jax
import jax.numpy as jnp
from jax import lax
import numpy as np

D_MODEL = 2048
BATCH = 16
SEQ = 2048
DEPTH = 2

HEAD_DIM = 64
SB_HEADS = 8
DIL_PATTERNS = ((128, 1), (512, 4), (2048, 16))
DIL_HEADS_PER_GROUP = 4
DIL_HEADS = DIL_HEADS_PER_GROUP * len(DIL_PATTERNS)
SSM_WIDTH = 512
SSM_GROUP = 16
SSM_GROUPS = SSM_WIDTH // SSM_GROUP
SSM_STATE = 64
DSA_HEADS = 8
IDX_HEADS = 8
IDX_DIM = 64
DSA_TOPK = 256
N_BRANCH = 4
D_FF = 5632
REL_BUCKETS = 32
REL_MAX_DIST = 128
REL_HEADS = DIL_HEADS + DSA_HEADS
Q_BLOCK = 128
DN_ALPHA = (2.0 * DEPTH) ** 0.25
DN_BETA = (8.0 * DEPTH) ** -0.25
LN_EPS = 1e-5
NEG_INF = -1e30
MACARON = 0.5

IN_SPLITS = (3 * SB_HEADS * HEAD_DIM, 3 * DIL_HEADS * HEAD_DIM, SSM_WIDTH, 3 * DSA_HEADS * HEAD_DIM,
             IDX_HEADS * IDX_DIM, IDX_DIM, IDX_HEADS, N_BRANCH * D_MODEL)
IN_COLS = sum(IN_SPLITS)
IN_OFFSETS = tuple(int(o) for o in np.cumsum(IN_SPLITS)[:-1])

kernel_name = "hybrid_gated_mixer_block"


def layer_norm(x, g, b):
    xf = x.astype(jnp.float32)
    mu = jnp.mean(xf, axis=-1, keepdims=True)
    var = jnp.mean(jnp.square(xf - mu), axis=-1, keepdims=True)
    return ((xf - mu) * lax.rsqrt(var + LN_EPS) * g.astype(jnp.float32) + b.astype(jnp.float32)).astype(x.dtype)


def swiglu_ffn(x, w_up, w_down):
    a, b = jnp.split(x @ w_up, 2, axis=-1)
    return (jax.nn.silu(a) * b) @ w_down


def t5_bucket(dist):
    max_exact = REL_BUCKETS // 2
    d = jnp.maximum(dist, 1).astype(jnp.float32)
    large = max_exact + (jnp.log(d / max_exact) / math.log(REL_MAX_DIST / max_exact)
                         * (REL_BUCKETS - max_exact)).astype(jnp.int32)
    large = jnp.minimum(large, REL_BUCKETS - 1)
    return jnp.where(dist < max_exact, dist, large)


def stick_breaking_attention(q, k, v):
    bsz, L, nh, dh = q.shape
    scale = dh ** -0.5
    key_pos = jnp.arange(L)

    def block(i):
        q0 = i * Q_BLOCK
        q_pos = q0 + jnp.arange(Q_BLOCK)
        qb = lax.dynamic_slice_in_dim(q, q0, Q_BLOCK, axis=1)
        z = jnp.einsum('bqhd,bkhd->bhqk', qb, k).astype(jnp.float32) * scale
        strict = key_pos[None, :] < q_pos[:, None]
        log_1mb = jnp.where(strict, jax.nn.log_sigmoid(-z), 0.0)
        suffix = lax.cumsum(log_1mb, axis=3, reverse=True) - log_1mb
        att = jnp.where(strict, jnp.exp(jax.nn.log_sigmoid(z) + suffix), 0.0)
        return jnp.einsum('bhqk,bkhd->bqhd', att, v.astype(jnp.float32)).astype(q.dtype)

    out = lax.map(block, jnp.arange(L // Q_BLOCK))
    return jnp.moveaxis(out, 0, 1).reshape(bsz, L, nh * dh)


def dilated_attention(q, k, v, rel_bias):
    bsz, L, _, dh = q.shape
    hg = DIL_HEADS_PER_GROUP
    scale = dh ** -0.5
    groups = [(q[:, :, g * hg:(g + 1) * hg], k[:, :, g * hg:(g + 1) * hg], v[:, :, g * hg:(g + 1) * hg],
               rel_bias[:, g * hg:(g + 1) * hg], window, dil)
              for g, (window, dil) in enumerate(DIL_PATTERNS)]

    def block(i):
        q0 = i * Q_BLOCK
        q_pos = q0 + jnp.arange(Q_BLOCK)
        outs, lses = [], []
        for qg, kg, vg, bg, window, dil in groups:
            steps = jnp.arange(window // dil + 1)
            idx = q_pos[:, None] - dil * steps[None, :]
            valid = idx >= 0
            idx = jnp.maximum(idx, 0)
            qb = lax.dynamic_slice_in_dim(qg, q0, Q_BLOCK, axis=1)
            kb = jnp.take(kg, idx, axis=1)
            vb = jnp.take(vg, idx, axis=1)
            s = jnp.einsum('bqhd,bqkhd->bhqk', qb, kb).astype(jnp.float32) * scale
            bias = bg[t5_bucket(dil * steps)].astype(jnp.float32)
            s = jnp.where(valid[None, None], s + bias.T[None, :, None, :], NEG_INF)
            lse = jax.nn.logsumexp(s, axis=-1)
            p = jnp.exp(s - lse[..., None])
            outs.append(jnp.einsum('bhqk,bqkhd->bqhd', p, vb.astype(jnp.float32)))
            lses.append(lse)
        wts = jax.nn.softmax(jnp.stack(lses), axis=0)
        o = jnp.einsum('gbhq,gbqhd->bqhd', wts, jnp.stack(outs))
        return o.astype(q.dtype)

    out = lax.map(block, jnp.arange(L // Q_BLOCK))
    return jnp.moveaxis(out, 0, 1).reshape(bsz, L, hg * dh)


def _complex_linear_combine(e1, e2):
    a1r, a1i, b1r, b1i = e1
    a2r, a2i, b2r, b2i = e2
    return (a2r * a1r - a2i * a1i, a2r * a1i + a2i * a1r,
            a2r * b1r - a2i * b1i + b2r, a2r * b1i + a2i * b1r + b2i)


def s5_glu(u, lam_re, lam_im, log_dt, b_re, b_im, c_re, c_im, d_skip, w_glu):
    bsz, L, _ = u.shape
    f32 = jnp.float32
    uf = u.astype(f32).reshape(bsz, L, SSM_GROUPS, SSM_GROUP)
    lr, li = lam_re.astype(f32), lam_im.astype(f32)
    dt = jnp.exp(log_dt.astype(f32))[:, None]
    mag = jnp.exp(lr * dt)
    a_re, a_im = mag * jnp.cos(li * dt), mag * jnp.sin(li * dt)
    den = lr * lr + li * li
    f_re = ((a_re - 1.0) * lr + a_im * li) / den
    f_im = (a_im * lr - (a_re - 1.0) * li) / den
    br, bi = b_re.astype(f32), b_im.astype(f32)
    bb_re = f_re[..., None] * br - f_im[..., None] * bi
    bb_im = f_re[..., None] * bi + f_im[..., None] * br
    bu_re = jnp.einsum('blgc,gpc->blgp', uf, bb_re)
    bu_im = jnp.einsum('blgc,gpc->blgp', uf, bb_im)
    a_re_l = jnp.broadcast_to(a_re, (1, L) + a_re.shape)
    a_im_l = jnp.broadcast_to(a_im, (1, L) + a_im.shape)
    _, _, x_re, x_im = lax.associative_scan(_complex_linear_combine, (a_re_l, a_im_l, bu_re, bu_im), axis=1)
    y = (jnp.einsum('gcp,blgp->blgc', c_re.astype(f32), x_re)
         - jnp.einsum('gcp,blgp->blgc', c_im.astype(f32), x_im)
         + d_skip.astype(f32).reshape(SSM_GROUPS, SSM_GROUP) * uf)
    y = jax.nn.gelu(y.reshape(bsz, L, SSM_WIDTH)).astype(u.dtype)
    ga, gb = jnp.split(y @ w_glu, 2, axis=-1)
    return ga * jax.nn.sigmoid(gb)


def dsa_attention(q, k, v, q_idx, k_idx, w_idx, rel_bias):
    bsz, L, nh, dh = q.shape
    top_k = min(DSA_TOPK, L // 4)
    scale = dh ** -0.5
    key_pos = jnp.arange(L)

    def block(i):
        q0 = i * Q_BLOCK
        q_pos = q0 + jnp.arange(Q_BLOCK)
        qi = lax.dynamic_slice_in_dim(q_idx, q0, Q_BLOCK, axis=1)
        wi = lax.dynamic_slice_in_dim(w_idx, q0, Q_BLOCK, axis=1)
        dots = jnp.einsum('bqhd,bkd->bqhk', qi, k_idx).astype(jnp.float32)
        index_score = jnp.einsum('bqh,bqhk->bqk', wi.astype(jnp.float32), jax.nn.relu(dots))
        index_score = jnp.where(key_pos[None, None, :] <= q_pos[None, :, None], index_score, NEG_INF)
        _, sel = lax.top_k(index_score, top_k)
        dist = q_pos[None, :, None] - sel
        valid = dist >= 0
        kb = jax.vmap(lambda kk, ii: kk[ii])(k, sel)
        vb = jax.vmap(lambda vv, ii: vv[ii])(v, sel)
        qb = lax.dynamic_slice_in_dim(q, q0, Q_BLOCK, axis=1)
        s = jnp.einsum('bqhd,bqkhd->bhqk', qb, kb).astype(jnp.float32) * scale
        bias = rel_bias[t5_bucket(jnp.maximum(dist, 0))].astype(jnp.float32)
        s = jnp.where(valid[:, None], s + jnp.transpose(bias, (0, 3, 1, 2)), NEG_INF)
        p = jax.nn.softmax(s, axis=-1)
        return jnp.einsum('bhqk,bqkhd->bqhd', p, vb.astype(jnp.float32)).astype(q.dtype)

    out = lax.map(block, jnp.arange(L // Q_BLOCK))
    return jnp.moveaxis(out, 0, 1).reshape(bsz, L, nh * dh)


def hybrid_mixer(x, w_in, rel_bias, lam_re, lam_im, log_dt, b_re, b_im, c_re, c_im, d_skip, w_glu,
                 w_br_sb, w_br_dil, w_br_ssm, w_br_dsa, w_out):
    bsz, L, _ = x.shape
    h = x @ w_in
    sb_qkv, dil_qkv, ssm_u, dsa_qkv, idx_q, idx_k, idx_w, gate_logits = jnp.split(h, IN_OFFSETS, axis=-1)
    sb_qkv = sb_qkv.reshape(bsz, L, 3, SB_HEADS, HEAD_DIM)
    y_sb = stick_breaking_attention(sb_qkv[:, :, 0], sb_qkv[:, :, 1], sb_qkv[:, :, 2])
    dil_qkv = dil_qkv.reshape(bsz, L, 3, DIL_HEADS, HEAD_DIM)
    y_dil = dilated_attention(dil_qkv[:, :, 0], dil_qkv[:, :, 1], dil_qkv[:, :, 2], rel_bias[:, :DIL_HEADS])
    y_ssm = s5_glu(ssm_u, lam_re, lam_im, log_dt, b_re, b_im, c_re, c_im, d_skip, w_glu)
    dsa_qkv = dsa_qkv.reshape(bsz, L, 3, DSA_HEADS, HEAD_DIM)
    y_dsa = dsa_attention(dsa_qkv[:, :, 0], dsa_qkv[:, :, 1], dsa_qkv[:, :, 2],
                          idx_q.reshape(bsz, L, IDX_HEADS, IDX_DIM), idx_k, idx_w, rel_bias[:, DIL_HEADS:])
    gates = jax.nn.sigmoid(gate_logits.reshape(bsz, L, N_BRANCH, D_MODEL))
    merged = (gates[:, :, 0] * (y_sb @ w_br_sb) + gates[:, :, 1] * (y_dil @ w_br_dil)
              + gates[:, :, 2] * (y_ssm @ w_br_ssm) + gates[:, :, 3] * (y_dsa @ w_br_dsa))
    return merged @ w_out


def setup_inputs(seed: int = 0) -> dict:
    key = jax.random.key(seed)
    ks = jax.random.split(key, 24)
    f32 = jnp.float32

    def nrm(k, shape, scale):
        return jax.random.normal(k, shape, f32) * scale

    G, P, C = SSM_GROUPS, SSM_STATE, SSM_GROUP
    return {
        "x": nrm(ks[0], (BATCH, SEQ, D_MODEL), 1.0),
        "ln_g": 1.0 + nrm(ks[1], (DEPTH, 3, D_MODEL), 0.02),
        "ln_b": nrm(ks[2], (DEPTH, 3, D_MODEL), 0.02),
        "ffn1_w_up": nrm(ks[3], (DEPTH, D_MODEL, 2 * D_FF), D_MODEL ** -0.5),
        "ffn1_w_down": nrm(ks[4], (DEPTH, D_FF, D_MODEL), DN_BETA * D_FF ** -0.5),
        "w_in": nrm(ks[5], (DEPTH, D_MODEL, IN_COLS), D_MODEL ** -0.5),
        "rel_bias": nrm(ks[6], (REL_BUCKETS, REL_HEADS), 0.2),
        "ssm_lam_re": -0.5 + nrm(ks[7], (DEPTH, G, P), 0.01),
        "ssm_lam_im": math.pi * jnp.arange(P, dtype=f32) + nrm(ks[8], (DEPTH, G, P), 0.01),
        "ssm_log_dt": jax.random.uniform(ks[9], (DEPTH, G), f32, math.log(1e-3), math.log(1e-1)),
        "ssm_b_re": nrm(ks[10], (DEPTH, G, P, C), C ** -0.5),
        "ssm_b_im": nrm(ks[11], (DEPTH, G, P, C), C ** -0.5),
        "ssm_c_re": nrm(ks[12], (DEPTH, G, C, P), P ** -0.5),
        "ssm_c_im": nrm(ks[13], (DEPTH, G, C, P), P ** -0.5),
        "ssm_d": nrm(ks[14], (DEPTH, SSM_WIDTH), 1.0),
        "ssm_w_glu": nrm(ks[15], (DEPTH, SSM_WIDTH, 2 * SSM_WIDTH), SSM_WIDTH ** -0.5),
        "w_br_sb": nrm(ks[16], (DEPTH, SB_HEADS * HEAD_DIM, D_MODEL), (SB_HEADS * HEAD_DIM) ** -0.5),
        "w_br_dil": nrm(ks[17], (DEPTH, DIL_HEADS_PER_GROUP * HEAD_DIM, D_MODEL), (DIL_HEADS_PER_GROUP * HEAD_DIM) ** -0.5),
        "w_br_ssm": nrm(ks[18], (DEPTH, SSM_WIDTH, D_MODEL), SSM_WIDTH ** -0.5),
        "w_br_dsa": nrm(ks[19], (DEPTH, DSA_HEADS * HEAD_DIM, D_MODEL), (DSA_HEADS * HEAD_DIM) ** -0.5),
        "w_out": nrm(ks[20], (DEPTH, D_MODEL, D_MODEL), DN_BETA * D_MODEL ** -0.5),
        "ffn2_w_up": nrm(ks[21], (DEPTH, D_MODEL, 2 * D_FF), D_MODEL ** -0.5),
        "ffn2_w_down": nrm(ks[22], (DEPTH, D_FF, D_MODEL), DN_BETA * D_FF ** -0.5),
    }


def reference(x, ln_g, ln_b, ffn1_w_up, ffn1_w_down, w_in, rel_bias, ssm_lam_re, ssm_lam_im, ssm_log_dt,
              ssm_b_re, ssm_b_im, ssm_c_re, ssm_c_im, ssm_d, ssm_w_glu, w_br_sb, w_br_dil, w_br_ssm, w_br_dsa,
              w_out, ffn2_w_up, ffn2_w_down):
    for l in range(DEPTH):
        x = layer_norm(DN_ALPHA * x + MACARON * swiglu_ffn(x, ffn1_w_up[l], ffn1_w_down[l]), ln_g[l, 0], ln_b[l, 0])
        mix = hybrid_mixer(x, w_in[l], rel_bias, ssm_lam_re[l], ssm_lam_im[l], ssm_log_dt[l],
                           ssm_b_re[l], ssm_b_im[l], ssm_c_re[l], ssm_c_im[l], ssm_d[l], ssm_w_glu[l],
                           w_br_sb[l], w_br_dil[l], w_br_ssm[l], w_br_dsa[l], w_out[l])
        x = layer_norm(DN_ALPHA * x + mix, ln_g[l, 1], ln_b[l, 1])
        x = layer_norm(DN_ALPHA * x + MACARON * swiglu_ffn(x, ffn2_w_up[l], ffn2_w_down[l]), ln_g[l, 2], ln_b[l, 2])
    return x
```

```python
import math
from contextlib import ExitStack

import numpy as np
import concourse.bass as bass
import concourse.mybir as mybir
from concourse.bass_utils import run_bass_kernel_spmd

F32 = mybir.dt.float32
BF16 = mybir.dt.bfloat16
AF = mybir.ActivationFunctionType
ALU = mybir.AluOpType
AX = mybir.AxisListType


def _is_psum(key):
    name = key[0] if isinstance(key, tuple) else key
    return isinstance(name, str) and name.startswith("P_")


class Prog:
    NQ = 12

    def __init__(self):
        self.nc = bass.Bass("TRN2", target_bir_lowering=False)
        nc = self.nc
        self.es = ExitStack()
        self.eng = {"pe": nc.tensor, "act": nc.scalar, "dve": nc.vector, "pool": nc.gpsimd, "sp": nc.sync}
        self.sem = {}
        self.cnt = {}
        for e in ("pe", "act", "dve", "pool"):
            self.sem[e] = self.es.enter_context(nc.semaphore("s_" + e))
            self.cnt[e] = 0
        self.dq = {}
        for q in ("sp", "pool", "act"):
            sems = [self.es.enter_context(nc.semaphore(f"d_{q}{i}")) for i in range(self.NQ)]
            for i, s in enumerate(sems):
                self.sem[(q, i)] = s
                self.cnt[(q, i)] = 0
            self.dq[q] = 0
        self.seen = {e: {} for e in self.eng}
        self.last_w = {}
        self.readers = {}
        self.n_ins = 0
        self.n_wait = 0

    def _uniq(self, name):
        self.n_alloc = getattr(self, "n_alloc", 0) + 1
        return f"{name}__{self.n_alloc}"

    def sb(self, name, shape, dtype, stack=None):
        return (stack or self.es).enter_context(self.nc.sbuf_tensor(self._uniq(name), list(shape), dtype))

    def ps(self, name, shape, dtype=F32, stack=None):
        return (stack or self.es).enter_context(self.nc.psum_tensor(self._uniq(name), list(shape), dtype))

    def _wait(self, e, key, val):
        if val <= 0:
            return
        if self.seen[e].get(key, 0) >= val:
            return
        self.eng[e].wait_ge(self.sem[key], val)
        self.seen[e][key] = val
        self.n_wait += 1

    def _deps(self, e, reads, writes, strict_pe=False):
        deps = {}
        for r in reads:
            lw = self.last_w.get(r)
            if lw:
                deps[lw[0]] = max(deps.get(lw[0], 0), lw[1])
            if _is_psum(r):
                for k, v in self.readers.get(r, {}).items():
                    if k != e:
                        deps[k] = max(deps.get(k, 0), v)
        for w in writes:
            lw = self.last_w.get(w)
            if lw:
                deps[lw[0]] = max(deps.get(lw[0], 0), lw[1])
            for k, v in self.readers.get(w, {}).items():
                deps[k] = max(deps.get(k, 0), v)
        for k, v in deps.items():
            if k == "pe" and e == "pe" and not strict_pe:
                continue
            self._wait(e, k, v)

    def _mark(self, key, val, reads, writes):
        for w in writes:
            self.last_w[w] = (key, val)
            self.readers[w] = {}
        for r in reads:
            if r in writes:
                continue
            d = self.readers.setdefault(r, {})
            d[key] = max(d.get(key, 0), val)

    def op(self, e, fn, reads=(), writes=(), strict_pe=False):
        self._deps(e, reads, writes, strict_pe)
        ins = fn()
        self.cnt[e] += 1
        ins.then_inc(self.sem[e], 1)
        self._mark(e, self.cnt[e], reads, writes)
        self.n_ins += 1
        return ins

    def dma(self, q, out, in_, reads=(), writes=(), **kw):
        i = self.dq[q] % self.NQ
        self.dq[q] += 1
        key = (q, i)
        self._wait(q, key, self.cnt[key])
        self._deps(q, reads, writes)
        ins = self.eng[q].dma_start(out=out, in_=in_, **kw)
        self.cnt[key] += 16
        ins.then_inc(self.sem[key], 16)
        self._mark(key, self.cnt[key], reads, writes)
        self.n_ins += 1
        return ins

    def barrier(self):
        for e in self.eng:
            for k, v in self.cnt.items():
                if k == e and e == "pe":
                    continue
                self._wait(e, k, v)
        self.last_w.clear()
        self.readers.clear()

    def finish(self):
        for k, v in self.cnt.items():
            self._wait("sp", k, v)
        self.es.close()


D = 2048
KC = D // 128
DFF = 5632
NJ = DFF // 128
DEPTH = 2
DN_ALPHA = (2.0 * DEPTH) ** 0.25
LN_EPS = 1e-5
TT = 512
NYC = 14
SEQ = 2048
IN_COLS = 14664
GATE0 = 6472


def win_blocks():
    blks = {}
    blks["sb_q"] = list(range(0, 512))
    blks["sb_k"] = list(range(512, 1024))
    blks["sb_v"] = list(range(1024, 1536))
    for g in range(3):
        blks[f"dil_q{g}"] = list(range(1536 + g * 256, 1536 + (g + 1) * 256))
        blks[f"dil_k{g}"] = list(range(2304 + g * 256, 2304 + (g + 1) * 256))
        blks[f"dil_v{g}"] = list(range(3072 + g * 256, 3072 + (g + 1) * 256))
    blks["ssm_u"] = [3840 + min(96 * m + q, 511) for m in range(6) for q in range(128)]
    blks["dsa_q"] = list(range(4352, 4864))
    blks["dsa_k"] = list(range(4864, 5376))
    blks["dsa_v"] = list(range(5376, 5888))
    blks["idx_q"] = list(range(5888, 6400))
    blks["idx_k"] = list(range(6400, 6464)) * 2
    blks["idx_w"] = list(range(6464, 6472))
    for c in range(KC):
        cols = []
        for b in range(4):
            cols += list(range(GATE0 + b * D + c * 128, GATE0 + b * D + (c + 1) * 128))
        blks[f"gate{c}"] = cols
    return blks


WIN_BLOCKS = win_blocks()
WIN_OFF = {}
_o = 0
for _k, _v in WIN_BLOCKS.items():
    WIN_OFF[_k] = (_o, len(_v))
    _o += KC * len(_v)
WIN_TOT = _o


def lay_w_in(w):
    w3 = w.reshape(KC, 128, IN_COLS)
    out = np.empty((128, WIN_TOT), np.float32)
    for k, cols in WIN_BLOCKS.items():
        o, n = WIN_OFF[k]
        out[:, o:o + KC * n] = w3[:, :, cols].transpose(1, 0, 2).reshape(128, KC * n)
    return out


class Builder:
    def __init__(self, ttot):
        self.p = Prog()
        self.nc = self.p.nc
        self.ttot = ttot
        nc = self.nc
        self.dram_in = {}
        self.XTf = nc.dram_tensor("XTf", [KC, 128, ttot], F32, kind="Internal").ap()
        self.XTb = nc.dram_tensor("XTb", [KC, 128, ttot], BF16, kind="Internal").ap()
        self.YT = nc.dram_tensor("YT", [NYC, 128, ttot], BF16, kind="Internal").ap()
        p = self.p
        self.ident_f = p.sb("ident_f", [128, 128], F32)
        self.ones_mean = p.sb("ones_mean", [128, 128], F32)
        self.lng = p.sb("lng", [128, DEPTH * 3 * KC], F32)
        self.lnb = p.sb("lnb", [128, DEPTH * 3 * KC], F32)

    def inp(self, name, shape, dtype=F32):
        t = self.nc.dram_tensor(name, list(shape), dtype, kind="ExternalInput").ap()
        self.dram_in[name] = t
        return t

    def load_consts(self):
        p, nc = self.p, self.nc
        ident = self.inp("cm_ident", [128, 128])
        lng = self.inp("ln_g_l", [128, DEPTH * 3 * KC])
        lnb = self.inp("ln_b_l", [128, DEPTH * 3 * KC])
        p.dma("sp", self.ident_f[:], ident, writes=["ident_f"])
        p.dma("sp", self.lng[:], lng, writes=["lng"])
        p.dma("sp", self.lnb[:], lnb, writes=["lnb"])
        p.op("pool", lambda: nc.gpsimd.memset(self.ones_mean[:], 1.0 / D), writes=["ones_mean"])

    def phase_in(self, x):
        p, nc = self.p, self.nc
        with ExitStack() as st:
            xin = [p.sb(f"pi_x{i}", [128, D], F32, st) for i in range(2)]
            sf = [p.sb(f"pi_sf{i}", [128, KC, 512], F32, st) for i in range(2)]
            sb_ = [p.sb(f"pi_sb{i}", [128, KC, 512], BF16, st) for i in range(2)]
            pt = [p.ps(f"pi_pt{i}", [128, 512], F32, st) for i in range(4)]
            nblk = self.ttot // 128
            n = 0
            for g in range(self.ttot // 512):
                gi = g % 2
                for b4 in range(4):
                    blk = g * 4 + b4
                    xi = blk % 2
                    p.dma("sp", xin[xi][:], x[blk * 128:(blk + 1) * 128, :], writes=[("pi_x", xi)])
                    for c4 in range(4):
                        ps = pt[n % 4]
                        pk = ("P_pi", n % 4)
                        n += 1
                        for cc in range(4):
                            c = c4 * 4 + cc
                            p.op("pe", lambda ps=ps, cc=cc, c=c, xi=xi: nc.tensor.transpose(
                                ps[:, cc * 128:(cc + 1) * 128], xin[xi][:, c * 128:(c + 1) * 128], self.ident_f[:]),
                                reads=[("pi_x", xi), "ident_f"], writes=[pk])
                        src = ps[:].rearrange("p (c t) -> p c t", c=4)
                        p.op("dve", lambda src=src, c4=c4, gi=gi, b4=b4: nc.vector.tensor_copy(
                            sf[gi][:, c4 * 4:(c4 + 1) * 4, b4 * 128:(b4 + 1) * 128], src),
                            reads=[pk], writes=[("pi_sf", gi, b4, c4)])
                        p.op("act", lambda src=src, c4=c4, gi=gi, b4=b4: nc.scalar.copy(
                            sb_[gi][:, c4 * 4:(c4 + 1) * 4, b4 * 128:(b4 + 1) * 128], src),
                            reads=[pk], writes=[("pi_sb", gi, b4, c4)])
                rk = [("pi_sf", gi, b4, c4) for b4 in range(4) for c4 in range(4)]
                rk2 = [("pi_sb", gi, b4, c4) for b4 in range(4) for c4 in range(4)]
                for c in range(KC):
                    p.dma("sp", self.XTf[c, :, g * 512:(g + 1) * 512], sf[gi][:, c, :],
                          reads=rk, writes=[("XTf", g)])
                    p.dma("sp", self.XTb[c, :, g * 512:(g + 1) * 512], sb_[gi][:, c, :],
                          reads=rk2, writes=[("XTb", g)])
            p.barrier()

    def phase_out(self, y):
        p, nc = self.p, self.nc
        with ExitStack() as st:
            sf = [p.sb(f"po_sf{i}", [128, KC, 512], F32, st) for i in range(2)]
            yo = [p.sb(f"po_y{i}", [128, D], F32, st) for i in range(2)]
            pt = [p.ps(f"po_pt{i}", [128, 512], F32, st) for i in range(4)]
            n = 0
            for g in range(self.ttot // 512):
                gi = g % 2
                for c in range(KC):
                    p.dma("sp", sf[gi][:, c, :], self.XTf[c, :, g * 512:(g + 1) * 512],
                          reads=[("XTf", g)], writes=[("po_sf", gi)])
                for b4 in range(4):
                    blk = g * 4 + b4
                    yi = blk % 2
                    for c4 in range(4):
                        ps = pt[n % 4]
                        pk = ("P_po", n % 4)
                        n += 1
                        for cc in range(4):
                            c = c4 * 4 + cc
                            p.op("pe", lambda ps=ps, cc=cc, c=c, gi=gi, b4=b4: nc.tensor.transpose(
                                ps[:, cc * 128:(cc + 1) * 128], sf[gi][:, c, b4 * 128:(b4 + 1) * 128], self.ident_f[:]),
                                reads=[("po_sf", gi), "ident_f"], writes=[pk])
                        eng = "dve" if c4 % 2 == 0 else "act"
                        if eng == "dve":
                            p.op("dve", lambda ps=ps, c4=c4, yi=yi: nc.vector.tensor_copy(
                                yo[yi][:, c4 * 512:(c4 + 1) * 512], ps[:]), reads=[pk], writes=[("po_y", yi, c4)])
                        else:
                            p.op("act", lambda ps=ps, c4=c4, yi=yi: nc.scalar.copy(
                                yo[yi][:, c4 * 512:(c4 + 1) * 512], ps[:]), reads=[pk], writes=[("po_y", yi, c4)])
                    p.dma("sp", y[blk * 128:(blk + 1) * 128, :], yo[yi][:],
                          reads=[("po_y", yi, c4) for c4 in range(4)])
            p.barrier()

    class _Epi:
        pass

    def epi_alloc(self, st, pre):
        p, nc = self.p, self.nc
        e = Builder._Epi()
        e.pre = pre
        e.rs = [p.sb(f"{pre}_rs{i}", [128, TT], F32, st) for i in range(2)]
        e.yT = p.sb(f"{pre}_y", [128, KC, TT], F32, st)
        e.ysq = [p.sb(f"{pre}_ysq{i}", [128, TT], F32, st) for i in range(2)]
        e.var = p.sb(f"{pre}_var", [128, TT], F32, st)
        e.rstd = p.sb(f"{pre}_rstd", [128, TT], F32, st)
        e.t1 = [p.sb(f"{pre}_t1{i}", [128, TT], F32, st) for i in range(2)]
        e.of = [p.sb(f"{pre}_of{i}", [128, TT], F32, st) for i in range(2)]
        e.ob = [p.sb(f"{pre}_ob{i}", [128, TT], BF16, st) for i in range(2)]
        e.epsb = p.sb(f"{pre}_eps", [128, 1], F32, st)
        e.S = [p.ps(f"{pre}_S{i}", [128, TT], F32, st) for i in range(2)]
        e.n = 0
        p.op("dve", lambda: nc.vector.memset(e.epsb[:], LN_EPS / (DN_ALPHA ** 2)), writes=[pre + "_eps"])
        return e

    def epi_load(self, e, c, t):
        p = self.p
        ci = e.n % 2
        tok = slice(t * TT, (t + 1) * TT)
        p.dma("sp", e.rs[ci][:], self.XTf[c, :, tok], reads=[("XTf", t)], writes=[(e.pre + "_rs", ci)])

    def epi_chunk(self, e, c, Yacc, Ykey, scale):
        p, nc = self.p, self.nc
        pre = e.pre
        ci = e.n % 2
        e.n += 1
        S = e.S
        p.op("dve", lambda: nc.vector.scalar_tensor_tensor(
            out=e.yT[:, c, :], in0=Yacc[:], scalar=scale, in1=e.rs[ci][:], op0=ALU.mult, op1=ALU.add),
            reads=[Ykey, (pre + "_rs", ci)], writes=[(pre + "_y", c)])
        p.op("act", lambda: nc.scalar.activation(out=e.ysq[ci][:], in_=e.yT[:, c, :], func=AF.Square),
             reads=[(pre + "_y", c)], writes=[(pre + "_ysq", ci)])
        p.op("pe", lambda: nc.tensor.matmul(S[0][:], lhsT=self.ones_mean[:], rhs=e.yT[:, c, :],
                                            start=(c == 0), stop=(c == KC - 1)),
             reads=["ones_mean", (pre + "_y", c)], writes=[("P_" + pre + "S", 0)])
        p.op("pe", lambda: nc.tensor.matmul(S[1][:], lhsT=self.ones_mean[:], rhs=e.ysq[ci][:],
                                            start=(c == 0), stop=(c == KC - 1)),
             reads=["ones_mean", (pre + "_ysq", ci)], writes=[("P_" + pre + "S", 1)])

    def epi_finalize(self, e, li, t):
        for piece in self.epi_finalize_pieces(e, li, t):
            piece()

    def epi_finalize_pieces(self, e, li, t):
        pieces = [lambda: self._epi_fin_head(e)]
        for c in range(KC):
            pieces.append(lambda c=c: self._epi_fin_chunk(e, li, t, c))
        return pieces

    def _epi_fin_head(self, e):
        p, nc = self.p, self.nc
        pre = e.pre
        S, var, rstd, epsb = e.S, e.var, e.rstd, e.epsb
        p.op("act", lambda: nc.scalar.activation(out=var[:], in_=S[0][:], func=AF.Square),
             reads=[("P_" + pre + "S", 0)], writes=[pre + "_var"])
        p.op("dve", lambda: nc.vector.tensor_tensor(out=var[:], in0=S[1][:], in1=var[:], op=ALU.subtract),
             reads=[("P_" + pre + "S", 1), pre + "_var"], writes=[pre + "_var"])
        p.op("act", lambda: nc.scalar.activation(out=var[:], in_=var[:], func=AF.Sqrt, bias=epsb[:], scale=1.0),
             reads=[pre + "_var", pre + "_eps"], writes=[pre + "_var"])
        p.op("dve", lambda: nc.vector.reciprocal(out=rstd[:], in_=var[:]),
             reads=[pre + "_var"], writes=[pre + "_rstd"])

    def _epi_fin_chunk(self, e, li, t, c):
        p, nc = self.p, self.nc
        pre = e.pre
        S, yT, var, rstd, t1, of, ob, epsb = e.S, e.yT, e.var, e.rstd, e.t1, e.of, e.ob, e.epsb
        tok = slice(t * TT, (t + 1) * TT)
        if True:
            ci = c % 2
            p.op("dve", lambda c=c, ci=ci: nc.vector.tensor_tensor(out=t1[ci][:], in0=yT[:, c, :], in1=S[0][:], op=ALU.subtract),
                 reads=[(pre + "_y", c), ("P_" + pre + "S", 0)], writes=[(pre + "_t1", ci)])
            p.op("dve", lambda ci=ci: nc.vector.tensor_tensor(out=t1[ci][:], in0=t1[ci][:], in1=rstd[:], op=ALU.mult),
                 reads=[(pre + "_t1", ci), pre + "_rstd"], writes=[(pre + "_t1", ci)])
            col = li * KC + c
            p.op("act", lambda c=c, ci=ci, col=col: nc.scalar.activation(
                out=of[ci][:], in_=t1[ci][:], func=AF.Identity, scale=self.lng[:, col:col + 1], bias=self.lnb[:, col:col + 1]),
                reads=[(pre + "_t1", ci), "lng", "lnb"], writes=[(pre + "_of", ci)])
            p.op("act", lambda c=c, ci=ci, col=col: nc.scalar.activation(
                out=ob[ci][:], in_=t1[ci][:], func=AF.Identity, scale=self.lng[:, col:col + 1], bias=self.lnb[:, col:col + 1]),
                reads=[(pre + "_t1", ci), "lng", "lnb"], writes=[(pre + "_ob", ci)])
            p.dma("sp", self.XTf[c, :, tok], of[ci][:], reads=[(pre + "_of", ci)], writes=[("XTf", t)])
            p.dma("sp", self.XTb[c, :, tok], ob[ci][:], reads=[(pre + "_ob", ci)], writes=[("XTb", t)])

    def phase_ffn(self, w_up_l, w_down_l, li):
        p, nc = self.p, self.nc
        with ExitStack() as st:
            xTb = [p.sb(f"ff_x{i}", [128, KC, TT], BF16, st) for i in range(2)]
            gT = p.sb("ff_g", [128, NJ, TT], BF16, st)
            wu = [[p.sb(f"ff_wu{i}{h}", [128, KC, 256], BF16, st) for h in range(2)] for i in range(2)]
            wd = [p.sb(f"ff_wd{i}", [128, NJ, 128], BF16, st) for i in range(2)]
            sa = [p.sb(f"ff_sa{i}", [128, TT], F32, st) for i in range(2)]
            e = self.epi_alloc(st, "ff")
            A = [p.ps(f"ff_A{i}", [128, TT], F32, st) for i in range(2)]
            B = [p.ps(f"ff_B{i}", [128, TT], F32, st) for i in range(2)]
            Y = [p.ps(f"ff_Y{i}", [128, TT], F32, st) for i in range(2)]
            ntile = self.ttot // TT
            nwu = 0
            nwd = 0
            nj = 0
            pending = []

            def load_x(t):
                for c in range(KC):
                    p.dma("sp", xTb[t % 2][:, c, :], self.XTb[c, :, t * TT:(t + 1) * TT],
                          reads=[("XTb", t)], writes=[("ff_x", t % 2)])

            load_x(0)
            for t in range(ntile):
                ti = t % 2
                tok = slice(t * TT, (t + 1) * TT)
                for g in range(NJ // 2):
                    if pending:
                        pending.pop(0)()
                    wi = nwu % 2
                    nwu += 1
                    for h in range(2):
                        p.dma("pool", wu[wi][h][:], w_up_l[g, h], writes=[("ff_wu", wi, h)], max_dma_last_dim=8192)
                    for jj in range(2):
                        j = 2 * g + jj
                        ji = nj % 2
                        nj += 1
                        for h, acc, ak in ((0, A[ji], ("P_ffA", ji)), (1, B[ji], ("P_ffB", ji))):
                            for kc in range(KC):
                                p.op("pe", lambda acc=acc, h=h, kc=kc, jj=jj, wi=wi, ti=ti: nc.tensor.matmul(
                                    acc[:], lhsT=wu[wi][h][:, kc, jj * 128:(jj + 1) * 128], rhs=xTb[ti][:, kc, :],
                                    start=(kc == 0), stop=(kc == KC - 1)),
                                    reads=[("ff_wu", wi, h), ("ff_x", ti)], writes=[ak])
                        p.op("act", lambda ji=ji: nc.scalar.activation(out=sa[ji][:], in_=A[ji][:], func=AF.Silu),
                             reads=[("P_ffA", ji)], writes=[("ff_sa", ji)])
                        p.op("dve", lambda ji=ji, j=j: nc.vector.tensor_tensor(
                            out=gT[:, j, :], in0=B[ji][:], in1=sa[ji][:], op=ALU.mult),
                            reads=[("P_ffB", ji), ("ff_sa", ji)], writes=[("ff_g", j)])
                if t + 1 < ntile:
                    load_x(t + 1)
                for c in range(KC):
                    wi = nwd % 2
                    nwd += 1
                    yi = c % 2
                    p.dma("pool", wd[wi][:], w_down_l[c], writes=[("ff_wd", wi)], max_dma_last_dim=8192)
                    self.epi_load(e, c, t)
                    for j in range(NJ):
                        p.op("pe", lambda yi=yi, wi=wi, j=j: nc.tensor.matmul(
                            Y[yi][:], lhsT=wd[wi][:, j, :], rhs=gT[:, j, :], start=(j == 0), stop=(j == NJ - 1)),
                            reads=[("ff_wd", wi), ("ff_g", j)], writes=[("P_ffY", yi)])
                    self.epi_chunk(e, c, Y[yi], ("P_ffY", yi), 0.5 / DN_ALPHA)
                while pending:
                    pending.pop(0)()
                pending = self.epi_finalize_pieces(e, li, t)
            while pending:
                pending.pop(0)()
            p.barrier()

    def phase_merge(self, w_in_l, w_br_l, w_out_l, li):
        p, nc = self.p, self.nc
        YB = [(0, 4), (4, 2), (6, 4), (10, 4)]
        with ExitStack() as st:
            xTbs = [p.sb(f"mg_x{i}", [128, KC, TT], BF16, st) for i in range(2)]
            yTbs = [p.sb(f"mg_yb{i}", [128, NYC, TT], BF16, st) for i in range(2)]
            mT = p.sb("mg_m", [128, KC, TT], BF16, st)
            wg = [p.sb(f"mg_wg{i}", [128, KC, 512], BF16, st) for i in range(2)]
            wb = [p.sb(f"mg_wb{i}", [128, NYC, 128], BF16, st) for i in range(2)]
            wo = [p.sb(f"mg_wo{i}", [128, KC, 128], BF16, st) for i in range(2)]
            sg = [p.sb(f"mg_sg{i}", [128, TT], F32, st) for i in range(2)]
            macc = [p.sb(f"mg_ma{i}", [128, TT], F32, st) for i in range(2)]
            e = self.epi_alloc(st, "mg")
            G = [p.ps(f"mg_G{i}", [128, TT], F32, st) for i in range(2)]
            Pb = [p.ps(f"mg_P{i}", [128, TT], F32, st) for i in range(2)]
            Y = [p.ps(f"mg_Y{i}", [128, TT], F32, st) for i in range(2)]
            ntile = self.ttot // TT
            nw = 0
            nb = 0
            nwo = 0
            pending = []

            def load_xy(t):
                tk = slice(t * TT, (t + 1) * TT)
                for c in range(KC):
                    p.dma("sp", xTbs[t % 2][:, c, :], self.XTb[c, :, tk], reads=[("XTb", t)], writes=[("mg_x", t % 2)])
                for c in range(NYC):
                    p.dma("sp", yTbs[t % 2][:, c, :], self.YT[c, :, tk], reads=[("YT", t)], writes=[("mg_yb", t % 2)])

            load_xy(0)
            for t in range(ntile):
                tok = slice(t * TT, (t + 1) * TT)
                xTb, yTb = xTbs[t % 2], yTbs[t % 2]
                xk, yk = ("mg_x", t % 2), ("mg_yb", t % 2)
                for c in range(KC):
                    if pending:
                        pending.pop(0)()
                    wi = nw % 2
                    nw += 1
                    o, n = WIN_OFF[f"gate{c}"]
                    p.dma("pool", wg[wi][:], w_in_l[:, o:o + KC * n].rearrange("p (k c) -> p k c", k=KC),
                          writes=[("mg_wg", wi)], max_dma_last_dim=8192)
                    p.dma("pool", wb[wi][:], w_br_l[c], writes=[("mg_wb", wi)], max_dma_last_dim=8192)
                    mi = c % 2
                    for b in range(4):
                        bi = nb % 2
                        nb += 1
                        for kc in range(KC):
                            p.op("pe", lambda bi=bi, wi=wi, kc=kc, b=b: nc.tensor.matmul(
                                G[bi][:], lhsT=wg[wi][:, kc, b * 128:(b + 1) * 128], rhs=xTb[:, kc, :],
                                start=(kc == 0), stop=(kc == KC - 1)),
                                reads=[("mg_wg", wi), xk], writes=[("P_mgG", bi)])
                        y0, ny = YB[b]
                        for k in range(ny):
                            p.op("pe", lambda bi=bi, wi=wi, k=k, y0=y0, ny=ny: nc.tensor.matmul(
                                Pb[bi][:], lhsT=wb[wi][:, y0 + k, :], rhs=yTb[:, y0 + k, :],
                                start=(k == 0), stop=(k == ny - 1)),
                                reads=[("mg_wb", wi), yk], writes=[("P_mgP", bi)])
                        p.op("act", lambda bi=bi: nc.scalar.activation(out=sg[bi][:], in_=G[bi][:], func=AF.Sigmoid),
                             reads=[("P_mgG", bi)], writes=[("mg_sg", bi)])
                        if b == 0:
                            p.op("dve", lambda bi=bi, mi=mi: nc.vector.tensor_tensor(
                                out=macc[mi][:], in0=Pb[bi][:], in1=sg[bi][:], op=ALU.mult),
                                reads=[("P_mgP", bi), ("mg_sg", bi)], writes=[("mg_ma", mi)])
                        else:
                            p.op("dve", lambda bi=bi: nc.vector.tensor_tensor(
                                out=sg[bi][:], in0=Pb[bi][:], in1=sg[bi][:], op=ALU.mult),
                                reads=[("P_mgP", bi), ("mg_sg", bi)], writes=[("mg_sg", bi)])
                            if b < 3:
                                p.op("dve", lambda bi=bi, mi=mi: nc.vector.tensor_tensor(
                                    out=macc[mi][:], in0=macc[mi][:], in1=sg[bi][:], op=ALU.add),
                                    reads=[("mg_ma", mi), ("mg_sg", bi)], writes=[("mg_ma", mi)])
                            else:
                                p.op("dve", lambda bi=bi, mi=mi, c=c: nc.vector.tensor_tensor(
                                    out=mT[:, c, :], in0=macc[mi][:], in1=sg[bi][:], op=ALU.add),
                                    reads=[("mg_ma", mi), ("mg_sg", bi)], writes=[("mg_m", c)])
                while pending:
                    pending.pop(0)()
                if t + 1 < ntile:
                    load_xy(t + 1)
                for c in range(KC):
                    wi = nwo % 2
                    nwo += 1
                    yi = c % 2
                    p.dma("pool", wo[wi][:], w_out_l[c], writes=[("mg_wo", wi)], max_dma_last_dim=8192)
                    self.epi_load(e, c, t)
                    for k in range(KC):
                        p.op("pe", lambda yi=yi, wi=wi, k=k: nc.tensor.matmul(
                            Y[yi][:], lhsT=wo[wi][:, k, :], rhs=mT[:, k, :], start=(k == 0), stop=(k == KC - 1)),
                            reads=[("mg_wo", wi), ("mg_m", k)], writes=[("P_mgY", yi)])
                    self.epi_chunk(e, c, Y[yi], ("P_mgY", yi), 1.0 / DN_ALPHA)
                pending = self.epi_finalize_pieces(e, li, t)
            while pending:
                pending.pop(0)()
            p.barrier()

    def load_mixer_consts(self):
        p, nc = self.p, self.nc
        self.c_sbmask = p.sb("c_sbmask", [128, 512], F32)
        self.c_triu = p.sb("c_triu", [128, 128], BF16)
        self.c_ones = p.sb("c_ones", [128, 128], BF16)
        self.ident_b = p.sb("ident_b", [128, 128], BF16)
        p.dma("sp", self.c_sbmask[:], self.inp("cm_sbmask", [128, 512]), writes=["c_sbmask"])
        p.dma("pool", self.c_triu[:], self.inp("cm_triu", [128, 128]), writes=["c_triu"])
        p.dma("pool", self.ident_b[:], self.dram_in["cm_ident"], writes=["ident_b"])
        p.op("pool", lambda: nc.gpsimd.memset(self.c_ones[:], 1.0), writes=["c_ones"])
        self.c_zeros = p.sb("c_zeros", [128, 128], BF16)
        p.op("pool", lambda: nc.gpsimd.memset(self.c_zeros[:], 0.0), writes=["c_zeros"])
        self.c_sbneg = p.sb("c_sbneg", [128, 512], BF16)
        p.op("pool", lambda: nc.gpsimd.tensor_scalar(out=self.c_sbneg[:], in0=self.c_sbmask[:], scalar1=-30000.0, scalar2=30000.0,
                                                     op0=ALU.mult, op1=ALU.add), reads=["c_sbmask"], writes=["c_sbneg"])

    def load_xT(self, xT, s, key):
        p = self.p
        for c in range(KC):
            for hh in range(2):
                p.dma("sp", xT[:, c, hh * 1024:(hh + 1) * 1024],
                      self.XTb[c, :, s * SEQ + hh * 1024: s * SEQ + (hh + 1) * 1024],
                      reads=[("XTb", (s * SEQ + hh * 1024) // TT + i) for i in range(2)], writes=[key])

    def load_w(self, w_in_l, name, dst, key):
        o, n = WIN_OFF[name]
        self.p.dma("pool", dst[:, :, 0:n], w_in_l[:, o:o + KC * n].rearrange("p (k c) -> p k c", k=KC),
                   writes=[key], max_dma_last_dim=8192)

    def proj_fm(self, W, wkey, col0, ncols, xT, xkey, t0, nt, acc, acckey):
        p, nc = self.p, self.nc
        for kc in range(KC):
            p.op("pe", lambda kc=kc: nc.tensor.matmul(
                acc[0:ncols, 0:nt], lhsT=W[:, kc, col0:col0 + ncols], rhs=xT[:, kc, t0:t0 + nt],
                start=(kc == 0), stop=(kc == KC - 1)), reads=[wkey, xkey], writes=[acckey])

    def proj_tm(self, W, wkey, col0, ncols, xT, xkey, tok_ap_fn, acc, acckey):
        p, nc = self.p, self.nc
        for kc in range(KC):
            p.op("pe", lambda kc=kc: nc.tensor.matmul(
                acc[:, 0:ncols], lhsT=tok_ap_fn(kc), rhs=W[:, kc, col0:col0 + ncols],
                start=(kc == 0), stop=(kc == KC - 1)), reads=[wkey, xkey], writes=[acckey])

    @staticmethod
    def pipeline(tiles, stages, skew=1, order=None, lags=None):
        n = len(tiles)
        ns = len(stages)
        lags = lags or [k * skew for k in range(ns)]
        for step in range(n + max(lags)):
            for k in (order or range(ns)):
                i = step - lags[k]
                if 0 <= i < n:
                    stages[k](i, tiles[i])

    def phase_sb(self, w_in_l, s):
        p, nc = self.p, self.nc
        with ExitStack() as st:
            qT = p.sb("sb_q", [128, 4, SEQ], BF16, st)
            kT = p.sb("sb_k", [128, 4, SEQ], BF16, st)
            V = p.sb("sb_v", [128, 16, 512], BF16, st)
            with ExitStack() as st2:
                xT = p.sb("sb_x", [128, KC, SEQ], BF16, st2)
                W = [p.sb(f"sb_w{i}", [128, KC, 512], BF16, st2) for i in range(2)]
                acc = [p.ps(f"sb_acc{i}", [128, 512], F32, st2) for i in range(4)]
                self.load_xT(xT, s, "sb_x")
                self.load_w(w_in_l, "sb_q", W[0], ("sb_w", 0))
                self.load_w(w_in_l, "sb_k", W[1], ("sb_w", 1))
                n = 0
                for wi, dst, sc, dk in ((0, qT, 0.125, "sb_q"), (1, kT, 1.0, "sb_k")):
                    for hp in range(4):
                        for tb in range(4):
                            a = acc[n % 4]
                            ak = ("P_sbacc", n % 4)
                            n += 1
                            self.proj_fm(W[wi], ("sb_w", wi), hp * 128, 128, xT, "sb_x", tb * 512, 512, a, ak)
                            if n % 2 == 0:
                                p.op("act", lambda a=a, dst=dst, hp=hp, tb=tb, sc=sc: nc.scalar.activation(
                                    out=dst[:, hp, tb * 512:(tb + 1) * 512], in_=a[:], func=AF.Copy, scale=sc),
                                    reads=[ak], writes=[(dk, hp)])
                            else:
                                p.op("dve", lambda a=a, dst=dst, hp=hp, tb=tb, sc=sc: nc.vector.tensor_scalar(
                                    out=dst[:, hp, tb * 512:(tb + 1) * 512], in0=a[:], scalar1=sc, scalar2=None, op0=ALU.mult),
                                    reads=[ak], writes=[(dk, hp)])
                self.load_w(w_in_l, "sb_v", W[0], ("sb_w", 0))
                for tkb in range(16):
                    a = acc[n % 4]
                    ak = ("P_sbacc", n % 4)
                    n += 1
                    self.proj_tm(W[0], ("sb_w", 0), 0, 512, xT, "sb_x",
                                 lambda kc, tkb=tkb: xT[:, kc, tkb * 128:(tkb + 1) * 128], a, ak)
                    if n % 2 == 0:
                        p.op("act", lambda a=a, tkb=tkb: nc.scalar.copy(out=V[:, tkb, :], in_=a[:]),
                             reads=[ak], writes=[("sb_v", tkb)])
                    else:
                        p.op("dve", lambda a=a, tkb=tkb: nc.vector.tensor_copy(out=V[:, tkb, :], in_=a[:]),
                             reads=[ak], writes=[("sb_v", tkb)])
                p.barrier()
            NB = 8
            E = [p.sb(f"sb_E{i}", [128, 512], F32, st) for i in range(NB)]
            L = [p.sb(f"sb_L{i}", [128, 512], F32, st) for i in range(NB)]
            Lh = [p.sb(f"sb_Lh{i}", [128, 512], BF16, st) for i in range(NB)]
            Wt = [p.sb(f"sb_W{i}", [128, 512], F32, st) for i in range(NB)]
            Ab = [p.sb(f"sb_Ab{i}", [128, 512], BF16, st) for i in range(NB)]
            osb = [p.sb(f"sb_o{i}", [128, 512], BF16, st) for i in range(2)]
            Z = [p.ps(f"sb_Z{i}", [128, 512], F32, st) for i in range(2)]
            SU = [p.ps(f"sb_SU{i}", [128, 512], F32, st) for i in range(2)]
            CS = [p.ps(f"sb_CS{i}", [128, 512], F32, st) for i in range(2)]
            O = [p.ps(f"sb_O{i}", [128, 512], F32, st) for i in range(2)]
            tiles = []
            for h in range(4):
                for qc in range(4):
                    kbs = list(range(4 * qc + 3, -1, -1))
                    for kb in kbs:
                        for slot, hh in ((0, h), (1, h + 4)):
                            tiles.append((hh, qc, kb, slot, kb == kbs[0], kb == 0))

            def geom(tl):
                h, qc, kb, slot, first, last = tl
                d = kb - 4 * qc
                c0 = max(d, 0) * 128
                return h, qc, kb, slot, first, last, d, slice(c0, 512), 512 - c0, h // 2, (h % 2) * 64

            def s0(i, tl):
                h, qc, kb, slot, first, last, d, cs, N, hp, po = geom(tl)
                zb = i % 2
                c0 = cs.start
                p.op("pe", lambda: nc.tensor.matmul(
                    Z[zb][:, cs], lhsT=kT[po:po + 64, hp, kb * 128:(kb + 1) * 128],
                    rhs=qT[po:po + 64, hp, qc * 512 + c0:(qc + 1) * 512], start=True, stop=True),
                    reads=[("sb_q", hp), ("sb_k", hp)], writes=[("P_sbZ", zb)])

            def s1(i, tl):
                h, qc, kb, slot, first, last, d, cs, N, hp, po = geom(tl)
                b, zb = i % NB, i % 2
                p.op("act", lambda: nc.scalar.activation(out=E[b][:, cs], in_=Z[zb][:, cs], func=AF.Exp),
                     reads=[("P_sbZ", zb)], writes=[("sb_E", b)])
                p.op("act", lambda: nc.scalar.activation(out=L[b][:, cs], in_=E[b][:, cs], func=AF.Ln, bias=1.0, scale=1.0),
                     reads=[("sb_E", b)], writes=[("sb_L", b)])
                if d >= 0:
                    p.op("pool", lambda: nc.gpsimd.tensor_tensor(out=Lh[b][:, cs], in0=L[b][:, cs],
                                                                 in1=self.c_sbmask[:, 0:N], op=ALU.mult),
                         reads=[("sb_L", b), "c_sbmask"], writes=[("sb_Lh", b)])
                else:
                    p.op("pool", lambda: nc.gpsimd.tensor_copy(out=Lh[b][:, cs], in_=L[b][:, cs]),
                         reads=[("sb_L", b)], writes=[("sb_Lh", b)])
                p.op("dve", lambda: nc.vector.tensor_tensor(out=Wt[b][:, cs], in0=Z[zb][:, cs], in1=L[b][:, cs], op=ALU.subtract),
                     reads=[("P_sbZ", zb), ("sb_L", b)], writes=[("sb_W", b)])

            def s2(i, tl):
                h, qc, kb, slot, first, last, d, cs, N, hp, po = geom(tl)
                b, zb = i % NB, i % 2
                p.op("pe", lambda: nc.tensor.matmul(SU[zb][:, cs], lhsT=self.c_triu[:], rhs=Lh[b][:, cs], start=True, stop=(d < 0)),
                     reads=["c_triu", ("sb_Lh", b)], writes=[("P_sbSU", zb)])
                if d >= 0:
                    p.op("pe", lambda: nc.tensor.matmul(SU[zb][:, cs], lhsT=self.ident_b[:], rhs=self.c_sbneg[:, 0:N], start=False, stop=True),
                         reads=["ident_b", "c_sbneg"], writes=[("P_sbSU", zb)])
                p.op("dve", lambda: nc.vector.tensor_tensor(out=Wt[b][:, cs], in0=Wt[b][:, cs], in1=SU[zb][:, cs], op=ALU.subtract),
                     reads=[("sb_W", b), ("P_sbSU", zb)], writes=[("sb_W", b)])
                if first:
                    p.op("pe", lambda: nc.tensor.matmul(CS[slot][:, :], lhsT=self.c_zeros[:], rhs=self.c_sbneg[:, :], start=True, stop=False),
                         reads=["c_zeros", "c_sbneg"], writes=[("P_sbCS", slot)])
                else:
                    p.op("dve", lambda: nc.vector.tensor_tensor(out=Wt[b][:, cs], in0=Wt[b][:, cs], in1=CS[slot][:, cs], op=ALU.subtract),
                         reads=[("sb_W", b), ("P_sbCS", slot)], writes=[("sb_W", b)])

            def s3(i, tl):
                h, qc, kb, slot, first, last, d, cs, N, hp, po = geom(tl)
                b = i % NB
                if not last:
                    p.op("pe", lambda: nc.tensor.matmul(CS[slot][:, cs], lhsT=self.c_ones[:], rhs=Lh[b][:, cs], start=False, stop=False),
                         reads=["c_ones", ("sb_Lh", b)], writes=[("P_sbCS", slot)])
                p.op("act", lambda: nc.scalar.activation(out=Ab[b][:, cs], in_=Wt[b][:, cs], func=AF.Exp),
                     reads=[("sb_W", b)], writes=[("sb_Ab", b)])
                p.op("pe", lambda: nc.tensor.matmul(O[slot][0:64, cs], lhsT=V[:, kb, h * 64:(h + 1) * 64], rhs=Ab[b][:, cs],
                                                    start=first, stop=last),
                     reads=[("sb_v", kb), ("sb_Ab", b)], writes=[("P_sbO", slot)])
                if last:
                    p.op("act", lambda: nc.scalar.copy(out=osb[slot][0:64, :], in_=O[slot][0:64, :]),
                         reads=[("P_sbO", slot)], writes=[("sb_o", slot)])
                    tok0 = s * SEQ + qc * 512
                    p.dma("sp", self.YT[hp, po:po + 64, tok0:tok0 + 512], osb[slot][0:64, :],
                          reads=[("sb_o", slot)], writes=[("YT", tok0 // TT)])

            self.pipeline(tiles, [s0, s1, s2, s3], order=(0, 3, 1, 2), lags=(0, 1, 3, 5))
            p.barrier()

    def phase_dil(self, w_in_l, dil_bias, s):
        p, nc = self.p, self.nc
        DILS = (1, 4, 16)
        with ExitStack() as st:
            qT = p.sb("dl_q", [128, 6, SEQ], BF16, st)
            kT = p.sb("dl_k", [128, 6, SEQ], BF16, st)
            V = p.sb("dl_v", [128, 48, 256], BF16, st)
            EB = p.sb("dl_EB", [128, 24 * 128], F32, st)
            p.dma("sp", EB[:], dil_bias, writes=["dl_EB"])
            p.op("act", lambda: nc.scalar.activation(out=EB[:], in_=EB[:], func=AF.Exp), reads=["dl_EB"], writes=["dl_EB"])
            with ExitStack() as st2:
                xT = p.sb("dl_x", [128, KC, SEQ], BF16, st2)
                W = [p.sb(f"dl_w{i}", [128, KC, 256], BF16, st2) for i in range(3)]
                acc = [p.ps(f"dl_acc{i}", [128, 512], F32, st2) for i in range(4)]
                self.load_xT(xT, s, "dl_x")
                n = 0
                for g, r in enumerate(DILS):
                    self.load_w(w_in_l, f"dil_q{g}", W[0], ("dl_w", 0))
                    self.load_w(w_in_l, f"dil_k{g}", W[1], ("dl_w", 1))
                    self.load_w(w_in_l, f"dil_v{g}", W[2], ("dl_w", 2))
                    for wi, dst, sc, dk in ((0, qT, 0.125, "dl_q"), (1, kT, 1.0, "dl_k")):
                        for hp in range(2):
                            for tb in range(4):
                                a = acc[n % 4]
                                ak = ("P_dlacc", n % 4)
                                n += 1
                                self.proj_fm(W[wi], ("dl_w", wi), hp * 128, 128, xT, "dl_x", tb * 512, 512, a, ak)
                                if n % 2 == 0:
                                    p.op("act", lambda a=a, dst=dst, hp=hp, tb=tb, sc=sc, g=g: nc.scalar.activation(
                                        out=dst[:, g * 2 + hp, tb * 512:(tb + 1) * 512], in_=a[:], func=AF.Copy, scale=sc),
                                        reads=[ak], writes=[(dk, g * 2 + hp)])
                                else:
                                    p.op("dve", lambda a=a, dst=dst, hp=hp, tb=tb, sc=sc, g=g: nc.vector.tensor_scalar(
                                        out=dst[:, g * 2 + hp, tb * 512:(tb + 1) * 512], in0=a[:], scalar1=sc, scalar2=None, op0=ALU.mult),
                                        reads=[ak], writes=[(dk, g * 2 + hp)])
                    nloc = 16 // r
                    for c in range(r):
                        for bl in range(nloc):
                            pi = c * nloc + bl
                            a = acc[n % 4]
                            ak = ("P_dlacc", n % 4)
                            n += 1
                            t0 = c + r * 128 * bl
                            self.proj_tm(W[2], ("dl_w", 2), 0, 256, xT, "dl_x",
                                         lambda kc, t0=t0, r=r: xT[:, kc, t0:t0 + 127 * r + 1:r], a, ak)
                            if n % 2 == 0:
                                p.op("act", lambda a=a, g=g, pi=pi: nc.scalar.copy(out=V[:, g * 16 + pi, :], in_=a[:, 0:256]),
                                     reads=[ak], writes=[("dl_v", g * 16 + pi)])
                            else:
                                p.op("dve", lambda a=a, g=g, pi=pi: nc.vector.tensor_copy(out=V[:, g * 16 + pi, :], in_=a[:, 0:256]),
                                     reads=[ak], writes=[("dl_v", g * 16 + pi)])
                p.barrier()
            Pe = [p.sb(f"dl_Pe{i}", [128, 256], F32, st) for i in range(2)]
            Pb = [p.sb(f"dl_Pb{i}", [128, 256], BF16, st) for i in range(2)]
            accU = p.sb("dl_aU", [64, SEQ], F32, st)
            accZ = p.sb("dl_aZ", [64, SEQ], F32, st)
            yb = p.sb("dl_yb", [64, SEQ], BF16, st)
            Sp = [p.ps(f"dl_S{i}", [128, 512], F32, st) for i in range(2)]
            Up = [p.ps(f"dl_U{i}", [128, 512], F32, st) for i in range(2)]
            Zp = [p.ps(f"dl_Z{i}", [128, 512], F32, st) for i in range(2)]
            for hh in range(4):
                po = (hh % 2) * 64
                tiles = []
                for g, r in enumerate(DILS):
                    nloc = 16 // r
                    for c in range(r):
                        for bl in range(nloc):
                            tiles.append((g, r, c, bl, nloc))

                def s1(i, tl):
                    g, r, c, bl, nloc = tl
                    b = i % 2
                    ch = g * 2 + hh // 2
                    tq = c + r * 128 * bl
                    nk = 2 if bl > 0 else 1
                    for kbi in range(nk):
                        tk = c + r * 128 * (bl - kbi)
                        p.op("pe", lambda kbi=kbi, tk=tk: nc.tensor.matmul(
                            Sp[b][:, kbi * 128:(kbi + 1) * 128], lhsT=kT[po:po + 64, ch, tk:tk + 127 * r + 1:r],
                            rhs=qT[po:po + 64, ch, tq:tq + 127 * r + 1:r], start=True, stop=True),
                            reads=[("dl_q", ch), ("dl_k", ch)], writes=[("P_dlS", b)])
                    p.op("act", lambda: nc.scalar.activation(out=Pe[b][:, 0:nk * 128], in_=Sp[b][:, 0:nk * 128], func=AF.Exp),
                         reads=[("P_dlS", b)], writes=[("dl_Pe", b)])
                    e0 = ((g * 4 + hh) * 2) * 128
                    p.op("dve", lambda: nc.vector.tensor_tensor(out=Pb[b][:, 0:nk * 128], in0=Pe[b][:, 0:nk * 128],
                                                                in1=EB[:, e0:e0 + nk * 128], op=ALU.mult),
                         reads=[("dl_Pe", b), "dl_EB"], writes=[("dl_Pb", b)])

                def s2(i, tl):
                    g, r, c, bl, nloc = tl
                    b = i % 2
                    tq = c + r * 128 * bl
                    nk = 2 if bl > 0 else 1
                    for kbi in range(nk):
                        vi = g * 16 + c * nloc + (bl - kbi)
                        p.op("pe", lambda kbi=kbi, vi=vi: nc.tensor.matmul(
                            Up[b][0:64, 0:128], lhsT=V[:, vi, hh * 64:(hh + 1) * 64], rhs=Pb[b][:, kbi * 128:(kbi + 1) * 128],
                            start=(kbi == 0), stop=(kbi == nk - 1)),
                            reads=[("dl_v", vi), ("dl_Pb", b)], writes=[("P_dlU", b)])
                    for kbi in range(nk):
                        p.op("pe", lambda kbi=kbi: nc.tensor.matmul(
                            Zp[b][0:64, 0:128], lhsT=self.c_ones[:, 0:64], rhs=Pb[b][:, kbi * 128:(kbi + 1) * 128],
                            start=(kbi == 0), stop=(kbi == nk - 1)),
                            reads=["c_ones", ("dl_Pb", b)], writes=[("P_dlZ", b)])
                    tsl = slice(tq, tq + 127 * r + 1, r)
                    if g == 0:
                        p.op("dve", lambda: nc.vector.tensor_copy(out=accU[:, tsl], in_=Up[b][0:64, 0:128]),
                             reads=[("P_dlU", b)], writes=["dl_aU"])
                        p.op("act", lambda: nc.scalar.copy(out=accZ[:, tsl], in_=Zp[b][0:64, 0:128]),
                             reads=[("P_dlZ", b)], writes=["dl_aZ"])
                    else:
                        p.op("dve", lambda: nc.vector.tensor_tensor(out=accU[:, tsl], in0=Up[b][0:64, 0:128], in1=accU[:, tsl], op=ALU.add),
                             reads=[("P_dlU", b), "dl_aU"], writes=["dl_aU"])
                        p.op("dve", lambda: nc.vector.tensor_tensor(out=accZ[:, tsl], in0=Zp[b][0:64, 0:128], in1=accZ[:, tsl], op=ALU.add),
                             reads=[("P_dlZ", b), "dl_aZ"], writes=["dl_aZ"])

                self.pipeline(tiles, [s1, s2])
                p.op("dve", lambda: nc.vector.reciprocal(out=accZ[:], in_=accZ[:]), reads=["dl_aZ"], writes=["dl_aZ"])
                p.op("dve", lambda: nc.vector.tensor_tensor(out=yb[:], in0=accU[:], in1=accZ[:], op=ALU.mult),
                     reads=["dl_aU", "dl_aZ"], writes=["dl_yb"])
                p.dma("sp", self.YT[4 + hh // 2, po:po + 64, s * SEQ:(s + 1) * SEQ], yb[:],
                      reads=["dl_yb"], writes=[("YT", (s * SEQ) // TT + i) for i in range(SEQ // TT)])
            p.barrier()

    def phase_dsa(self, w_in_l, dsa_bias, dsa_b31, c_negmask, s):
        p, nc = self.p, self.nc
        with ExitStack() as st:
            qT = p.sb("ds_q", [128, 4, SEQ], BF16, st)
            kT = p.sb("ds_k", [128, 4, SEQ], BF16, st)
            V = p.sb("ds_v", [128, 16, 512], BF16, st)
            qiT = p.sb("ds_qi", [128, 4, SEQ], BF16, st)
            kiT = p.sb("ds_ki", [128, SEQ], BF16, st)
            wabs = p.sb("ds_wabs", [128, 16, 8], F32, st)
            wsgn = p.sb("ds_wsgn", [128, 16, 8], F32, st)
            EB = p.sb("ds_EB", [128, 16 * 128], F32, st)
            b31 = p.sb("ds_b31", [128, 8], F32, st)
            nb31 = p.sb("ds_nb31", [128, 8], F32, st)
            negm = p.sb("ds_negm", [128, 128], F32, st)
            p.dma("sp", EB[:], dsa_bias, writes=["ds_EB"])
            p.dma("sp", b31[:], dsa_b31, writes=["ds_b31"])
            p.dma("sp", negm[:], c_negmask, writes=["ds_negm"])
            p.op("dve", lambda: nc.vector.tensor_scalar(out=nb31[:], in0=b31[:], scalar1=-1.0, scalar2=None, op0=ALU.mult),
                 reads=["ds_b31"], writes=["ds_nb31"])
            EBh = p.sb("ds_EBh", [128, 16 * 128], BF16, st)
            EBl = p.sb("ds_EBl", [128, 16 * 128], BF16, st)
            for h in range(8):
                hs = slice(h * 256, (h + 1) * 256)
                p.op("dve", lambda h=h, hs=hs: nc.vector.tensor_scalar(out=EB[:, hs], in0=EB[:, hs], scalar1=nb31[:, h:h + 1], scalar2=None, op0=ALU.add),
                     reads=["ds_EB", "ds_nb31"], writes=["ds_EB"])
            p.op("dve", lambda: nc.vector.tensor_copy(out=EBh[:], in_=EB[:]), reads=["ds_EB"], writes=["ds_EBh"])
            p.op("dve", lambda: nc.vector.tensor_tensor(out=EBl[:], in0=EB[:], in1=EBh[:], op=ALU.subtract),
                 reads=["ds_EB", "ds_EBh"], writes=["ds_EBl"])
            with ExitStack() as st2:
                xT = p.sb("ds_x", [128, KC, SEQ], BF16, st2)
                W = [p.sb(f"ds_w{i}", [128, KC, 512], BF16, st2) for i in range(2)]
                acc = [p.ps(f"ds_acc{i}", [128, 512], F32, st2) for i in range(4)]
                self.load_xT(xT, s, "ds_x")
                n = 0
                plan = (("dsa_q", qT, 0.125, "ds_q", 4), ("dsa_k", kT, 1.0, "ds_k", 4), ("idx_q", qiT, 1.0, "ds_qi", 4),
                        ("idx_k", kiT, 1.0, "ds_ki", 1))
                for pi, (wname, dst, sc, dk, nhp) in enumerate(plan):
                    wi = pi % 2
                    self.load_w(w_in_l, wname, W[wi], ("ds_w", wi))
                    for hp in range(nhp):
                        for tb in range(4):
                            a = acc[n % 4]
                            ak = ("P_dsacc", n % 4)
                            n += 1
                            self.proj_fm(W[wi], ("ds_w", wi), hp * 128, 128, xT, "ds_x", tb * 512, 512, a, ak)
                            o = dst[:, hp, tb * 512:(tb + 1) * 512] if nhp > 1 else dst[:, tb * 512:(tb + 1) * 512]
                            if n % 2 == 0:
                                p.op("act", lambda a=a, o=o, sc=sc: nc.scalar.activation(out=o, in_=a[:], func=AF.Copy, scale=sc),
                                     reads=[ak], writes=[(dk, hp)])
                            else:
                                p.op("dve", lambda a=a, o=o, sc=sc: nc.vector.tensor_scalar(
                                    out=o, in0=a[:], scalar1=sc, scalar2=None, op0=ALU.mult), reads=[ak], writes=[(dk, hp)])
                self.load_w(w_in_l, "dsa_v", W[0], ("ds_w", 0))
                self.load_w(w_in_l, "idx_w", W[1], ("ds_w", 1))
                for tkb in range(16):
                    a = acc[n % 4]
                    ak = ("P_dsacc", n % 4)
                    n += 1
                    self.proj_tm(W[0], ("ds_w", 0), 0, 512, xT, "ds_x",
                                 lambda kc, tkb=tkb: xT[:, kc, tkb * 128:(tkb + 1) * 128], a, ak)
                    p.op("act", lambda a=a, tkb=tkb: nc.scalar.copy(out=V[:, tkb, :], in_=a[:]),
                         reads=[ak], writes=[("ds_v", tkb)])
                    a = acc[n % 4]
                    ak = ("P_dsacc", n % 4)
                    n += 1
                    self.proj_tm(W[1], ("ds_w", 1), 0, 8, xT, "ds_x",
                                 lambda kc, tkb=tkb: xT[:, kc, tkb * 128:(tkb + 1) * 128], a, ak)
                    p.op("act", lambda a=a, tkb=tkb: nc.scalar.activation(out=wabs[:, tkb, :], in_=a[:, 0:8], func=AF.Abs),
                         reads=[ak], writes=["ds_wabs"])
                    p.op("act", lambda a=a, tkb=tkb: nc.scalar.activation(out=wsgn[:, tkb, :], in_=a[:, 0:8], func=AF.Sign),
                         reads=[ak], writes=["ds_wsgn"])
                p.barrier()
            I = [p.sb(f"ds_I{i}", [128, SEQ], F32, st) for i in range(2)]
            NIT = 24
            m8 = p.sb("ds_m8", [128, 8], F32, st)
            rmin = [p.sb(f"ds_rmin{i}", [128, 1], F32, st) for i in range(2)]
            blo = p.sb("ds_blo", [128, 1], F32, st)
            bmid = p.sb("ds_bmid", [128, 1], F32, st)
            bcnt = p.sb("ds_bcnt", [128, 1], F32, st)
            binc = p.sb("ds_binc", [128, 1], F32, st)
            bw = p.sb("ds_bw", [128, NIT], F32, st)
            pw2 = p.sb("ds_pw2", [128, NIT], F32, st)
            for k in range(NIT):
                p.op("pool", lambda k=k: nc.gpsimd.memset(pw2[:, k:k + 1], 0.5 ** (k + 1)), writes=["ds_pw2"])
            Mq = [p.sb(f"ds_Mq{i}", [128, SEQ], BF16, st) for i in range(2)]
            rl = [p.sb(f"ds_rl{i}", [128, 512], F32, st) for i in range(2)]
            Pb = [p.sb(f"ds_Pb{i}", [128, 512], BF16, st) for i in range(2)]
            rz = p.sb("ds_rz", [64, 128], F32, st)
            yfull = p.sb("ds_y", [64, 8, SEQ], BF16, st)
            DpI = [p.ps(f"ds_DI{i}", [128, 512], F32, st) for i in range(2)]
            DpA = [p.ps(f"ds_DA{i}", [128, 512], F32, st) for i in range(2)]
            Up = [p.ps(f"ds_U{i}", [128, 512], F32, st) for i in range(2)]
            Zp = [p.ps(f"ds_Z{i}", [128, 512], F32, st) for i in range(2)]
            cnt = {"d": 0, "t": 0, "a": 0, "u": 0}

            def idx_thunks(i):
                nk = 128 * (i + 1)
                qs = slice(i * 128, (i + 1) * 128)
                ib = i % 2
                Ii = I[ib]
                ik = ("ds_I", ib)
                th = []
                for h in range(8):
                    hp, po = h // 2, (h % 2) * 64
                    for k0 in range(0, nk, 512):
                        kn = min(512, nk - k0)

                        def f(h=h, hp=hp, po=po, k0=k0, kn=kn):
                            b = cnt["d"] % 2
                            cnt["d"] += 1
                            p.op("pe", lambda: nc.tensor.matmul(
                                DpI[b][:, 0:kn], lhsT=qiT[po:po + 64, hp, qs], rhs=kiT[po:po + 64, k0:k0 + kn], start=True, stop=True),
                                reads=[("ds_qi", hp), ("ds_ki", 0)], writes=[("P_dsDI", b)])
                            p.op("act", lambda: nc.scalar.activation(
                                out=rl[b][:, 0:kn], in_=DpI[b][:, 0:kn], func=AF.Relu, scale=wabs[:, i, h:h + 1]),
                                reads=[("P_dsDI", b), "ds_wabs"], writes=[("ds_rl", b)])
                            if h == 0:
                                p.op("dve", lambda: nc.vector.tensor_scalar(
                                    out=Ii[:, k0:k0 + kn], in0=rl[b][:, 0:kn], scalar1=wsgn[:, i, h:h + 1], scalar2=None, op0=ALU.mult),
                                    reads=[("ds_rl", b), "ds_wsgn"], writes=[ik])
                            else:
                                p.op("dve", lambda: nc.vector.scalar_tensor_tensor(
                                    out=Ii[:, k0:k0 + kn], in0=rl[b][:, 0:kn], scalar=wsgn[:, i, h:h + 1], in1=Ii[:, k0:k0 + kn],
                                    op0=ALU.mult, op1=ALU.add),
                                    reads=[("ds_rl", b), "ds_wsgn", ik], writes=[ik])
                        th.append(f)
                if i >= 2:
                    th.append(lambda: p.op("dve", lambda: nc.vector.tensor_reduce(
                        out=rmin[ib][:], in_=Ii[:, 0:nk], op=ALU.min, axis=AX.X), reads=[ik], writes=[("ds_rmin", ib)]))
                th.append(lambda: p.op("pool", lambda: nc.gpsimd.tensor_tensor(
                    out=Ii[:, i * 128:nk], in0=Ii[:, i * 128:nk], in1=negm[:], op=ALU.add),
                    reads=[ik, "ds_negm"], writes=[ik]))
                return th

            def topk_thunks(i):
                nk = 128 * (i + 1)
                ib = i % 2
                Ii, Mi = I[ib], Mq[ib]
                ik, mk = ("ds_I", ib), ("ds_Mq", ib)
                bk = "ds_bis"
                th = []
                if i >= 2:
                    def init():
                        p.op("dve", lambda: nc.vector.max(out=m8[:], in_=Ii[:, 0:nk]), reads=[ik], writes=["ds_m8"])
                        p.op("dve", lambda: nc.vector.tensor_copy(out=blo[:], in_=rmin[ib][:]), reads=[("ds_rmin", ib), bk], writes=[bk])
                        p.op("dve", lambda: nc.vector.tensor_tensor(out=binc[:], in0=m8[:, 0:1], in1=rmin[ib][:], op=ALU.subtract),
                             reads=["ds_m8", ("ds_rmin", ib), bk], writes=[bk])
                        p.op("dve", lambda: nc.vector.tensor_scalar(out=bw[:], in0=pw2[:], scalar1=binc[:, 0:1], scalar2=None, op0=ALU.mult),
                             reads=["ds_pw2", bk], writes=[bk])
                    th.append(init)
                    for k in range(NIT):
                        def f(k=k):
                            p.op("dve", lambda: nc.vector.tensor_tensor(out=bmid[:], in0=blo[:], in1=bw[:, k:k + 1], op=ALU.add),
                                 reads=[bk], writes=[bk])
                            p.op("dve", lambda: nc.vector.tensor_scalar(out=Mi[:, 0:nk], in0=Ii[:, 0:nk], scalar1=bmid[:, 0:1], scalar2=0.0,
                                                                        op0=ALU.is_ge, op1=ALU.add, accum_out=bcnt[:]),
                                 reads=[ik, bk], writes=[mk, bk])
                            p.op("dve", lambda: nc.vector.tensor_scalar(out=binc[:], in0=bcnt[:], scalar1=255.5, scalar2=bw[:, k:k + 1],
                                                                        op0=ALU.is_ge, op1=ALU.mult),
                                 reads=[bk], writes=[bk])
                            p.op("dve", lambda: nc.vector.tensor_tensor(out=blo[:], in0=blo[:], in1=binc[:], op=ALU.add),
                                 reads=[bk], writes=[bk])
                        th.append(f)
                    th.append(lambda: p.op("dve", lambda: nc.vector.tensor_scalar(
                        out=Mi[:, 0:nk], in0=Ii[:, 0:nk], scalar1=blo[:, 0:1], scalar2=-30000.0, op0=ALU.is_lt, op1=ALU.mult),
                        reads=[ik, bk], writes=[mk]))
                return th

            def att_thunks(i):
                qs = slice(i * 128, (i + 1) * 128)
                ib = i % 2
                Mi = Mq[ib]
                mk = ("ds_Mq", ib)
                th = []
                for h in range(8):
                    hp, po = h // 2, (h % 2) * 64
                    for k4 in range(0, i + 1, 4):
                        kn = min(4, i + 1 - k4)

                        def f(h=h, hp=hp, po=po, k4=k4, kn=kn):
                            b = cnt["a"] % 2
                            cnt["a"] += 1
                            if k4 == 0:
                                cnt["u"] += 1
                            ub = cnt["u"] % 2
                            for j in range(kn):
                                kb = k4 + j
                                kind = i - kb
                                osl = DpA[b][:, j * 128:(j + 1) * 128]
                                extra = (1 if i >= 2 else 0) + (2 if kind <= 1 else 0)
                                p.op("pe", lambda kb=kb, osl=osl, extra=extra: nc.tensor.matmul(
                                    osl, lhsT=kT[po:po + 64, hp, kb * 128:(kb + 1) * 128],
                                    rhs=qT[po:po + 64, hp, qs], start=True, stop=(extra == 0)),
                                    reads=[("ds_q", hp), ("ds_k", hp)], writes=[("P_dsDA", b)])
                                if i >= 2:
                                    extra -= 1
                                    p.op("pe", lambda kb=kb, osl=osl, extra=extra: nc.tensor.matmul(
                                        osl, lhsT=Mi[:, kb * 128:(kb + 1) * 128], rhs=self.ident_b[:], start=False, stop=(extra == 0)),
                                        reads=[mk, "ident_b"], writes=[("P_dsDA", b)], strict_pe=(po == 0))
                                if kind <= 1:
                                    e0 = (h * 2 + kind) * 128
                                    for EBx, nm in ((EBh, "ds_EBh"), (EBl, "ds_EBl")):
                                        extra -= 1
                                        p.op("pe", lambda osl=osl, EBx=EBx, e0=e0, extra=extra: nc.tensor.matmul(
                                            osl, lhsT=self.ident_b[:], rhs=EBx[:, e0:e0 + 128], start=False, stop=(extra == 0)),
                                            reads=[nm, "ident_b"], writes=[("P_dsDA", b)], strict_pe=(po == 0 and i < 2 and EBx is EBh))
                            p.op("act", lambda: nc.scalar.activation(
                                out=Pb[b][:, 0:kn * 128], in_=DpA[b][:, 0:kn * 128], func=AF.Exp, bias=b31[:, h:h + 1], scale=1.0),
                                reads=[("P_dsDA", b), "ds_b31"], writes=[("ds_Pb", b)])
                            for j in range(kn):
                                kb = k4 + j
                                p.op("pe", lambda j=j, kb=kb: nc.tensor.matmul(
                                    Up[ub][0:64, 0:128], lhsT=V[:, kb, h * 64:(h + 1) * 64], rhs=Pb[b][:, j * 128:(j + 1) * 128],
                                    start=(kb == 0), stop=(kb == i)),
                                    reads=[("ds_v", kb), ("ds_Pb", b)], writes=[("P_dsU", ub)])
                            for j in range(kn):
                                kb = k4 + j
                                p.op("pe", lambda j=j, kb=kb: nc.tensor.matmul(
                                    Zp[ub][0:64, 0:128], lhsT=self.c_ones[:, 0:64], rhs=Pb[b][:, j * 128:(j + 1) * 128],
                                    start=(kb == 0), stop=(kb == i)),
                                    reads=["c_ones", ("ds_Pb", b)], writes=[("P_dsZ", ub)])
                            if k4 + kn == i + 1:
                                p.op("dve", lambda: nc.vector.reciprocal(out=rz[:], in_=Zp[ub][0:64, 0:128]),
                                     reads=[("P_dsZ", ub)], writes=["ds_rz"])
                                p.op("dve", lambda: nc.vector.tensor_tensor(out=yfull[:, h, qs], in0=Up[ub][0:64, 0:128], in1=rz[:], op=ALU.mult),
                                     reads=[("P_dsU", ub), "ds_rz"], writes=[("ds_y", h)])
                        th.append(f)
                return th

            def interleave(lists):
                lists = [l for l in lists if l]
                pos = [0] * len(lists)
                total = sum(len(l) for l in lists)
                for _ in range(total):
                    j = min((jj for jj in range(len(lists)) if pos[jj] < len(lists[jj])),
                            key=lambda jj: (pos[jj] + 0.5) / len(lists[jj]))
                    lists[j][pos[j]]()
                    pos[j] += 1

            for step in range(16 + 2):
                ls = []
                if step - 1 >= 0 and step - 1 < 16:
                    ls.append(topk_thunks(step - 1))
                if step - 2 >= 0:
                    ls.append(att_thunks(step - 2))
                if step < 16:
                    ls.append(idx_thunks(step))
                interleave(ls)
            for h in range(8):
                hp, po = h // 2, (h % 2) * 64
                p.dma("sp", self.YT[10 + hp, po:po + 64, s * SEQ:(s + 1) * SEQ], yfull[:, h, :],
                      reads=[("ds_y", h)], writes=[("YT", (s * SEQ) // TT + j) for j in range(SEQ // TT)])
            p.barrier()

    def ssm_discretize(self, st, pre, lr, li, ldt, shape):
        p, nc = self.p, self.nc
        P_, F_ = shape
        TWO_PI = 2.0 * math.pi
        cnt = [0]

        def T(dtype=F32):
            cnt[0] += 1
            return p.sb(f"{pre}_t{cnt[0]}", [P_, F_], dtype, st)

        key = pre + "_k"

        def dve(fn):
            p.op("dve", fn, reads=[key], writes=[key])

        def act(fn):
            p.op("act", fn, reads=[key], writes=[key])

        dt = T(); mag = T(); ang = T(); tmp = T(); ki = T(mybir.dt.int32); kf = T(); r = T(); msk = T()
        sn = T(); cs = T(); a_re = T(); a_im = T(); den = T(); f_re = T(); f_im = T(); am1 = T()
        act(lambda: nc.scalar.activation(out=dt[:], in_=ldt[:], func=AF.Exp))
        dve(lambda: nc.vector.tensor_tensor(out=tmp[:], in0=lr[:], in1=dt[:], op=ALU.mult))
        act(lambda: nc.scalar.activation(out=mag[:], in_=tmp[:], func=AF.Exp))
        dve(lambda: nc.vector.tensor_tensor(out=ang[:], in0=li[:], in1=dt[:], op=ALU.mult))
        for off, dst in ((0.0, sn), (0.5 * math.pi, cs)):
            dve(lambda off=off: nc.vector.tensor_scalar(out=tmp[:], in0=ang[:], scalar1=off, scalar2=1.0 / TWO_PI,
                                                        op0=ALU.add, op1=ALU.mult))
            dve(lambda: nc.vector.tensor_copy(out=ki[:], in_=tmp[:]))
            dve(lambda: nc.vector.tensor_copy(out=kf[:], in_=ki[:]))
            dve(lambda off=off: nc.vector.tensor_scalar(out=r[:], in0=ang[:], scalar1=off, scalar2=None, op0=ALU.add))
            dve(lambda: nc.vector.scalar_tensor_tensor(out=r[:], in0=kf[:], scalar=-TWO_PI, in1=r[:], op0=ALU.mult, op1=ALU.add))
            dve(lambda: nc.vector.tensor_scalar(out=msk[:], in0=r[:], scalar1=math.pi, scalar2=None, op0=ALU.is_gt))
            dve(lambda: nc.vector.scalar_tensor_tensor(out=r[:], in0=msk[:], scalar=-TWO_PI, in1=r[:], op0=ALU.mult, op1=ALU.add))
            dve(lambda: nc.vector.tensor_scalar(out=msk[:], in0=r[:], scalar1=-math.pi, scalar2=None, op0=ALU.is_lt))
            dve(lambda: nc.vector.scalar_tensor_tensor(out=r[:], in0=msk[:], scalar=TWO_PI, in1=r[:], op0=ALU.mult, op1=ALU.add))
            dve(lambda: nc.vector.tensor_scalar(out=r[:], in0=r[:], scalar1=math.pi, scalar2=-math.pi, op0=ALU.min, op1=ALU.max))
            act(lambda dst=dst: nc.scalar.activation(out=dst[:], in_=r[:], func=AF.Sin))
        dve(lambda: nc.vector.tensor_tensor(out=a_re[:], in0=mag[:], in1=cs[:], op=ALU.mult))
        dve(lambda: nc.vector.tensor_tensor(out=a_im[:], in0=mag[:], in1=sn[:], op=ALU.mult))
        dve(lambda: nc.vector.tensor_tensor(out=den[:], in0=lr[:], in1=lr[:], op=ALU.mult))
        dve(lambda: nc.vector.tensor_tensor(out=tmp[:], in0=li[:], in1=li[:], op=ALU.mult))
        dve(lambda: nc.vector.tensor_tensor(out=den[:], in0=den[:], in1=tmp[:], op=ALU.add))
        dve(lambda: nc.vector.reciprocal(out=den[:], in_=den[:]))
        dve(lambda: nc.vector.tensor_scalar(out=am1[:], in0=a_re[:], scalar1=-1.0, scalar2=None, op0=ALU.add))
        dve(lambda: nc.vector.tensor_tensor(out=f_re[:], in0=am1[:], in1=lr[:], op=ALU.mult))
        dve(lambda: nc.vector.tensor_tensor(out=tmp[:], in0=a_im[:], in1=li[:], op=ALU.mult))
        dve(lambda: nc.vector.tensor_tensor(out=f_re[:], in0=f_re[:], in1=tmp[:], op=ALU.add))
        dve(lambda: nc.vector.tensor_tensor(out=f_re[:], in0=f_re[:], in1=den[:], op=ALU.mult))
        dve(lambda: nc.vector.tensor_tensor(out=f_im[:], in0=a_im[:], in1=lr[:], op=ALU.mult))
        dve(lambda: nc.vector.tensor_tensor(out=tmp[:], in0=am1[:], in1=li[:], op=ALU.mult))
        dve(lambda: nc.vector.tensor_tensor(out=f_im[:], in0=f_im[:], in1=tmp[:], op=ALU.subtract))
        dve(lambda: nc.vector.tensor_tensor(out=f_im[:], in0=f_im[:], in1=den[:], op=ALU.mult))
        return a_re, a_im, f_re, f_im, key, (cs, sn, mag)

    def phase_ssm(self, w_in_l, sp, s):
        p, nc = self.p, self.nc
        NS = 11
        with ExitStack() as st:
            ufT = p.sb("sm_uf", [128, 6, SEQ], F32, st)
            ubT = p.sb("sm_ub", [128, 6, SEQ], BF16, st)
            BBre = p.sb("sm_BBre", [128, 6, 128], BF16, st)
            BBim = p.sb("sm_BBim", [128, 6, 128], BF16, st)
            Are = p.sb("sm_Are", [128, 16, NS], F32, st)
            Aim = p.sb("sm_Aim", [128, 16, NS], F32, st)
            nAim = p.sb("sm_nAim", [128, 16, NS], F32, st)
            rho = p.sb("sm_rho", [128, 16], F32, st)
            Cre = p.sb("sm_Cre", [128, 16, 32], BF16, st)
            nCim = p.sb("sm_nCim", [128, 16, 32], BF16, st)
            dsk = p.sb("sm_d", [128, 6], F32, st)
            p.dma("sp", dsk[:], sp["d"], writes=["sm_d"])
            p.dma("pool", Cre[:], sp["c_re"], writes=["sm_Cre"])
            with ExitStack() as st1:
                def ld(name, shape, src):
                    t = p.sb(name, shape, F32, st1)
                    p.dma("sp", t[:], src, writes=[name])
                    return t
                lrB = ld("sm_lrB", [128, 768], sp["lamB_re"]); liB = ld("sm_liB", [128, 768], sp["lamB_im"])
                ldB = ld("sm_ldB", [128, 768], sp["lamB_dt"])
                bre = ld("sm_bre", [128, 768], sp["b_re"]); bim = ld("sm_bim", [128, 768], sp["b_im"])
                lrS = ld("sm_lrS", [128, 16], sp["lamS_re"]); liS = ld("sm_liS", [128, 16], sp["lamS_im"])
                ldS = ld("sm_ldS", [128, 16], sp["lamS_dt"])
                cim = ld("sm_cim", [128, 512], sp["c_im"])
                p.barrier()
                _, _, f_re, f_im, kB, _ = self.ssm_discretize(st1, "smB", lrB, liB, ldB, (128, 768))
                t1 = p.sb("sm_pt1", [128, 768], F32, st1)
                t2 = p.sb("sm_pt2", [128, 768], F32, st1)
                kk = ["sm_prep", kB]
                seq = [
                    lambda: nc.vector.tensor_tensor(out=t1[:], in0=bre[:], in1=f_re[:], op=ALU.mult),
                    lambda: nc.vector.tensor_tensor(out=t2[:], in0=bim[:], in1=f_im[:], op=ALU.mult),
                    lambda: nc.vector.tensor_tensor(out=BBre[:].rearrange("p m j -> p (m j)"), in0=t1[:], in1=t2[:], op=ALU.subtract),
                    lambda: nc.vector.tensor_tensor(out=t1[:], in0=bre[:], in1=f_im[:], op=ALU.mult),
                    lambda: nc.vector.tensor_tensor(out=t2[:], in0=bim[:], in1=f_re[:], op=ALU.mult),
                    lambda: nc.vector.tensor_tensor(out=BBim[:].rearrange("p m j -> p (m j)"), in0=t1[:], in1=t2[:], op=ALU.add),
                    lambda: nc.vector.tensor_scalar(out=nCim[:].rearrange("p m j -> p (m j)"), in0=cim[:], scalar1=-1.0, scalar2=None, op0=ALU.mult),
                ]
                for fn in seq:
                    p.op("dve", fn, reads=kk, writes=kk)
                _, _, _, _, kS, (ucs, usn, umag) = self.ssm_discretize(st1, "smS", lrS, liS, ldS, (128, 16))
                kk = ["sm_prep", kS]
                sq1 = p.sb("sm_sq1", [128, 16], F32, st1)
                sq2 = p.sb("sm_sq2", [128, 16], F32, st1)
                p.op("dve", lambda: nc.vector.tensor_copy(out=Are[:, :, 0], in_=ucs[:]), reads=kk, writes=kk)
                p.op("dve", lambda: nc.vector.tensor_copy(out=Aim[:, :, 0], in_=usn[:]), reads=kk, writes=kk)
                p.op("dve", lambda: nc.vector.tensor_copy(out=rho[:], in_=umag[:]), reads=kk, writes=kk)
                for k in range(1, NS):
                    for fn in (
                        lambda k=k: nc.vector.tensor_tensor(out=sq1[:], in0=Are[:, :, k - 1], in1=Are[:, :, k - 1], op=ALU.mult),
                        lambda k=k: nc.vector.tensor_tensor(out=sq2[:], in0=Aim[:, :, k - 1], in1=Aim[:, :, k - 1], op=ALU.mult),
                        lambda k=k: nc.vector.tensor_tensor(out=Are[:, :, k], in0=sq1[:], in1=sq2[:], op=ALU.subtract),
                        lambda k=k: nc.vector.tensor_tensor(out=sq1[:], in0=Are[:, :, k - 1], in1=Aim[:, :, k - 1], op=ALU.mult),
                        lambda k=k: nc.vector.tensor_scalar(out=Aim[:, :, k], in0=sq1[:], scalar1=2.0, scalar2=None, op0=ALU.mult),
                    ):
                        p.op("dve", fn, reads=kk, writes=kk)
                p.op("dve", lambda: nc.vector.tensor_scalar(out=nAim[:], in0=Aim[:], scalar1=-1.0, scalar2=None, op0=ALU.mult),
                     reads=kk, writes=kk)
                p.barrier()
            with ExitStack() as st2:
                xT = p.sb("sm_x", [128, KC, SEQ], BF16, st2)
                W = p.sb("sm_w", [128, KC, 768], BF16, st2)
                acc = [p.ps(f"sm_acc{i}", [128, 512], F32, st2) for i in range(2)]
                self.load_xT(xT, s, "sm_x")
                self.load_w(w_in_l, "ssm_u", W, "sm_w")
                n = 0
                for m in range(6):
                    for tb in range(4):
                        a = acc[n % 2]
                        ak = ("P_smacc", n % 2)
                        n += 1
                        self.proj_fm(W, "sm_w", m * 128, 128, xT, "sm_x", tb * 512, 512, a, ak)
                        p.op("act", lambda a=a, m=m, tb=tb: nc.scalar.copy(out=ufT[:, m, tb * 512:(tb + 1) * 512], in_=a[:]),
                             reads=[ak], writes=[("sm_uf", m)])
                        p.op("dve", lambda a=a, m=m, tb=tb: nc.vector.tensor_copy(out=ubT[:, m, tb * 512:(tb + 1) * 512], in_=a[:]),
                             reads=[ak], writes=[("sm_ub", m)])
                p.barrier()
            glT = p.sb("sm_gl", [128, 6, SEQ], BF16, st)
            with ExitStack() as st3:
                Ere = [p.sb(f"sm_Ere{i}", [128, SEQ], F32, st3) for i in range(2)]
                Eim = [p.sb(f"sm_Eim{i}", [128, SEQ], F32, st3) for i in range(2)]
                Mre = p.sb("sm_Mre", [128, SEQ], F32, st3)
                Mim = p.sb("sm_Mim", [128, SEQ], F32, st3)
                Sre = p.sb("sm_Sre", [128, SEQ], F32, st3)
                Sim = p.sb("sm_Sim", [128, SEQ], F32, st3)
                rhoT = [p.sb("sm_rhoT", [128, SEQ], F32, st3)] * 2
                tm = [p.sb(f"sm_tm{i}", [128, 512], F32, st3) for i in range(2)]
                xbr = p.sb("sm_xbr", [128, SEQ], BF16, st3)
                xbi = p.sb("sm_xbi", [128, SEQ], BF16, st3)
                yvs = [p.sb(f"sm_yv{i}", [128, 512], F32, st3) for i in range(2)]
                y2s = [p.sb(f"sm_y2{i}", [128, 512], F32, st3) for i in range(2)]
                BUr = [p.ps(f"sm_BUr{i}", [128, 512], F32, st3) for i in range(2)]
                BUi = [p.ps(f"sm_BUi{i}", [128, 512], F32, st3) for i in range(2)]
                Yp = [p.ps(f"sm_Y{i}", [128, 512], F32, st3) for i in range(4)]

                def gen_table(sb_):
                    e = sb_ % 2
                    ek = ("sm_E", e)
                    steps = []

                    def init():
                        p.op("pool", lambda: nc.gpsimd.memset(Ere[e][:, 0:1], 1.0), writes=[ek])
                        p.op("pool", lambda: nc.gpsimd.memset(Eim[e][:, 0:1], 0.0), writes=[ek])
                    steps.append(init)
                    for k in range(NS):
                        steps.append(lambda k=k: gen_step(sb_, e, ek, k))
                    return steps

                def gen_step(sb_, e, ek, k):
                    if True:
                        d = 1 << k
                        ur, ui, nui = Are[:, sb_, k:k + 1], Aim[:, sb_, k:k + 1], nAim[:, sb_, k:k + 1]
                        p.op("act", lambda d=d, ur=ur: nc.scalar.activation(out=Ere[e][:, d:2 * d], in_=Ere[e][:, 0:d], func=AF.Copy, scale=ur),
                             reads=[ek, "sm_A"], writes=[ek])
                        p.op("act", lambda d=d, ur=ur: nc.scalar.activation(out=Eim[e][:, d:2 * d], in_=Eim[e][:, 0:d], func=AF.Copy, scale=ur),
                             reads=[ek, "sm_A"], writes=[ek])
                        p.op("dve", lambda d=d, nui=nui: nc.vector.scalar_tensor_tensor(
                            out=Ere[e][:, d:2 * d], in0=Eim[e][:, 0:d], scalar=nui, in1=Ere[e][:, d:2 * d], op0=ALU.mult, op1=ALU.add),
                            reads=[ek, "sm_A"], writes=[ek])
                        p.op("dve", lambda d=d, ui=ui: nc.vector.scalar_tensor_tensor(
                            out=Eim[e][:, d:2 * d], in0=Ere[e][:, 0:d], scalar=ui, in1=Eim[e][:, d:2 * d], op0=ALU.mult, op1=ALU.add),
                            reads=[ek, "sm_A"], writes=[ek])

                for th_ in gen_table(0):
                    th_()
                nbu = 0
                for sb_ in range(16):
                    m, q0 = sb_ // 3, (sb_ % 3) * 32
                    e = sb_ % 2
                    ek = ("sm_E", e)
                    nxt = gen_table(sb_ + 1) if sb_ + 1 < 16 else []
                    for tb in range(4):
                        ts_ = slice(tb * 512, (tb + 1) * 512)
                        bb = nbu % 2
                        nbu += 1
                        BUr_, BUi_ = [BUr[bb]], [BUi[bb]]
                        kr_, ki_ = ("P_smBUr", bb), ("P_smBUi", bb)
                        p.op("pe", lambda ts_=ts_: nc.tensor.matmul(BUr_[0][:], lhsT=BBre[q0:q0 + 32, m, :], rhs=ubT[q0:q0 + 32, m, ts_], start=True, stop=True),
                             reads=["sm_BB", ("sm_ub", m)], writes=[kr_])
                        p.op("pe", lambda ts_=ts_: nc.tensor.matmul(BUi_[0][:], lhsT=BBim[q0:q0 + 32, m, :], rhs=ubT[q0:q0 + 32, m, ts_], start=True, stop=True),
                             reads=["sm_BB", ("sm_ub", m)], writes=[ki_])
                        p.op("dve", lambda ts_=ts_: nc.vector.tensor_tensor(out=tm[0][:], in0=BUi_[0][:], in1=Eim[e][:, ts_], op=ALU.mult),
                             reads=[ki_, ek], writes=[("sm_tm", 0)])
                        p.op("dve", lambda ts_=ts_: nc.vector.tensor_tensor(out=Mre[:, ts_], in0=BUr_[0][:], in1=Ere[e][:, ts_], op=ALU.mult),
                             reads=[kr_, ek], writes=["sm_Mre"])
                        if nxt:
                            nxt.pop(0)()
                        p.op("dve", lambda ts_=ts_: nc.vector.tensor_tensor(out=tm[1][:], in0=BUr_[0][:], in1=Eim[e][:, ts_], op=ALU.mult),
                             reads=[kr_, ek], writes=[("sm_tm", 1)])
                        p.op("dve", lambda ts_=ts_: nc.vector.tensor_tensor(out=Mim[:, ts_], in0=BUi_[0][:], in1=Ere[e][:, ts_], op=ALU.mult),
                             reads=[ki_, ek], writes=["sm_Mim"])
                        if nxt:
                            nxt.pop(0)()
                        p.op("dve", lambda ts_=ts_: nc.vector.tensor_tensor(out=Mre[:, ts_], in0=Mre[:, ts_], in1=tm[0][:], op=ALU.add),
                             reads=["sm_Mre", ("sm_tm", 0)], writes=["sm_Mre"])
                        p.op("dve", lambda ts_=ts_: nc.vector.tensor_tensor(out=Mim[:, ts_], in0=Mim[:, ts_], in1=tm[1][:], op=ALU.subtract),
                             reads=["sm_Mim", ("sm_tm", 1)], writes=["sm_Mim"])
                        if nxt:
                            nxt.pop(0)()
                    p.op("act", lambda: nc.scalar.activation(out=rhoT[e][:], in_=Ere[e][:], func=AF.Identity, scale=0.0, bias=rho[:, sb_:sb_ + 1]),
                         reads=[ek, "sm_A", "sm_Sre", "sm_Sim"], writes=["sm_rhoT"])
                    p.op("dve", lambda: nc.vector.tensor_tensor_scan(out=Sre[:], data0=rhoT[e][:], data1=Mre[:], initial=0.0, op0=ALU.mult, op1=ALU.add),
                         reads=["sm_rhoT", "sm_Mre"], writes=["sm_Sre"])
                    p.op("dve", lambda: nc.vector.tensor_tensor_scan(out=Sim[:], data0=rhoT[e][:], data1=Mim[:], initial=0.0, op0=ALU.mult, op1=ALU.add),
                         reads=["sm_rhoT", "sm_Mim"], writes=["sm_Sim"])
                    p.op("dve", lambda: nc.vector.tensor_tensor(out=Mre[:], in0=Sim[:], in1=Eim[e][:], op=ALU.mult),
                         reads=["sm_Sim", ek, "sm_Mre"], writes=["sm_Mre"])
                    p.op("dve", lambda: nc.vector.tensor_tensor(out=Mim[:], in0=Sre[:], in1=Ere[e][:], op=ALU.mult),
                         reads=["sm_Sre", ek, "sm_Mim"], writes=["sm_Mim"])
                    p.op("dve", lambda: nc.vector.tensor_tensor(out=xbr[:], in0=Mim[:], in1=Mre[:], op=ALU.subtract),
                         reads=["sm_Mre", "sm_Mim"], writes=["sm_xbr"])
                    p.op("dve", lambda: nc.vector.tensor_tensor(out=Mre[:], in0=Sre[:], in1=Eim[e][:], op=ALU.mult),
                         reads=["sm_Sre", ek, "sm_Mre", "sm_xbr"], writes=["sm_Mre"])
                    p.op("dve", lambda: nc.vector.tensor_tensor(out=Mim[:], in0=Sim[:], in1=Ere[e][:], op=ALU.mult),
                         reads=["sm_Sim", ek, "sm_Mim", "sm_xbr"], writes=["sm_Mim"])
                    p.op("dve", lambda: nc.vector.tensor_tensor(out=xbi[:], in0=Mim[:], in1=Mre[:], op=ALU.add),
                         reads=["sm_Mre", "sm_Mim"], writes=["sm_xbi"])
                    while nxt:
                        nxt.pop(0)()
                    for tb in range(4):
                        ts_ = slice(tb * 512, (tb + 1) * 512)
                        p.op("pe", lambda tb=tb, ts_=ts_: nc.tensor.matmul(Yp[tb][q0:q0 + 32, :], lhsT=Cre[:, sb_, :], rhs=xbr[:, ts_],
                                                                           start=True, stop=False),
                             reads=["sm_Cre", "sm_xbr"], writes=[("P_smY", tb)])
                        p.op("pe", lambda tb=tb, ts_=ts_: nc.tensor.matmul(Yp[tb][q0:q0 + 32, :], lhsT=nCim[:, sb_, :], rhs=xbi[:, ts_],
                                                                           start=False, stop=True),
                             reads=["sm_Cre", "sm_xbi"], writes=[("P_smY", tb)])
                    if sb_ % 3 == 2 or sb_ == 15:
                        nr = 96 if m < 5 else 32

                        def gA(tb):
                            ts_ = slice(tb * 512, (tb + 1) * 512)
                            yv_, y2_ = yvs[tb % 2], y2s[tb % 2]
                            kv, k2 = ("sm_yv", tb % 2), ("sm_y2", tb % 2)
                            p.op("dve", lambda: nc.vector.scalar_tensor_tensor(
                                out=yv_[0:nr, :], in0=ufT[0:nr, m, ts_], scalar=dsk[0:nr, m:m + 1], in1=Yp[tb][0:nr, :], op0=ALU.mult, op1=ALU.add),
                                reads=[("sm_uf", m), "sm_d", ("P_smY", tb)], writes=[kv])
                            p.op("dve", lambda: nc.vector.tensor_tensor(out=y2_[0:nr, :], in0=yv_[0:nr, :], in1=yv_[0:nr, :], op=ALU.mult),
                                 reads=[kv], writes=[k2])
                            p.op("dve", lambda: nc.vector.tensor_scalar(out=y2_[0:nr, :], in0=y2_[0:nr, :], scalar1=0.044715, scalar2=1.0,
                                                                        op0=ALU.mult, op1=ALU.add),
                                 reads=[k2], writes=[k2])
                            p.op("dve", lambda: nc.vector.tensor_tensor(out=y2_[0:nr, :], in0=y2_[0:nr, :], in1=yv_[0:nr, :], op=ALU.mult),
                                 reads=[k2, kv], writes=[k2])
                            p.op("act", lambda: nc.scalar.activation(out=y2_[0:nr, :], in_=y2_[0:nr, :], func=AF.Sigmoid, scale=2.0 * math.sqrt(2.0 / math.pi)),
                                 reads=[k2], writes=[k2])

                        def gB(tb):
                            ts_ = slice(tb * 512, (tb + 1) * 512)
                            yv_, y2_ = yvs[tb % 2], y2s[tb % 2]
                            kv, k2 = ("sm_yv", tb % 2), ("sm_y2", tb % 2)
                            p.op("dve", lambda: nc.vector.tensor_tensor(out=glT[0:nr, m, ts_], in0=yv_[0:nr, :], in1=y2_[0:nr, :], op=ALU.mult),
                                 reads=[kv, k2], writes=[("sm_gl", m)])

                        gA(0); gA(1); gB(0); gA(2); gB(1); gA(3); gB(2); gB(3)
                p.barrier()
            Wg = p.sb("sm_Wg", [128, 6, 1024], BF16, st)
            osb = [p.sb(f"sm_o{i}", [128, 512], BF16, st) for i in range(2)]
            sgm = p.sb("sm_sg2", [128, 512], F32, st)
            GA = p.ps("sm_GA", [128, 512], F32, st)
            GB = p.ps("sm_GB", [128, 512], F32, st)
            p.dma("pool", Wg[:], sp["w_glu"], writes=["sm_Wg"], max_dma_last_dim=8192)
            no = 0
            for oc in range(4):
                for tb in range(4):
                    ts_ = slice(tb * 512, (tb + 1) * 512)
                    for acc_, c0, ak in ((GA, oc * 128, "P_smGA"), (GB, 512 + oc * 128, "P_smGB")):
                        for kc in range(6):
                            kr = 96 if kc < 5 else 32
                            p.op("pe", lambda acc_=acc_, c0=c0, kc=kc, ts_=ts_, kr=kr: nc.tensor.matmul(
                                acc_[:], lhsT=Wg[0:kr, kc, c0:c0 + 128], rhs=glT[0:kr, kc, ts_], start=(kc == 0), stop=(kc == 5)),
                                reads=["sm_Wg", ("sm_gl", kc)], writes=[ak])
                    oi = no % 2
                    no += 1
                    p.op("act", lambda: nc.scalar.activation(out=sgm[:], in_=GB[:], func=AF.Sigmoid), reads=["P_smGB"], writes=["sm_sg"])
                    p.op("dve", lambda oi=oi: nc.vector.tensor_tensor(out=osb[oi][:], in0=GA[:], in1=sgm[:], op=ALU.mult),
                         reads=["P_smGA", "sm_sg"], writes=[("sm_o", oi)])
                    tok0 = s * SEQ + tb * 512
                    p.dma("sp", self.YT[6 + oc, :, tok0:tok0 + 512], osb[oi][:], reads=[("sm_o", oi)], writes=[("YT", tok0 // TT)])
            p.barrier()


def t5_bucket_np(dist):
    dist = np.asarray(dist, np.int64)
    d = np.maximum(dist, 1).astype(np.float32)
    large = 16 + (np.log(d / np.float32(16)) / np.float32(math.log(128 / 16)) * np.float32(16)).astype(np.int32)
    large = np.minimum(large, 31)
    return np.where(dist < 16, dist, large)


def lay_dil_bias(rel_bias):
    out = np.full((128, 3, 4, 2, 128), -1e30, np.float32)
    sl = np.arange(128)[:, None]
    tl = np.arange(128)[None, :]
    for g, r in enumerate((1, 4, 16)):
        for kind in range(2):
            dloc = tl - sl + 128 * kind
            valid = (dloc >= 0) & (dloc <= 128)
            bk = t5_bucket_np(np.clip(dloc, 0, None) * r)
            for hh in range(4):
                vals = rel_bias[bk, g * 4 + hh]
                out[:, g, hh, kind, :] = np.where(valid, vals, np.float32(-1e30))
    return out.reshape(128, 24 * 128)


def lay_dsa_bias(rel_bias):
    out = np.full((128, 8, 2, 128), -1e30, np.float32)
    sl = np.arange(128)[:, None]
    tl = np.arange(128)[None, :]
    for kind in range(2):
        d = tl - sl + 128 * kind
        valid = d >= 0
        bk = t5_bucket_np(np.clip(d, 0, None))
        for h in range(8):
            out[:, h, kind, :] = np.where(valid, rel_bias[bk, 12 + h], np.float32(-1e30))
    b31 = np.ascontiguousarray(np.broadcast_to(rel_bias[31, 12:20][None, :], (128, 8))).astype(np.float32)
    return out.reshape(128, 16 * 128), b31


def lay_ssm(lam_re, lam_im, log_dt, b_re, b_im, c_re, c_im, d_skip, w_glu):
    out = {}
    q = np.arange(128)
    qd, gl, c = np.minimum(q // 32, 2), (q % 32) // 16, q % 16
    qvalid = q < 96
    mB = np.arange(6)
    j = np.arange(128)
    glp, pj = j // 64, j % 64
    sbq = 3 * mB[None, :, None] + qd[:, None, None]
    okB = (sbq < 16) & qvalid[:, None, None]
    sbq = np.minimum(sbq, 15)
    gB = 2 * sbq + glp[None, None, :]
    pB = np.broadcast_to(pj[None, None, :], gB.shape)
    out["lamB_re"] = lam_re[gB, pB].reshape(128, 768).astype(np.float32)
    out["lamB_im"] = lam_im[gB, pB].reshape(128, 768).astype(np.float32)
    out["lamB_dt"] = log_dt[gB].reshape(128, 768).astype(np.float32)
    same = (gl[:, None, None] == glp[None, None, :]) & okB
    cB = np.broadcast_to(c[:, None, None], gB.shape)
    out["b_re"] = np.where(same, b_re[gB, pB, cB], 0.0).reshape(128, 768).astype(np.float32)
    out["b_im"] = np.where(same, b_im[gB, pB, cB], 0.0).reshape(128, 768).astype(np.float32)
    sbS = np.arange(16)[None, :]
    gS = 2 * sbS + glp[:, None]
    pS = np.broadcast_to(pj[:, None], gS.shape)
    out["lamS_re"] = lam_re[gS, pS].astype(np.float32)
    out["lamS_im"] = lam_im[gS, pS].astype(np.float32)
    out["lamS_dt"] = log_dt[gS].astype(np.float32)
    ch = np.arange(32)
    glc, cc = ch // 16, ch % 16
    gC = 2 * np.arange(16)[None, :, None] + glp[:, None, None] + 0 * glc[None, None, :]
    sameC = (glp[:, None, None] == glc[None, None, :]) & np.ones((1, 16, 1), bool)
    cC = np.broadcast_to(cc[None, None, :], gC.shape)
    pC = np.broadcast_to(pj[:, None, None], gC.shape)
    out["c_re"] = np.where(sameC, c_re[gC, cC, pC], 0.0).astype(np.float32)
    out["c_im"] = np.where(sameC, c_im[gC, cC, pC], 0.0).reshape(128, 512).astype(np.float32)
    chan = 96 * mB[None, :] + q[:, None]
    okc = (chan < 512) & qvalid[:, None]
    chc = np.minimum(chan, 511)
    out["d"] = np.where(okc, d_skip[chc], 0.0).astype(np.float32)
    out["w_glu"] = np.where(okc[:, :, None], w_glu[chc, :], 0.0).astype(np.float32)
    return out


SSM_SHAPES = {"lamB_re": [128, 768], "lamB_im": [128, 768], "lamB_dt": [128, 768], "b_re": [128, 768], "b_im": [128, 768],
              "lamS_re": [128, 16], "lamS_im": [128, 16], "lamS_dt": [128, 16], "c_re": [128, 16, 32], "c_im": [128, 512],
              "d": [128, 6], "w_glu": [128, 6, 1024]}


def lay_up(w):
    return np.ascontiguousarray(w.reshape(16, 128, 2, 22, 256).transpose(3, 2, 1, 0, 4))


def lay_down(w):
    return np.ascontiguousarray(w.reshape(44, 128, 16, 128).transpose(2, 1, 0, 3))


def lay_br(ws):
    w = np.concatenate(ws, 0).reshape(NYC, 128, 16, 128)
    return np.ascontiguousarray(w.transpose(2, 1, 0, 3))


def lay_sq(w):
    return np.ascontiguousarray(w.reshape(16, 128, 16, 128).transpose(2, 1, 0, 3))


def lay_ln(v):
    return np.ascontiguousarray(v.reshape(6, 16, 128).transpose(2, 0, 1).reshape(128, 96))


N_CORES = 8
TTOT_CORE = 2 * SEQ


def build_full(ttot=TTOT_CORE, depth=DEPTH):
    b = Builder(ttot)
    nseq = ttot // SEQ
    x = b.inp("x", [ttot, D])
    y = b.nc.dram_tensor("y", [ttot, D], F32, kind="ExternalOutput").ap()
    L = []
    for l in range(depth):
        d = {}
        for f in ("ffn1", "ffn2"):
            d[f + "_up"] = b.inp(f"{f}_up{l}", [22, 2, 128, 16, 256])
            d[f + "_down"] = b.inp(f"{f}_down{l}", [16, 128, 44, 128])
        d["w_in"] = b.inp(f"w_in{l}", [128, WIN_TOT])
        d["w_br"] = b.inp(f"w_br{l}", [16, 128, NYC, 128])
        d["w_out"] = b.inp(f"w_out{l}", [16, 128, 16, 128])
        d["ssm"] = {k: b.inp(f"ssm{l}_{k}", SSM_SHAPES[k]) for k in SSM_SHAPES}
        L.append(d)
    dil_bias = b.inp("dil_bias", [128, 24 * 128])
    dsa_bias = b.inp("dsa_bias", [128, 16 * 128])
    dsa_b31 = b.inp("dsa_b31", [128, 8])
    negm = b.inp("cm_negmask", [128, 128])
    b.load_consts()
    b.load_mixer_consts()
    b.p.barrier()
    b.phase_in(x)
    for l in range(depth):
        d = L[l]
        b.phase_ffn(d["ffn1_up"], d["ffn1_down"], l * 3 + 0)
        for s in range(nseq):
            b.phase_sb(d["w_in"], s)
            b.phase_dil(d["w_in"], dil_bias, s)
            b.phase_ssm(d["w_in"], d["ssm"], s)
            b.phase_dsa(d["w_in"], dsa_bias, dsa_b31, negm, s)
        b.phase_merge(d["w_in"], d["w_br"], d["w_out"], l * 3 + 1)
        b.phase_ffn(d["ffn2_up"], d["ffn2_down"], l * 3 + 2)
    b.phase_out(y)
    b.p.finish()
    return b


def host_inputs(inp, depth=DEPTH):
    f = lambda a: np.asarray(a, np.float32)
    sh = {}
    for l in range(depth):
        sh[f"ffn1_up{l}"] = lay_up(f(inp["ffn1_w_up"][l]))
        sh[f"ffn1_down{l}"] = lay_down(f(inp["ffn1_w_down"][l]))
        sh[f"ffn2_up{l}"] = lay_up(f(inp["ffn2_w_up"][l]))
        sh[f"ffn2_down{l}"] = lay_down(f(inp["ffn2_w_down"][l]))
        sh[f"w_in{l}"] = lay_w_in(f(inp["w_in"][l]))
        sh[f"w_br{l}"] = lay_br([f(inp[k][l]) for k in ("w_br_sb", "w_br_dil", "w_br_ssm", "w_br_dsa")])
        sh[f"w_out{l}"] = lay_sq(f(inp["w_out"][l]))
        ss = lay_ssm(f(inp["ssm_lam_re"][l]), f(inp["ssm_lam_im"][l]), f(inp["ssm_log_dt"][l]), f(inp["ssm_b_re"][l]),
                     f(inp["ssm_b_im"][l]), f(inp["ssm_c_re"][l]), f(inp["ssm_c_im"][l]), f(inp["ssm_d"][l]),
                     f(inp["ssm_w_glu"][l]))
        for k, v in ss.items():
            sh[f"ssm{l}_{k}"] = np.ascontiguousarray(v)
    rb = f(inp["rel_bias"])
    sh["dil_bias"] = lay_dil_bias(rb)
    sh["dsa_bias"], sh["dsa_b31"] = lay_dsa_bias(rb)
    sh["ln_g_l"] = lay_ln(f(inp["ln_g"]))
    sh["ln_b_l"] = lay_ln(f(inp["ln_b"]))
    pp = np.arange(128)[:, None]
    sh["cm_ident"] = np.eye(128, dtype=np.float32)
    sh["cm_sbmask"] = (pp < np.arange(512)[None, :]).astype(np.float32)
    sh["cm_triu"] = (pp > np.arange(128)[None, :]).astype(np.float32)
    sh["cm_negmask"] = np.where(np.arange(128)[None, :] <= pp, 0.0, -1e30).astype(np.float32)
    return sh


def kernel(**inputs):
    x = np.asarray(inputs["x"], np.float32)
    bsz, L, _ = x.shape
    assert (bsz, L) == (16, SEQ)
    shared = host_inputs(inputs)
    b = build_full()
    xs = x.reshape(N_CORES, TTOT_CORE, D)
    in_maps = []
    for c in range(N_CORES):
        m = dict(shared)
        m["x"] = np.ascontiguousarray(xs[c])
        in_maps.append({k: v for k, v in m.items() if k in b.dram_in})
    res = run_bass_kernel_spmd(b.nc, in_maps, core_ids=list(range(N_CORES)))
    out = np.stack([np.asarray(r["y"]) for r in res.results], 0)
    return out.reshape(bsz, L, D).astype(np.float32)
```

```python
import math
from contextlib import ExitStack

import numpy as np
import concourse.bass as bass
import concourse.mybir as mybir
from concourse.bass_utils import run_bass_kernel_spmd

F32 = mybir.dt.float32
BF16 = mybir.dt.bfloat16
AF = mybir.ActivationFunctionType
ALU = mybir.AluOpType
AX = mybir.AxisListType


def _is_psum(key):
    name = key[0] if isinstance(key, tuple) else key
    return isinstance(name, str) and name.startswith("P_")


class Prog:
    NQ = 12

    def __init__(self):
        self.nc = bass.Bass("TRN2", target_bir_lowering=False)
        nc = self.nc
        self.es = ExitStack()
        self.eng = {"pe": nc.tensor, "act": nc.scalar, "dve": nc.vector, "pool": nc.gpsimd, "sp": nc.sync}
        self.sem = {}
        self.cnt = {}
        for e in ("pe", "act", "dve", "pool"):
            self.sem[e] = self.es.enter_context(nc.semaphore("s_" + e))
            self.cnt[e] = 0
        self.dq = {}
        for q in ("sp", "pool", "act"):
            sems = [self.es.enter_context(nc.semaphore(f"d_{q}{i}")) for i in range(self.NQ)]
            for i, s in enumerate(sems):
                self.sem[(q, i)] = s
                self.cnt[(q, i)] = 0
            self.dq[q] = 0
        self.seen = {e: {} for e in self.eng}
        self.last_w = {}
        self.readers = {}
        self.n_ins = 0
        self.n_wait = 0

    def _uniq(self, name):
        self.n_alloc = getattr(self, "n_alloc", 0) + 1
        return f"{name}__{self.n_alloc}"

    def sb(self, name, shape, dtype, stack=None):
        return (stack or self.es).enter_context(self.nc.sbuf_tensor(self._uniq(name), list(shape), dtype))

    def ps(self, name, shape, dtype=F32, stack=None):
        return (stack or self.es).enter_context(self.nc.psum_tensor(self._uniq(name), list(shape), dtype))

    def _wait(self, e, key, val):
        if val <= 0:
            return
        if self.seen[e].get(key, 0) >= val:
            return
        self.eng[e].wait_ge(self.sem[key], val)
        self.seen[e][key] = val
        self.n_wait += 1

    def _deps(self, e, reads, writes, strict_pe=False):
        deps = {}
        for r in reads:
            lw = self.last_w.get(r)
            if lw:
                deps[lw[0]] = max(deps.get(lw[0], 0), lw[1])
            if _is_psum(r):
                for k, v in self.readers.get(r, {}).items():
                    if k != e:
                        deps[k] = max(deps.get(k, 0), v)
        for w in writes:
            lw = self.last_w.get(w)
            if lw:
                deps[lw[0]] = max(deps.get(lw[0], 0), lw[1])
            for k, v in self.readers.get(w, {}).items():
                deps[k] = max(deps.get(k, 0), v)
        for k, v in deps.items():
            if k == "pe" and e == "pe" and not strict_pe:
                continue
            self._wait(e, k, v)

    def _mark(self, key, val, reads, writes):
        for w in writes:
            self.last_w[w] = (key, val)
            self.readers[w] = {}
        for r in reads:
            if r in writes:
                continue
            d = self.readers.setdefault(r, {})
            d[key] = max(d.get(key, 0), val)

    def op(self, e, fn, reads=(), writes=(), strict_pe=False):
        self._deps(e, reads, writes, strict_pe)
        ins = fn()
        self.cnt[e] += 1
        ins.then_inc(self.sem[e], 1)
        self._mark(e, self.cnt[e], reads, writes)
        self.n_ins += 1
        return ins

    def dma(self, q, out, in_, reads=(), writes=(), **kw):
        i = self.dq[q] % self.NQ
        self.dq[q] += 1
        key = (q, i)
        self._wait(q, key, self.cnt[key])
        self._deps(q, reads, writes)
        ins = self.eng[q].dma_start(out=out, in_=in_, **kw)
        self.cnt[key] += 16
        ins.then_inc(self.sem[key], 16)
        self._mark(key, self.cnt[key], reads, writes)
        self.n_ins += 1
        return ins

    def barrier(self):
        for e in self.eng:
            for k, v in self.cnt.items():
                if k == e and e == "pe":
                    continue
                self._wait(e, k, v)
        self.last_w.clear()
        self.readers.clear()

    def finish(self):
        for k, v in self.cnt.items():
            self._wait("sp", k, v)
        self.es.close()


D = 2048
KC = D // 128
DFF = 5632
NJ = DFF // 128
DEPTH = 2
DN_ALPHA = (2.0 * DEPTH) ** 0.25
LN_EPS = 1e-5
TT = 512
NYC = 14
SEQ = 2048
IN_COLS = 14664
GATE0 = 6472


def win_blocks():
    blks = {}
    blks["sb_q"] = list(range(0, 512))
    blks["sb_k"] = list(range(512, 1024))
    blks["sb_v"] = list(range(1024, 1536))
    for g in range(3):
        blks[f"dil_q{g}"] = list(range(1536 + g * 256, 1536 + (g + 1) * 256))
        blks[f"dil_k{g}"] = list(range(2304 + g * 256, 2304 + (g + 1) * 256))
        blks[f"dil_v{g}"] = list(range(3072 + g * 256, 3072 + (g + 1) * 256))
    blks["ssm_u"] = [3840 + min(96 * m + q, 511) for m in range(6) for q in range(128)]
    blks["dsa_q"] = list(range(4352, 4864))
    blks["dsa_k"] = list(range(4864, 5376))
    blks["dsa_v"] = list(range(5376, 5888))
    blks["idx_q"] = list(range(5888, 6400))
    blks["idx_k"] = list(range(6400, 6464)) * 2
    blks["idx_w"] = list(range(6464, 6472))
    for c in range(KC):
        cols = []
        for b in range(4):
            cols += list(range(GATE0 + b * D + c * 128, GATE0 + b * D + (c + 1) * 128))
        blks[f"gate{c}"] = cols
    return blks


WIN_BLOCKS = win_blocks()
WIN_OFF = {}
_o = 0
for _k, _v in WIN_BLOCKS.items():
    WIN_OFF[_k] = (_o, len(_v))
    _o += KC * len(_v)
WIN_TOT = _o


def lay_w_in(w):
    w3 = w.reshape(KC, 128, IN_COLS)
    out = np.empty((128, WIN_TOT), np.float32)
    for k, cols in WIN_BLOCKS.items():
        o, n = WIN_OFF[k]
        out[:, o:o + KC * n] = w3[:, :, cols].transpose(1, 0, 2).reshape(128, KC * n)
    return out


class Builder:
    def __init__(self, ttot):
        self.p = Prog()
        self.nc = self.p.nc
        self.ttot = ttot
        nc = self.nc
        self.dram_in = {}
        self.XTf = nc.dram_tensor("XTf", [KC, 128, ttot], F32, kind="Internal").ap()
        self.XTb = nc.dram_tensor("XTb", [KC, 128, ttot], BF16, kind="Internal").ap()
        self.YT = nc.dram_tensor("YT", [NYC, 128, ttot], BF16, kind="Internal").ap()
        p = self.p
        self.ident_f = p.sb("ident_f", [128, 128], F32)
        self.ones_mean = p.sb("ones_mean", [128, 128], F32)
        self.lng = p.sb("lng", [128, DEPTH * 3 * KC], F32)
        self.lnb = p.sb("lnb", [128, DEPTH * 3 * KC], F32)

    def inp(self, name, shape, dtype=F32):
        t = self.nc.dram_tensor(name, list(shape), dtype, kind="ExternalInput").ap()
        self.dram_in[name] = t
        return t

    def load_consts(self):
        p, nc = self.p, self.nc
        ident = self.inp("cm_ident", [128, 128])
        lng = self.inp("ln_g_l", [128, DEPTH * 3 * KC])
        lnb = self.inp("ln_b_l", [128, DEPTH * 3 * KC])
        p.dma("sp", self.ident_f[:], ident, writes=["ident_f"])
        p.dma("sp", self.lng[:], lng, writes=["lng"])
        p.dma("sp", self.lnb[:], lnb, writes=["lnb"])
        p.op("pool", lambda: nc.gpsimd.memset(self.ones_mean[:], 1.0 / D), writes=["ones_mean"])

    def phase_in(self, x):
        p, nc = self.p, self.nc
        with ExitStack() as st:
            xin = [p.sb(f"pi_x{i}", [128, D], F32, st) for i in range(2)]
            sf = [p.sb(f"pi_sf{i}", [128, KC, 512], F32, st) for i in range(2)]
            sb_ = [p.sb(f"pi_sb{i}", [128, KC, 512], BF16, st) for i in range(2)]
            pt = [p.ps(f"pi_pt{i}", [128, 512], F32, st) for i in range(4)]
            nblk = self.ttot // 128
            n = 0
            for g in range(self.ttot // 512):
                gi = g % 2
                for b4 in range(4):
                    blk = g * 4 + b4
                    xi = blk % 2
                    p.dma("sp", xin[xi][:], x[blk * 128:(blk + 1) * 128, :], writes=[("pi_x", xi)])
                    for c4 in range(4):
                        ps = pt[n % 4]
                        pk = ("P_pi", n % 4)
                        n += 1
                        for cc in range(4):
                            c = c4 * 4 + cc
                            p.op("pe", lambda ps=ps, cc=cc, c=c, xi=xi: nc.tensor.transpose(
                                ps[:, cc * 128:(cc + 1) * 128], xin[xi][:, c * 128:(c + 1) * 128], self.ident_f[:]),
                                reads=[("pi_x", xi), "ident_f"], writes=[pk])
                        src = ps[:].rearrange("p (c t) -> p c t", c=4)
                        p.op("dve", lambda src=src, c4=c4, gi=gi, b4=b4: nc.vector.tensor_copy(
                            sf[gi][:, c4 * 4:(c4 + 1) * 4, b4 * 128:(b4 + 1) * 128], src),
                            reads=[pk], writes=[("pi_sf", gi, b4, c4)])
                        p.op("act", lambda src=src, c4=c4, gi=gi, b4=b4: nc.scalar.copy(
                            sb_[gi][:, c4 * 4:(c4 + 1) * 4, b4 * 128:(b4 + 1) * 128], src),
                            reads=[pk], writes=[("pi_sb", gi, b4, c4)])
                rk = [("pi_sf", gi, b4, c4) for b4 in range(4) for c4 in range(4)]
                rk2 = [("pi_sb", gi, b4, c4) for b4 in range(4) for c4 in range(4)]
                for c in range(KC):
                    p.dma("sp", self.XTf[c, :, g * 512:(g + 1) * 512], sf[gi][:, c, :],
                          reads=rk, writes=[("XTf", g)])
                    p.dma("sp", self.XTb[c, :, g * 512:(g + 1) * 512], sb_[gi][:, c, :],
                          reads=rk2, writes=[("XTb", g)])
            p.barrier()

    def phase_out(self, y):
        p, nc = self.p, self.nc
        with ExitStack() as st:
            sf = [p.sb(f"po_sf{i}", [128, KC, 512], F32, st) for i in range(2)]
            yo = [p.sb(f"po_y{i}", [128, D], F32, st) for i in range(2)]
            pt = [p.ps(f"po_pt{i}", [128, 512], F32, st) for i in range(4)]
            n = 0
            for g in range(self.ttot // 512):
                gi = g % 2
                for c in range(KC):
                    p.dma("sp", sf[gi][:, c, :], self.XTf[c, :, g * 512:(g + 1) * 512],
                          reads=[("XTf", g)], writes=[("po_sf", gi)])
                for b4 in range(4):
                    blk = g * 4 + b4
                    yi = blk % 2
                    for c4 in range(4):
                        ps = pt[n % 4]
                        pk = ("P_po", n % 4)
                        n += 1
                        for cc in range(4):
                            c = c4 * 4 + cc
                            p.op("pe", lambda ps=ps, cc=cc, c=c, gi=gi, b4=b4: nc.tensor.transpose(
                                ps[:, cc * 128:(cc + 1) * 128], sf[gi][:, c, b4 * 128:(b4 + 1) * 128], self.ident_f[:]),
                                reads=[("po_sf", gi), "ident_f"], writes=[pk])
                        eng = "dve" if c4 % 2 == 0 else "act"
                        if eng == "dve":
                            p.op("dve", lambda ps=ps, c4=c4, yi=yi: nc.vector.tensor_copy(
                                yo[yi][:, c4 * 512:(c4 + 1) * 512], ps[:]), reads=[pk], writes=[("po_y", yi, c4)])
                        else:
                            p.op("act", lambda ps=ps, c4=c4, yi=yi: nc.scalar.copy(
                                yo[yi][:, c4 * 512:(c4 + 1) * 512], ps[:]), reads=[pk], writes=[("po_y", yi, c4)])
                    p.dma("sp", y[blk * 128:(blk + 1) * 128, :], yo[yi][:],
                          reads=[("po_y", yi, c4) for c4 in range(4)])
            p.barrier()

    class _Epi:
        pass

    def epi_alloc(self, st, pre):
        p, nc = self.p, self.nc
        e = Builder._Epi()
        e.pre = pre
        e.rs = [p.sb(f"{pre}_rs{i}", [128, TT], F32, st) for i in range(2)]
        e.yT = p.sb(f"{pre}_y", [128, KC, TT], F32, st)
        e.ysq = [p.sb(f"{pre}_ysq{i}", [128, TT], F32, st) for i in range(2)]
        e.var = p.sb(f"{pre}_var", [128, TT], F32, st)
        e.rstd = p.sb(f"{pre}_rstd", [128, TT], F32, st)
        e.t1 = [p.sb(f"{pre}_t1{i}", [128, TT], F32, st) for i in range(2)]
        e.of = [p.sb(f"{pre}_of{i}", [128, TT], F32, st) for i in range(2)]
        e.ob = [p.sb(f"{pre}_ob{i}", [128, TT], BF16, st) for i in range(2)]
        e.epsb = p.sb(f"{pre}_eps", [128, 1], F32, st)
        e.S = [p.ps(f"{pre}_S{i}", [128, TT], F32, st) for i in range(2)]
        e.n = 0
        p.op("dve", lambda: nc.vector.memset(e.epsb[:], LN_EPS / (DN_ALPHA ** 2)), writes=[pre + "_eps"])
        return e

    def epi_load(self, e, c, t):
        p = self.p
        ci = e.n % 2
        tok = slice(t * TT, (t + 1) * TT)
        p.dma("sp", e.rs[ci][:], self.XTf[c, :, tok], reads=[("XTf", t)], writes=[(e.pre + "_rs", ci)])

    def epi_chunk(self, e, c, Yacc, Ykey, scale):
        p, nc = self.p, self.nc
        pre = e.pre
        ci = e.n % 2
        e.n += 1
        S = e.S
        p.op("dve", lambda: nc.vector.scalar_tensor_tensor(
            out=e.yT[:, c, :], in0=Yacc[:], scalar=scale, in1=e.rs[ci][:], op0=ALU.mult, op1=ALU.add),
            reads=[Ykey, (pre + "_rs", ci)], writes=[(pre + "_y", c)])
        p.op("act", lambda: nc.scalar.activation(out=e.ysq[ci][:], in_=e.yT[:, c, :], func=AF.Square),
             reads=[(pre + "_y", c)], writes=[(pre + "_ysq", ci)])
        p.op("pe", lambda: nc.tensor.matmul(S[0][:], lhsT=self.ones_mean[:], rhs=e.yT[:, c, :],
                                            start=(c == 0), stop=(c == KC - 1)),
             reads=["ones_mean", (pre + "_y", c)], writes=[("P_" + pre + "S", 0)])
        p.op("pe", lambda: nc.tensor.matmul(S[1][:], lhsT=self.ones_mean[:], rhs=e.ysq[ci][:],
                                            start=(c == 0), stop=(c == KC - 1)),
             reads=["ones_mean", (pre + "_ysq", ci)], writes=[("P_" + pre + "S", 1)])

    def epi_finalize(self, e, li, t):
        for piece in self.epi_finalize_pieces(e, li, t):
            piece()

    def epi_finalize_pieces(self, e, li, t):
        pieces = [lambda: self._epi_fin_head(e)]
        for c in range(KC):
            pieces.append(lambda c=c: self._epi_fin_chunk(e, li, t, c))
        return pieces

    def _epi_fin_head(self, e):
        p, nc = self.p, self.nc
        pre = e.pre
        S, var, rstd, epsb = e.S, e.var, e.rstd, e.epsb
        p.op("act", lambda: nc.scalar.activation(out=var[:], in_=S[0][:], func=AF.Square),
             reads=[("P_" + pre + "S", 0)], writes=[pre + "_var"])
        p.op("dve", lambda: nc.vector.tensor_tensor(out=var[:], in0=S[1][:], in1=var[:], op=ALU.subtract),
             reads=[("P_" + pre + "S", 1), pre + "_var"], writes=[pre + "_var"])
        p.op("act", lambda: nc.scalar.activation(out=var[:], in_=var[:], func=AF.Sqrt, bias=epsb[:], scale=1.0),
             reads=[pre + "_var", pre + "_eps"], writes=[pre + "_var"])
        p.op("dve", lambda: nc.vector.reciprocal(out=rstd[:], in_=var[:]),
             reads=[pre + "_var"], writes=[pre + "_rstd"])

    def _epi_fin_chunk(self, e, li, t, c):
        p, nc = self.p, self.nc
        pre = e.pre
        S, yT, var, rstd, t1, of, ob, epsb = e.S, e.yT, e.var, e.rstd, e.t1, e.of, e.ob, e.epsb
        tok = slice(t * TT, (t + 1) * TT)
        if True:
            ci = c % 2
            p.op("dve", lambda c=c, ci=ci: nc.vector.tensor_tensor(out=t1[ci][:], in0=yT[:, c, :], in1=S[0][:], op=ALU.subtract),
                 reads=[(pre + "_y", c), ("P_" + pre + "S", 0)], writes=[(pre + "_t1", ci)])
            p.op("dve", lambda ci=ci: nc.vector.tensor_tensor(out=t1[ci][:], in0=t1[ci][:], in1=rstd[:], op=ALU.mult),
                 reads=[(pre + "_t1", ci), pre + "_rstd"], writes=[(pre + "_t1", ci)])
            col = li * KC + c
            p.op("act", lambda c=c, ci=ci, col=col: nc.scalar.activation(
                out=of[ci][:], in_=t1[ci][:], func=AF.Identity, scale=self.lng[:, col:col + 1], bias=self.lnb[:, col:col + 1]),
                reads=[(pre + "_t1", ci), "lng", "lnb"], writes=[(pre + "_of", ci)])
            p.op("act", lambda c=c, ci=ci, col=col: nc.scalar.activation(
                out=ob[ci][:], in_=t1[ci][:], func=AF.Identity, scale=self.lng[:, col:col + 1], bias=self.lnb[:, col:col + 1]),
                reads=[(pre + "_t1", ci), "lng", "lnb"], writes=[(pre + "_ob", ci)])
            p.dma("sp", self.XTf[c, :, tok], of[ci][:], reads=[(pre + "_of", ci)], writes=[("XTf", t)])
            p.dma("sp", self.XTb[c, :, tok], ob[ci][:], reads=[(pre + "_ob", ci)], writes=[("XTb", t)])

    def phase_ffn(self, w_up_l, w_down_l, li):
        p, nc = self.p, self.nc
        with ExitStack() as st:
            xTb = [p.sb(f"ff_x{i}", [128, KC, TT], BF16, st) for i in range(2)]
            gT = p.sb("ff_g", [128, NJ, TT], BF16, st)
            wu = [[p.sb(f"ff_wu{i}{h}", [128, KC, 256], BF16, st) for h in range(2)] for i in range(2)]
            wd = [p.sb(f"ff_wd{i}", [128, NJ, 128], BF16, st) for i in range(2)]
            sa = [p.sb(f"ff_sa{i}", [128, TT], F32, st) for i in range(2)]
            e = self.epi_alloc(st, "ff")
            A = [p.ps(f"ff_A{i}", [128, TT], F32, st) for i in range(2)]
            B = [p.ps(f"ff_B{i}", [128, TT], F32, st) for i in range(2)]
            Y = [p.ps(f"ff_Y{i}", [128, TT], F32, st) for i in range(2)]
            ntile = self.ttot // TT
            nwu = 0
            nwd = 0
            nj = 0
            pending = []

            def load_x(t):
                for c in range(KC):
                    p.dma("sp", xTb[t % 2][:, c, :], self.XTb[c, :, t * TT:(t + 1) * TT],
                          reads=[("XTb", t)], writes=[("ff_x", t % 2)])

            load_x(0)
            for t in range(ntile):
                ti = t % 2
                tok = slice(t * TT, (t + 1) * TT)
                for g in range(NJ // 2):
                    for _ in range(2):
                        if pending:
                            pending.pop(0)()
                    wi = nwu % 2
                    nwu += 1
                    for h in range(2):
                        p.dma("pool", wu[wi][h][:], w_up_l[g, h], writes=[("ff_wu", wi, h)], max_dma_last_dim=8192)
                    for jj in range(2):
                        j = 2 * g + jj
                        ji = nj % 2
                        nj += 1
                        for h, acc, ak in ((0, A[ji], ("P_ffA", ji)), (1, B[ji], ("P_ffB", ji))):
                            for kc in range(KC):
                                p.op("pe", lambda acc=acc, h=h, kc=kc, jj=jj, wi=wi, ti=ti: nc.tensor.matmul(
                                    acc[:], lhsT=wu[wi][h][:, kc, jj * 128:(jj + 1) * 128], rhs=xTb[ti][:, kc, :],
                                    start=(kc == 0), stop=(kc == KC - 1)),
                                    reads=[("ff_wu", wi, h), ("ff_x", ti)], writes=[ak])
                        p.op("act", lambda ji=ji: nc.scalar.activation(out=sa[ji][:], in_=A[ji][:], func=AF.Silu),
                             reads=[("P_ffA", ji)], writes=[("ff_sa", ji)])
                        p.op("dve", lambda ji=ji, j=j: nc.vector.tensor_tensor(
                            out=gT[:, j, :], in0=B[ji][:], in1=sa[ji][:], op=ALU.mult),
                            reads=[("P_ffB", ji), ("ff_sa", ji)], writes=[("ff_g", j)])
                if t + 1 < ntile:
                    load_x(t + 1)
                for c in range(KC):
                    wi = nwd % 2
                    nwd += 1
                    yi = c % 2
                    p.dma("pool", wd[wi][:], w_down_l[c], writes=[("ff_wd", wi)], max_dma_last_dim=8192)
                    self.epi_load(e, c, t)
                    for j in range(NJ):
                        p.op("pe", lambda yi=yi, wi=wi, j=j: nc.tensor.matmul(
                            Y[yi][:], lhsT=wd[wi][:, j, :], rhs=gT[:, j, :], start=(j == 0), stop=(j == NJ - 1)),
                            reads=[("ff_wd", wi), ("ff_g", j)], writes=[("P_ffY", yi)])
                    self.epi_chunk(e, c, Y[yi], ("P_ffY", yi), 0.5 / DN_ALPHA)
                while pending:
                    pending.pop(0)()
                pending = self.epi_finalize_pieces(e, li, t)
            while pending:
                pending.pop(0)()
            p.barrier()

    def phase_merge(self, w_in_l, w_br_l, w_out_l, li):
        p, nc = self.p, self.nc
        YB = [(0, 4), (4, 2), (6, 4), (10, 4)]
        with ExitStack() as st:
            xTbs = [p.sb(f"mg_x{i}", [128, KC, TT], BF16, st) for i in range(2)]
            yTbs = [p.sb(f"mg_yb{i}", [128, NYC, TT], BF16, st) for i in range(2)]
            mT = p.sb("mg_m", [128, KC, TT], BF16, st)
            wg = [p.sb(f"mg_wg{i}", [128, KC, 512], BF16, st) for i in range(2)]
            wb = [p.sb(f"mg_wb{i}", [128, NYC, 128], BF16, st) for i in range(2)]
            wo = [p.sb(f"mg_wo{i}", [128, KC, 128], BF16, st) for i in range(2)]
            sg = [p.sb(f"mg_sg{i}", [128, TT], F32, st) for i in range(2)]
            macc = [p.sb(f"mg_ma{i}", [128, TT], F32, st) for i in range(2)]
            e = self.epi_alloc(st, "mg")
            G = [p.ps(f"mg_G{i}", [128, TT], F32, st) for i in range(2)]
            Pb = [p.ps(f"mg_P{i}", [128, TT], F32, st) for i in range(2)]
            Y = [p.ps(f"mg_Y{i}", [128, TT], F32, st) for i in range(2)]
            ntile = self.ttot // TT
            nw = 0
            nb = 0
            nwo = 0
            pending = []

            def load_xy(t):
                tk = slice(t * TT, (t + 1) * TT)
                for c in range(KC):
                    p.dma("sp", xTbs[t % 2][:, c, :], self.XTb[c, :, tk], reads=[("XTb", t)], writes=[("mg_x", t % 2)])
                for c in range(NYC):
                    p.dma("sp", yTbs[t % 2][:, c, :], self.YT[c, :, tk], reads=[("YT", t)], writes=[("mg_yb", t % 2)])

            load_xy(0)
            for t in range(ntile):
                tok = slice(t * TT, (t + 1) * TT)
                xTb, yTb = xTbs[t % 2], yTbs[t % 2]
                xk, yk = ("mg_x", t % 2), ("mg_yb", t % 2)
                for c in range(KC):
                    for _ in range(2):
                        if pending:
                            pending.pop(0)()
                    wi = nw % 2
                    nw += 1
                    o, n = WIN_OFF[f"gate{c}"]
                    p.dma("pool", wg[wi][:], w_in_l[:, o:o + KC * n].rearrange("p (k c) -> p k c", k=KC),
                          writes=[("mg_wg", wi)], max_dma_last_dim=8192)
                    p.dma("pool", wb[wi][:], w_br_l[c], writes=[("mg_wb", wi)], max_dma_last_dim=8192)
                    mi = c % 2
                    for b in range(4):
                        bi = nb % 2
                        nb += 1
                        for kc in range(KC):
                            p.op("pe", lambda bi=bi, wi=wi, kc=kc, b=b: nc.tensor.matmul(
                                G[bi][:], lhsT=wg[wi][:, kc, b * 128:(b + 1) * 128], rhs=xTb[:, kc, :],
                                start=(kc == 0), stop=(kc == KC - 1)),
                                reads=[("mg_wg", wi), xk], writes=[("P_mgG", bi)])
                        y0, ny = YB[b]
                        for k in range(ny):
                            p.op("pe", lambda bi=bi, wi=wi, k=k, y0=y0, ny=ny: nc.tensor.matmul(
                                Pb[bi][:], lhsT=wb[wi][:, y0 + k, :], rhs=yTb[:, y0 + k, :],
                                start=(k == 0), stop=(k == ny - 1)),
                                reads=[("mg_wb", wi), yk], writes=[("P_mgP", bi)])
                        p.op("act", lambda bi=bi: nc.scalar.activation(out=sg[bi][:], in_=G[bi][:], func=AF.Sigmoid),
                             reads=[("P_mgG", bi)], writes=[("mg_sg", bi)])
                        if b == 0:
                            p.op("dve", lambda bi=bi, mi=mi: nc.vector.tensor_tensor(
                                out=macc[mi][:], in0=Pb[bi][:], in1=sg[bi][:], op=ALU.mult),
                                reads=[("P_mgP", bi), ("mg_sg", bi)], writes=[("mg_ma", mi)])
                        else:
                            p.op("dve", lambda bi=bi: nc.vector.tensor_tensor(
                                out=sg[bi][:], in0=Pb[bi][:], in1=sg[bi][:], op=ALU.mult),
                                reads=[("P_mgP", bi), ("mg_sg", bi)], writes=[("mg_sg", bi)])
                            if b < 3:
                                p.op("dve", lambda bi=bi, mi=mi: nc.vector.tensor_tensor(
                                    out=macc[mi][:], in0=macc[mi][:], in1=sg[bi][:], op=ALU.add),
                                    reads=[("mg_ma", mi), ("mg_sg", bi)], writes=[("mg_ma", mi)])
                            else:
                                p.op("dve", lambda bi=bi, mi=mi, c=c: nc.vector.tensor_tensor(
                                    out=mT[:, c, :], in0=macc[mi][:], in1=sg[bi][:], op=ALU.add),
                                    reads=[("mg_ma", mi), ("mg_sg", bi)], writes=[("mg_m", c)])
                while pending:
                    pending.pop(0)()
                if t + 1 < ntile:
                    load_xy(t + 1)
                for c in range(KC):
                    wi = nwo % 2
                    nwo += 1
                    yi = c % 2
                    p.dma("pool", wo[wi][:], w_out_l[c], writes=[("mg_wo", wi)], max_dma_last_dim=8192)
                    self.epi_load(e, c, t)
                    for k in range(KC):
                        p.op("pe", lambda yi=yi, wi=wi, k=k: nc.tensor.matmul(
                            Y[yi][:], lhsT=wo[wi][:, k, :], rhs=mT[:, k, :], start=(k == 0), stop=(k == KC - 1)),
                            reads=[("mg_wo", wi), ("mg_m", k)], writes=[("P_mgY", yi)])
                    self.epi_chunk(e, c, Y[yi], ("P_mgY", yi), 1.0 / DN_ALPHA)
                pending = self.epi_finalize_pieces(e, li, t)
            while pending:
                pending.pop(0)()
            p.barrier()

    def load_mixer_consts(self):
        p, nc = self.p, self.nc
        self.c_sbmask = p.sb("c_sbmask", [128, 512], F32)
        self.c_triu = p.sb("c_triu", [128, 128], BF16)
        self.c_ones = p.sb("c_ones", [128, 128], BF16)
        self.ident_b = p.sb("ident_b", [128, 128], BF16)
        p.dma("sp", self.c_sbmask[:], self.inp("cm_sbmask", [128, 512]), writes=["c_sbmask"])
        p.dma("pool", self.c_triu[:], self.inp("cm_triu", [128, 128]), writes=["c_triu"])
        p.dma("pool", self.ident_b[:], self.dram_in["cm_ident"], writes=["ident_b"])
        p.op("pool", lambda: nc.gpsimd.memset(self.c_ones[:], 1.0), writes=["c_ones"])
        self.c_zeros = p.sb("c_zeros", [128, 128], BF16)
        p.op("pool", lambda: nc.gpsimd.memset(self.c_zeros[:], 0.0), writes=["c_zeros"])
        self.c_sbneg = p.sb("c_sbneg", [128, 512], BF16)
        p.op("pool", lambda: nc.gpsimd.tensor_scalar(out=self.c_sbneg[:], in0=self.c_sbmask[:], scalar1=-30000.0, scalar2=30000.0,
                                                     op0=ALU.mult, op1=ALU.add), reads=["c_sbmask"], writes=["c_sbneg"])

    def load_xT(self, xT, s, key):
        p = self.p
        for c in range(KC):
            for hh in range(2):
                p.dma("sp", xT[:, c, hh * 1024:(hh + 1) * 1024],
                      self.XTb[c, :, s * SEQ + hh * 1024: s * SEQ + (hh + 1) * 1024],
                      reads=[("XTb", (s * SEQ + hh * 1024) // TT + i) for i in range(2)], writes=[key])

    def load_w(self, w_in_l, name, dst, key):
        o, n = WIN_OFF[name]
        self.p.dma("pool", dst[:, :, 0:n], w_in_l[:, o:o + KC * n].rearrange("p (k c) -> p k c", k=KC),
                   writes=[key], max_dma_last_dim=8192)

    def proj_fm(self, W, wkey, col0, ncols, xT, xkey, t0, nt, acc, acckey):
        p, nc = self.p, self.nc
        for kc in range(KC):
            p.op("pe", lambda kc=kc: nc.tensor.matmul(
                acc[0:ncols, 0:nt], lhsT=W[:, kc, col0:col0 + ncols], rhs=xT[:, kc, t0:t0 + nt],
                start=(kc == 0), stop=(kc == KC - 1)), reads=[wkey, xkey], writes=[acckey])

    def proj_tm(self, W, wkey, col0, ncols, xT, xkey, tok_ap_fn, acc, acckey):
        p, nc = self.p, self.nc
        for kc in range(KC):
            p.op("pe", lambda kc=kc: nc.tensor.matmul(
                acc[:, 0:ncols], lhsT=tok_ap_fn(kc), rhs=W[:, kc, col0:col0 + ncols],
                start=(kc == 0), stop=(kc == KC - 1)), reads=[wkey, xkey], writes=[acckey])

    @staticmethod
    def pipeline(tiles, stages, skew=1, order=None, lags=None):
        n = len(tiles)
        ns = len(stages)
        lags = lags or [k * skew for k in range(ns)]
        for step in range(n + max(lags)):
            for k in (order or range(ns)):
                i = step - lags[k]
                if 0 <= i < n:
                    stages[k](i, tiles[i])

    def phase_sb(self, w_in_l, s):
        p, nc = self.p, self.nc
        with ExitStack() as st:
            qT = p.sb("sb_q", [128, 4, SEQ], BF16, st)
            kT = p.sb("sb_k", [128, 4, SEQ], BF16, st)
            V = p.sb("sb_v", [128, 16, 512], BF16, st)
            with ExitStack() as st2:
                xT = p.sb("sb_x", [128, KC, SEQ], BF16, st2)
                W = [p.sb(f"sb_w{i}", [128, KC, 512], BF16, st2) for i in range(2)]
                acc = [p.ps(f"sb_acc{i}", [128, 512], F32, st2) for i in range(4)]
                self.load_xT(xT, s, "sb_x")
                self.load_w(w_in_l, "sb_q", W[0], ("sb_w", 0))
                self.load_w(w_in_l, "sb_k", W[1], ("sb_w", 1))
                n = 0
                for wi, dst, sc, dk in ((0, qT, 0.125, "sb_q"), (1, kT, 1.0, "sb_k")):
                    for hp in range(4):
                        for tb in range(4):
                            a = acc[n % 4]
                            ak = ("P_sbacc", n % 4)
                            n += 1
                            self.proj_fm(W[wi], ("sb_w", wi), hp * 128, 128, xT, "sb_x", tb * 512, 512, a, ak)
                            if n % 2 == 0:
                                p.op("act", lambda a=a, dst=dst, hp=hp, tb=tb, sc=sc: nc.scalar.activation(
                                    out=dst[:, hp, tb * 512:(tb + 1) * 512], in_=a[:], func=AF.Copy, scale=sc),
                                    reads=[ak], writes=[(dk, hp)])
                            else:
                                p.op("dve", lambda a=a, dst=dst, hp=hp, tb=tb, sc=sc: nc.vector.tensor_scalar(
                                    out=dst[:, hp, tb * 512:(tb + 1) * 512], in0=a[:], scalar1=sc, scalar2=None, op0=ALU.mult),
                                    reads=[ak], writes=[(dk, hp)])
                self.load_w(w_in_l, "sb_v", W[0], ("sb_w", 0))
                for tkb in range(16):
                    a = acc[n % 4]
                    ak = ("P_sbacc", n % 4)
                    n += 1
                    self.proj_tm(W[0], ("sb_w", 0), 0, 512, xT, "sb_x",
                                 lambda kc, tkb=tkb: xT[:, kc, tkb * 128:(tkb + 1) * 128], a, ak)
                    if n % 2 == 0:
                        p.op("act", lambda a=a, tkb=tkb: nc.scalar.copy(out=V[:, tkb, :], in_=a[:]),
                             reads=[ak], writes=[("sb_v", tkb)])
                    else:
                        p.op("dve", lambda a=a, tkb=tkb: nc.vector.tensor_copy(out=V[:, tkb, :], in_=a[:]),
                             reads=[ak], writes=[("sb_v", tkb)])
                p.barrier()
            NB = 8
            E = [p.sb(f"sb_E{i}", [128, 512], F32, st) for i in range(NB)]
            L = [p.sb(f"sb_L{i}", [128, 512], F32, st) for i in range(NB)]
            Lh = [p.sb(f"sb_Lh{i}", [128, 512], BF16, st) for i in range(NB)]
            Wt = [p.sb(f"sb_W{i}", [128, 512], F32, st) for i in range(NB)]
            Ab = [p.sb(f"sb_Ab{i}", [128, 512], BF16, st) for i in range(NB)]
            osb = [p.sb(f"sb_o{i}", [128, 512], BF16, st) for i in range(2)]
            Z = [p.ps(f"sb_Z{i}", [128, 512], F32, st) for i in range(2)]
            SU = [p.ps(f"sb_SU{i}", [128, 512], F32, st) for i in range(2)]
            CS = [p.ps(f"sb_CS{i}", [128, 512], F32, st) for i in range(2)]
            O = [p.ps(f"sb_O{i}", [128, 512], F32, st) for i in range(2)]
            tiles = []
            for h in range(4):
                for qc in range(4):
                    kbs = list(range(4 * qc + 3, -1, -1))
                    for kb in kbs:
                        for slot, hh in ((0, h), (1, h + 4)):
                            tiles.append((hh, qc, kb, slot, kb == kbs[0], kb == 0))

            def geom(tl):
                h, qc, kb, slot, first, last = tl
                d = kb - 4 * qc
                c0 = max(d, 0) * 128
                return h, qc, kb, slot, first, last, d, slice(c0, 512), 512 - c0, h // 2, (h % 2) * 64

            def s0(i, tl):
                h, qc, kb, slot, first, last, d, cs, N, hp, po = geom(tl)
                zb = i % 2
                c0 = cs.start
                p.op("pe", lambda: nc.tensor.matmul(
                    Z[zb][:, cs], lhsT=kT[po:po + 64, hp, kb * 128:(kb + 1) * 128],
                    rhs=qT[po:po + 64, hp, qc * 512 + c0:(qc + 1) * 512], start=True, stop=True),
                    reads=[("sb_q", hp), ("sb_k", hp)], writes=[("P_sbZ", zb)])

            def s1(i, tl):
                h, qc, kb, slot, first, last, d, cs, N, hp, po = geom(tl)
                b, zb = i % NB, i % 2
                p.op("act", lambda: nc.scalar.activation(out=E[b][:, cs], in_=Z[zb][:, cs], func=AF.Exp),
                     reads=[("P_sbZ", zb)], writes=[("sb_E", b)])
                p.op("act", lambda: nc.scalar.activation(out=L[b][:, cs], in_=E[b][:, cs], func=AF.Ln, bias=1.0, scale=1.0),
                     reads=[("sb_E", b)], writes=[("sb_L", b)])
                if d >= 0:
                    p.op("pool", lambda: nc.gpsimd.tensor_tensor(out=Lh[b][:, cs], in0=L[b][:, cs],
                                                                 in1=self.c_sbmask[:, 0:N], op=ALU.mult),
                         reads=[("sb_L", b), "c_sbmask"], writes=[("sb_Lh", b)])
                else:
                    p.op("pool", lambda: nc.gpsimd.tensor_copy(out=Lh[b][:, cs], in_=L[b][:, cs]),
                         reads=[("sb_L", b)], writes=[("sb_Lh", b)])
                p.op("dve", lambda: nc.vector.tensor_tensor(out=Wt[b][:, cs], in0=Z[zb][:, cs], in1=L[b][:, cs], op=ALU.subtract),
                     reads=[("P_sbZ", zb), ("sb_L", b)], writes=[("sb_W", b)])

            def s2(i, tl):
                h, qc, kb, slot, first, last, d, cs, N, hp, po = geom(tl)
                b, zb = i % NB, i % 2
                p.op("pe", lambda: nc.tensor.matmul(SU[zb][:, cs], lhsT=self.c_triu[:], rhs=Lh[b][:, cs], start=True, stop=(d < 0)),
                     reads=["c_triu", ("sb_Lh", b)], writes=[("P_sbSU", zb)])
                if d >= 0:
                    p.op("pe", lambda: nc.tensor.matmul(SU[zb][:, cs], lhsT=self.ident_b[:], rhs=self.c_sbneg[:, 0:N], start=False, stop=True),
                         reads=["ident_b", "c_sbneg"], writes=[("P_sbSU", zb)])
                p.op("dve", lambda: nc.vector.tensor_tensor(out=Wt[b][:, cs], in0=Wt[b][:, cs], in1=SU[zb][:, cs], op=ALU.subtract),
                     reads=[("sb_W", b), ("P_sbSU", zb)], writes=[("sb_W", b)])
                if first:
                    p.op("pe", lambda: nc.tensor.matmul(CS[slot][:, :], lhsT=self.c_zeros[:], rhs=self.c_sbneg[:, :], start=True, stop=False),
                         reads=["c_zeros", "c_sbneg"], writes=[("P_sbCS", slot)])
                else:
                    p.op("dve", lambda: nc.vector.tensor_tensor(out=Wt[b][:, cs], in0=Wt[b][:, cs], in1=CS[slot][:, cs], op=ALU.subtract),
                         reads=[("sb_W", b), ("P_sbCS", slot)], writes=[("sb_W", b)])

            def s3(i, tl):
                h, qc, kb, slot, first, last, d, cs, N, hp, po = geom(tl)
                b = i % NB
                if not last:
                    p.op("pe", lambda: nc.tensor.matmul(CS[slot][:, cs], lhsT=self.c_ones[:], rhs=Lh[b][:, cs], start=False, stop=False),
                         reads=["c_ones", ("sb_Lh", b)], writes=[("P_sbCS", slot)])
                p.op("act", lambda: nc.scalar.activation(out=Ab[b][:, cs], in_=Wt[b][:, cs], func=AF.Exp),
                     reads=[("sb_W", b)], writes=[("sb_Ab", b)])
                p.op("pe", lambda: nc.tensor.matmul(O[slot][0:64, cs], lhsT=V[:, kb, h * 64:(h + 1) * 64], rhs=Ab[b][:, cs],
                                                    start=first, stop=last),
                     reads=[("sb_v", kb), ("sb_Ab", b)], writes=[("P_sbO", slot)])
                if last:
                    p.op("act", lambda: nc.scalar.copy(out=osb[slot][0:64, :], in_=O[slot][0:64, :]),
                         reads=[("P_sbO", slot)], writes=[("sb_o", slot)])
                    tok0 = s * SEQ + qc * 512
                    p.dma("sp", self.YT[hp, po:po + 64, tok0:tok0 + 512], osb[slot][0:64, :],
                          reads=[("sb_o", slot)], writes=[("YT", tok0 // TT)])

            self.pipeline(tiles, [s0, s1, s2, s3], order=(0, 3, 1, 2), lags=(0, 1, 3, 5))
            p.barrier()

    def phase_dil(self, w_in_l, dil_bias, s):
        p, nc = self.p, self.nc
        DILS = (1, 4, 16)
        with ExitStack() as st:
            qT = p.sb("dl_q", [128, 6, SEQ], BF16, st)
            kT = p.sb("dl_k", [128, 6, SEQ], BF16, st)
            V = p.sb("dl_v", [128, 48, 256], BF16, st)
            EB = p.sb("dl_EB", [128, 24 * 128], F32, st)
            p.dma("sp", EB[:], dil_bias, writes=["dl_EB"])
            p.op("act", lambda: nc.scalar.activation(out=EB[:], in_=EB[:], func=AF.Exp), reads=["dl_EB"], writes=["dl_EB"])
            with ExitStack() as st2:
                xT = p.sb("dl_x", [128, KC, SEQ], BF16, st2)
                W = [p.sb(f"dl_w{i}", [128, KC, 256], BF16, st2) for i in range(3)]
                acc = [p.ps(f"dl_acc{i}", [128, 512], F32, st2) for i in range(4)]
                self.load_xT(xT, s, "dl_x")
                n = 0
                for g, r in enumerate(DILS):
                    self.load_w(w_in_l, f"dil_q{g}", W[0], ("dl_w", 0))
                    self.load_w(w_in_l, f"dil_k{g}", W[1], ("dl_w", 1))
                    self.load_w(w_in_l, f"dil_v{g}", W[2], ("dl_w", 2))
                    for wi, dst, sc, dk in ((0, qT, 0.125, "dl_q"), (1, kT, 1.0, "dl_k")):
                        for hp in range(2):
                            for tb in range(4):
                                a = acc[n % 4]
                                ak = ("P_dlacc", n % 4)
                                n += 1
                                self.proj_fm(W[wi], ("dl_w", wi), hp * 128, 128, xT, "dl_x", tb * 512, 512, a, ak)
                                if n % 2 == 0:
                                    p.op("act", lambda a=a, dst=dst, hp=hp, tb=tb, sc=sc, g=g: nc.scalar.activation(
                                        out=dst[:, g * 2 + hp, tb * 512:(tb + 1) * 512], in_=a[:], func=AF.Copy, scale=sc),
                                        reads=[ak], writes=[(dk, g * 2 + hp)])
                                else:
                                    p.op("dve", lambda a=a, dst=dst, hp=hp, tb=tb, sc=sc, g=g: nc.vector.tensor_scalar(
                                        out=dst[:, g * 2 + hp, tb * 512:(tb + 1) * 512], in0=a[:], scalar1=sc, scalar2=None, op0=ALU.mult),
                                        reads=[ak], writes=[(dk, g * 2 + hp)])
                    nloc = 16 // r
                    for c in range(r):
                        for bl in range(nloc):
                            pi = c * nloc + bl
                            a = acc[n % 4]
                            ak = ("P_dlacc", n % 4)
                            n += 1
                            t0 = c + r * 128 * bl
                            self.proj_tm(W[2], ("dl_w", 2), 0, 256, xT, "dl_x",
                                         lambda kc, t0=t0, r=r: xT[:, kc, t0:t0 + 127 * r + 1:r], a, ak)
                            if n % 2 == 0:
                                p.op("act", lambda a=a, g=g, pi=pi: nc.scalar.copy(out=V[:, g * 16 + pi, :], in_=a[:, 0:256]),
                                     reads=[ak], writes=[("dl_v", g * 16 + pi)])
                            else:
                                p.op("dve", lambda a=a, g=g, pi=pi: nc.vector.tensor_copy(out=V[:, g * 16 + pi, :], in_=a[:, 0:256]),
                                     reads=[ak], writes=[("dl_v", g * 16 + pi)])
                p.barrier()
            Pe = [p.sb(f"dl_Pe{i}", [128, 256], F32, st) for i in range(2)]
            Pb = [p.sb(f"dl_Pb{i}", [128, 256], BF16, st) for i in range(2)]
            accU = p.sb("dl_aU", [64, SEQ], F32, st)
            accZ = p.sb("dl_aZ", [64, SEQ], F32, st)
            yb = p.sb("dl_yb", [64, SEQ], BF16, st)
            Sp = [p.ps(f"dl_S{i}", [128, 512], F32, st) for i in range(2)]
            Up = [p.ps(f"dl_U{i}", [128, 512], F32, st) for i in range(2)]
            Zp = [p.ps(f"dl_Z{i}", [128, 512], F32, st) for i in range(2)]
            for hh in range(4):
                po = (hh % 2) * 64
                tiles = []
                for g, r in enumerate(DILS):
                    nloc = 16 // r
                    for c in range(r):
                        for bl in range(nloc):
                            tiles.append((g, r, c, bl, nloc))

                def s1(i, tl):
                    g, r, c, bl, nloc = tl
                    b = i % 2
                    ch = g * 2 + hh // 2
                    tq = c + r * 128 * bl
                    nk = 2 if bl > 0 else 1
                    for kbi in range(nk):
                        tk = c + r * 128 * (bl - kbi)
                        p.op("pe", lambda kbi=kbi, tk=tk: nc.tensor.matmul(
                            Sp[b][:, kbi * 128:(kbi + 1) * 128], lhsT=kT[po:po + 64, ch, tk:tk + 127 * r + 1:r],
                            rhs=qT[po:po + 64, ch, tq:tq + 127 * r + 1:r], start=True, stop=True),
                            reads=[("dl_q", ch), ("dl_k", ch)], writes=[("P_dlS", b)])
                    p.op("act", lambda: nc.scalar.activation(out=Pe[b][:, 0:nk * 128], in_=Sp[b][:, 0:nk * 128], func=AF.Exp),
                         reads=[("P_dlS", b)], writes=[("dl_Pe", b)])
                    e0 = ((g * 4 + hh) * 2) * 128
                    p.op("dve", lambda: nc.vector.tensor_tensor(out=Pb[b][:, 0:nk * 128], in0=Pe[b][:, 0:nk * 128],
                                                                in1=EB[:, e0:e0 + nk * 128], op=ALU.mult),
                         reads=[("dl_Pe", b), "dl_EB"], writes=[("dl_Pb", b)])

                def s2(i, tl):
                    g, r, c, bl, nloc = tl
                    b = i % 2
                    tq = c + r * 128 * bl
                    nk = 2 if bl > 0 else 1
                    for kbi in range(nk):
                        vi = g * 16 + c * nloc + (bl - kbi)
                        p.op("pe", lambda kbi=kbi, vi=vi: nc.tensor.matmul(
                            Up[b][0:64, 0:128], lhsT=V[:, vi, hh * 64:(hh + 1) * 64], rhs=Pb[b][:, kbi * 128:(kbi + 1) * 128],
                            start=(kbi == 0), stop=(kbi == nk - 1)),
                            reads=[("dl_v", vi), ("dl_Pb", b)], writes=[("P_dlU", b)])
                    for kbi in range(nk):
                        p.op("pe", lambda kbi=kbi: nc.tensor.matmul(
                            Zp[b][0:64, 0:128], lhsT=self.c_ones[:, 0:64], rhs=Pb[b][:, kbi * 128:(kbi + 1) * 128],
                            start=(kbi == 0), stop=(kbi == nk - 1)),
                            reads=["c_ones", ("dl_Pb", b)], writes=[("P_dlZ", b)])
                    tsl = slice(tq, tq + 127 * r + 1, r)
                    if g == 0:
                        p.op("dve", lambda: nc.vector.tensor_copy(out=accU[:, tsl], in_=Up[b][0:64, 0:128]),
                             reads=[("P_dlU", b)], writes=["dl_aU"])
                        p.op("act", lambda: nc.scalar.copy(out=accZ[:, tsl], in_=Zp[b][0:64, 0:128]),
                             reads=[("P_dlZ", b)], writes=["dl_aZ"])
                    else:
                        p.op("dve", lambda: nc.vector.tensor_tensor(out=accU[:, tsl], in0=Up[b][0:64, 0:128], in1=accU[:, tsl], op=ALU.add),
                             reads=[("P_dlU", b), "dl_aU"], writes=["dl_aU"])
                        p.op("dve", lambda: nc.vector.tensor_tensor(out=accZ[:, tsl], in0=Zp[b][0:64, 0:128], in1=accZ[:, tsl], op=ALU.add),
                             reads=[("P_dlZ", b), "dl_aZ"], writes=["dl_aZ"])

                self.pipeline(tiles, [s1, s2])
                p.op("dve", lambda: nc.vector.reciprocal(out=accZ[:], in_=accZ[:]), reads=["dl_aZ"], writes=["dl_aZ"])
                p.op("dve", lambda: nc.vector.tensor_tensor(out=yb[:], in0=accU[:], in1=accZ[:], op=ALU.mult),
                     reads=["dl_aU", "dl_aZ"], writes=["dl_yb"])
                p.dma("sp", self.YT[4 + hh // 2, po:po + 64, s * SEQ:(s + 1) * SEQ], yb[:],
                      reads=["dl_yb"], writes=[("YT", (s * SEQ) // TT + i) for i in range(SEQ // TT)])
            p.barrier()

    def phase_dsa(self, w_in_l, dsa_bias, dsa_b31, c_negmask, s):
        p, nc = self.p, self.nc
        with ExitStack() as st:
            qT = p.sb("ds_q", [128, 4, SEQ], BF16, st)
            kT = p.sb("ds_k", [128, 4, SEQ], BF16, st)
            V = p.sb("ds_v", [128, 16, 512], BF16, st)
            qiT = p.sb("ds_qi", [128, 4, SEQ], BF16, st)
            kiT = p.sb("ds_ki", [128, SEQ], BF16, st)
            wabs = p.sb("ds_wabs", [128, 16, 8], F32, st)
            wsgn = p.sb("ds_wsgn", [128, 16, 8], F32, st)
            EB = p.sb("ds_EB", [128, 16 * 128], F32, st)
            b31 = p.sb("ds_b31", [128, 8], F32, st)
            nb31 = p.sb("ds_nb31", [128, 8], F32, st)
            negm = p.sb("ds_negm", [128, 128], F32, st)
            p.dma("sp", EB[:], dsa_bias, writes=["ds_EB"])
            p.dma("sp", b31[:], dsa_b31, writes=["ds_b31"])
            p.dma("sp", negm[:], c_negmask, writes=["ds_negm"])
            p.op("dve", lambda: nc.vector.tensor_scalar(out=nb31[:], in0=b31[:], scalar1=-1.0, scalar2=None, op0=ALU.mult),
                 reads=["ds_b31"], writes=["ds_nb31"])
            EBh = p.sb("ds_EBh", [128, 16 * 128], BF16, st)
            EBl = p.sb("ds_EBl", [128, 16 * 128], BF16, st)
            for h in range(8):
                hs = slice(h * 256, (h + 1) * 256)
                p.op("dve", lambda h=h, hs=hs: nc.vector.tensor_scalar(out=EB[:, hs], in0=EB[:, hs], scalar1=nb31[:, h:h + 1], scalar2=None, op0=ALU.add),
                     reads=["ds_EB", "ds_nb31"], writes=["ds_EB"])
            p.op("dve", lambda: nc.vector.tensor_copy(out=EBh[:], in_=EB[:]), reads=["ds_EB"], writes=["ds_EBh"])
            p.op("dve", lambda: nc.vector.tensor_tensor(out=EBl[:], in0=EB[:], in1=EBh[:], op=ALU.subtract),
                 reads=["ds_EB", "ds_EBh"], writes=["ds_EBl"])
            with ExitStack() as st2:
                xT = p.sb("ds_x", [128, KC, SEQ], BF16, st2)
                W = [p.sb(f"ds_w{i}", [128, KC, 512], BF16, st2) for i in range(2)]
                acc = [p.ps(f"ds_acc{i}", [128, 512], F32, st2) for i in range(4)]
                self.load_xT(xT, s, "ds_x")
                n = 0
                plan = (("dsa_q", qT, 0.125, "ds_q", 4), ("dsa_k", kT, 1.0, "ds_k", 4), ("idx_q", qiT, 1.0, "ds_qi", 4),
                        ("idx_k", kiT, 1.0, "ds_ki", 1))
                for pi, (wname, dst, sc, dk, nhp) in enumerate(plan):
                    wi = pi % 2
                    self.load_w(w_in_l, wname, W[wi], ("ds_w", wi))
                    for hp in range(nhp):
                        for tb in range(4):
                            a = acc[n % 4]
                            ak = ("P_dsacc", n % 4)
                            n += 1
                            self.proj_fm(W[wi], ("ds_w", wi), hp * 128, 128, xT, "ds_x", tb * 512, 512, a, ak)
                            o = dst[:, hp, tb * 512:(tb + 1) * 512] if nhp > 1 else dst[:, tb * 512:(tb + 1) * 512]
                            if n % 2 == 0:
                                p.op("act", lambda a=a, o=o, sc=sc: nc.scalar.activation(out=o, in_=a[:], func=AF.Copy, scale=sc),
                                     reads=[ak], writes=[(dk, hp)])
                            else:
                                p.op("dve", lambda a=a, o=o, sc=sc: nc.vector.tensor_scalar(
                                    out=o, in0=a[:], scalar1=sc, scalar2=None, op0=ALU.mult), reads=[ak], writes=[(dk, hp)])
                self.load_w(w_in_l, "dsa_v", W[0], ("ds_w", 0))
                self.load_w(w_in_l, "idx_w", W[1], ("ds_w", 1))
                for tkb in range(16):
                    a = acc[n % 4]
                    ak = ("P_dsacc", n % 4)
                    n += 1
                    self.proj_tm(W[0], ("ds_w", 0), 0, 512, xT, "ds_x",
                                 lambda kc, tkb=tkb: xT[:, kc, tkb * 128:(tkb + 1) * 128], a, ak)
                    p.op("act", lambda a=a, tkb=tkb: nc.scalar.copy(out=V[:, tkb, :], in_=a[:]),
                         reads=[ak], writes=[("ds_v", tkb)])
                    a = acc[n % 4]
                    ak = ("P_dsacc", n % 4)
                    n += 1
                    self.proj_tm(W[1], ("ds_w", 1), 0, 8, xT, "ds_x",
                                 lambda kc, tkb=tkb: xT[:, kc, tkb * 128:(tkb + 1) * 128], a, ak)
                    p.op("act", lambda a=a, tkb=tkb: nc.scalar.activation(out=wabs[:, tkb, :], in_=a[:, 0:8], func=AF.Abs),
                         reads=[ak], writes=["ds_wabs"])
                    p.op("act", lambda a=a, tkb=tkb: nc.scalar.activation(out=wsgn[:, tkb, :], in_=a[:, 0:8], func=AF.Sign),
                         reads=[ak], writes=["ds_wsgn"])
                p.barrier()
            I = [p.sb(f"ds_I{i}", [128, SEQ], F32, st) for i in range(2)]
            NIT = 24
            m8 = p.sb("ds_m8", [128, 8], F32, st)
            rmin = [p.sb(f"ds_rmin{i}", [128, 1], F32, st) for i in range(2)]
            blo = p.sb("ds_blo", [128, 1], F32, st)
            bmid = p.sb("ds_bmid", [128, 1], F32, st)
            bcnt = p.sb("ds_bcnt", [128, 1], F32, st)
            binc = p.sb("ds_binc", [128, 1], F32, st)
            bw = p.sb("ds_bw", [128, NIT], F32, st)
            pw2 = p.sb("ds_pw2", [128, NIT], F32, st)
            for k in range(NIT):
                p.op("pool", lambda k=k: nc.gpsimd.memset(pw2[:, k:k + 1], 0.5 ** (k + 1)), writes=["ds_pw2"])
            Mq = [p.sb(f"ds_Mq{i}", [128, SEQ], BF16, st) for i in range(2)]
            rl = [p.sb(f"ds_rl{i}", [128, 512], F32, st) for i in range(2)]
            Pb = [p.sb(f"ds_Pb{i}", [128, 512], BF16, st) for i in range(2)]
            rz = p.sb("ds_rz", [64, 128], F32, st)
            yfull = p.sb("ds_y", [64, 8, SEQ], BF16, st)
            DpI = [p.ps(f"ds_DI{i}", [128, 512], F32, st) for i in range(2)]
            DpA = [p.ps(f"ds_DA{i}", [128, 512], F32, st) for i in range(2)]
            Up = [p.ps(f"ds_U{i}", [128, 512], F32, st) for i in range(2)]
            Zp = [p.ps(f"ds_Z{i}", [128, 512], F32, st) for i in range(2)]
            cnt = {"d": 0, "t": 0, "a": 0, "u": 0}

            def idx_thunks(i):
                nk = 128 * (i + 1)
                qs = slice(i * 128, (i + 1) * 128)
                ib = i % 2
                Ii = I[ib]
                ik = ("ds_I", ib)
                th = []
                for h in range(8):
                    hp, po = h // 2, (h % 2) * 64
                    for k0 in range(0, nk, 512):
                        kn = min(512, nk - k0)

                        def f(h=h, hp=hp, po=po, k0=k0, kn=kn):
                            b = cnt["d"] % 2
                            cnt["d"] += 1
                            p.op("pe", lambda: nc.tensor.matmul(
                                DpI[b][:, 0:kn], lhsT=qiT[po:po + 64, hp, qs], rhs=kiT[po:po + 64, k0:k0 + kn], start=True, stop=True),
                                reads=[("ds_qi", hp), ("ds_ki", 0)], writes=[("P_dsDI", b)])
                            p.op("act", lambda: nc.scalar.activation(
                                out=rl[b][:, 0:kn], in_=DpI[b][:, 0:kn], func=AF.Relu, scale=wabs[:, i, h:h + 1]),
                                reads=[("P_dsDI", b), "ds_wabs"], writes=[("ds_rl", b)])
                            if h == 0:
                                p.op("dve", lambda: nc.vector.tensor_scalar(
                                    out=Ii[:, k0:k0 + kn], in0=rl[b][:, 0:kn], scalar1=wsgn[:, i, h:h + 1], scalar2=None, op0=ALU.mult),
                                    reads=[("ds_rl", b), "ds_wsgn"], writes=[ik])
                            else:
                                p.op("dve", lambda: nc.vector.scalar_tensor_tensor(
                                    out=Ii[:, k0:k0 + kn], in0=rl[b][:, 0:kn], scalar=wsgn[:, i, h:h + 1], in1=Ii[:, k0:k0 + kn],
                                    op0=ALU.mult, op1=ALU.add),
                                    reads=[("ds_rl", b), "ds_wsgn", ik], writes=[ik])
                        th.append(f)
                if i >= 2:
                    th.append(lambda: p.op("dve", lambda: nc.vector.tensor_reduce(
                        out=rmin[ib][:], in_=Ii[:, 0:nk], op=ALU.min, axis=AX.X), reads=[ik], writes=[("ds_rmin", ib)]))
                th.append(lambda: p.op("pool", lambda: nc.gpsimd.tensor_tensor(
                    out=Ii[:, i * 128:nk], in0=Ii[:, i * 128:nk], in1=negm[:], op=ALU.add),
                    reads=[ik, "ds_negm"], writes=[ik]))
                return th

            def topk_thunks(i):
                nk = 128 * (i + 1)
                ib = i % 2
                Ii, Mi = I[ib], Mq[ib]
                ik, mk = ("ds_I", ib), ("ds_Mq", ib)
                bk = "ds_bis"
                th = []
                if i >= 2:
                    def init():
                        p.op("dve", lambda: nc.vector.max(out=m8[:], in_=Ii[:, 0:nk]), reads=[ik], writes=["ds_m8"])
                        p.op("dve", lambda: nc.vector.tensor_copy(out=blo[:], in_=rmin[ib][:]), reads=[("ds_rmin", ib), bk], writes=[bk])
                        p.op("dve", lambda: nc.vector.tensor_tensor(out=binc[:], in0=m8[:, 0:1], in1=rmin[ib][:], op=ALU.subtract),
                             reads=["ds_m8", ("ds_rmin", ib), bk], writes=[bk])
                        p.op("dve", lambda: nc.vector.tensor_scalar(out=bw[:], in0=pw2[:], scalar1=binc[:, 0:1], scalar2=None, op0=ALU.mult),
                             reads=["ds_pw2", bk], writes=[bk])
                    th.append(init)
                    for k in range(NIT):
                        def f(k=k):
                            p.op("dve", lambda: nc.vector.tensor_tensor(out=bmid[:], in0=blo[:], in1=bw[:, k:k + 1], op=ALU.add),
                                 reads=[bk], writes=[bk])
                            p.op("dve", lambda: nc.vector.tensor_scalar(out=Mi[:, 0:nk], in0=Ii[:, 0:nk], scalar1=bmid[:, 0:1], scalar2=0.0,
                                                                        op0=ALU.is_ge, op1=ALU.add, accum_out=bcnt[:]),
                                 reads=[ik, bk], writes=[mk, bk])
                            p.op("dve", lambda: nc.vector.tensor_scalar(out=binc[:], in0=bcnt[:], scalar1=255.5, scalar2=bw[:, k:k + 1],
                                                                        op0=ALU.is_ge, op1=ALU.mult),
                                 reads=[bk], writes=[bk])
                            p.op("dve", lambda: nc.vector.tensor_tensor(out=blo[:], in0=blo[:], in1=binc[:], op=ALU.add),
                                 reads=[bk], writes=[bk])
                        th.append(f)
                    th.append(lambda: p.op("dve", lambda: nc.vector.tensor_scalar(
                        out=Mi[:, 0:nk], in0=Ii[:, 0:nk], scalar1=blo[:, 0:1], scalar2=-30000.0, op0=ALU.is_lt, op1=ALU.mult),
                        reads=[ik, bk], writes=[mk]))
                return th

            def att_thunks(i):
                qs = slice(i * 128, (i + 1) * 128)
                ib = i % 2
                Mi = Mq[ib]
                mk = ("ds_Mq", ib)
                th = []
                for h in range(8):
                    hp, po = h // 2, (h % 2) * 64
                    for k4 in range(0, i + 1, 4):
                        kn = min(4, i + 1 - k4)

                        def f(h=h, hp=hp, po=po, k4=k4, kn=kn):
                            b = cnt["a"] % 2
                            cnt["a"] += 1
                            if k4 == 0:
                                cnt["u"] += 1
                            ub = cnt["u"] % 2
                            for j in range(kn):
                                kb = k4 + j
                                kind = i - kb
                                osl = DpA[b][:, j * 128:(j + 1) * 128]
                                extra = (1 if i >= 2 else 0) + (2 if kind <= 1 else 0)
                                p.op("pe", lambda kb=kb, osl=osl, extra=extra: nc.tensor.matmul(
                                    osl, lhsT=kT[po:po + 64, hp, kb * 128:(kb + 1) * 128],
                                    rhs=qT[po:po + 64, hp, qs], start=True, stop=(extra == 0)),
                                    reads=[("ds_q", hp), ("ds_k", hp)], writes=[("P_dsDA", b)])
                                if i >= 2:
                                    extra -= 1
                                    p.op("pe", lambda kb=kb, osl=osl, extra=extra: nc.tensor.matmul(
                                        osl, lhsT=Mi[:, kb * 128:(kb + 1) * 128], rhs=self.ident_b[:], start=False, stop=(extra == 0)),
                                        reads=[mk, "ident_b"], writes=[("P_dsDA", b)], strict_pe=(po == 0))
                                if kind <= 1:
                                    e0 = (h * 2 + kind) * 128
                                    for EBx, nm in ((EBh, "ds_EBh"), (EBl, "ds_EBl")):
                                        extra -= 1
                                        p.op("pe", lambda osl=osl, EBx=EBx, e0=e0, extra=extra: nc.tensor.matmul(
                                            osl, lhsT=self.ident_b[:], rhs=EBx[:, e0:e0 + 128], start=False, stop=(extra == 0)),
                                            reads=[nm, "ident_b"], writes=[("P_dsDA", b)], strict_pe=(po == 0 and i < 2 and EBx is EBh))
                            p.op("act", lambda: nc.scalar.activation(
                                out=Pb[b][:, 0:kn * 128], in_=DpA[b][:, 0:kn * 128], func=AF.Exp, bias=b31[:, h:h + 1], scale=1.0),
                                reads=[("P_dsDA", b), "ds_b31"], writes=[("ds_Pb", b)])
                            for j in range(kn):
                                kb = k4 + j
                                p.op("pe", lambda j=j, kb=kb: nc.tensor.matmul(
                                    Up[ub][0:64, 0:128], lhsT=V[:, kb, h * 64:(h + 1) * 64], rhs=Pb[b][:, j * 128:(j + 1) * 128],
                                    start=(kb == 0), stop=(kb == i)),
                                    reads=[("ds_v", kb), ("ds_Pb", b)], writes=[("P_dsU", ub)])
                            for j in range(kn):
                                kb = k4 + j
                                p.op("pe", lambda j=j, kb=kb: nc.tensor.matmul(
                                    Zp[ub][0:64, 0:128], lhsT=self.c_ones[:, 0:64], rhs=Pb[b][:, j * 128:(j + 1) * 128],
                                    start=(kb == 0), stop=(kb == i)),
                                    reads=["c_ones", ("ds_Pb", b)], writes=[("P_dsZ", ub)])
                            if k4 + kn == i + 1:
                                p.op("dve", lambda: nc.vector.reciprocal(out=rz[:], in_=Zp[ub][0:64, 0:128]),
                                     reads=[("P_dsZ", ub)], writes=["ds_rz"])
                                p.op("dve", lambda: nc.vector.tensor_tensor(out=yfull[:, h, qs], in0=Up[ub][0:64, 0:128], in1=rz[:], op=ALU.mult),
                                     reads=[("P_dsU", ub), "ds_rz"], writes=[("ds_y", h)])
                        th.append(f)
                return th

            def interleave(lists):
                lists = [l for l in lists if l]
                pos = [0] * len(lists)
                total = sum(len(l) for l in lists)
                for _ in range(total):
                    j = min((jj for jj in range(len(lists)) if pos[jj] < len(lists[jj])),
                            key=lambda jj: (pos[jj] + 0.5) / len(lists[jj]))
                    lists[j][pos[j]]()
                    pos[j] += 1

            for step in range(16 + 2):
                ls = []
                if step - 1 >= 0 and step - 1 < 16:
                    ls.append(topk_thunks(step - 1))
                if step - 2 >= 0:
                    ls.append(att_thunks(step - 2))
                if step < 16:
                    ls.append(idx_thunks(step))
                interleave(ls)
            for h in range(8):
                hp, po = h // 2, (h % 2) * 64
                p.dma("sp", self.YT[10 + hp, po:po + 64, s * SEQ:(s + 1) * SEQ], yfull[:, h, :],
                      reads=[("ds_y", h)], writes=[("YT", (s * SEQ) // TT + j) for j in range(SEQ // TT)])
            p.barrier()

    def ssm_discretize(self, st, pre, lr, li, ldt, shape):
        p, nc = self.p, self.nc
        P_, F_ = shape
        TWO_PI = 2.0 * math.pi
        cnt = [0]

        def T(dtype=F32):
            cnt[0] += 1
            return p.sb(f"{pre}_t{cnt[0]}", [P_, F_], dtype, st)

        key = pre + "_k"

        def dve(fn):
            p.op("dve", fn, reads=[key], writes=[key])

        def act(fn):
            p.op("act", fn, reads=[key], writes=[key])

        dt = T(); mag = T(); ang = T(); tmp = T(); ki = T(mybir.dt.int32); kf = T(); r = T(); msk = T()
        sn = T(); cs = T(); a_re = T(); a_im = T(); den = T(); f_re = T(); f_im = T(); am1 = T()
        act(lambda: nc.scalar.activation(out=dt[:], in_=ldt[:], func=AF.Exp))
        dve(lambda: nc.vector.tensor_tensor(out=tmp[:], in0=lr[:], in1=dt[:], op=ALU.mult))
        act(lambda: nc.scalar.activation(out=mag[:], in_=tmp[:], func=AF.Exp))
        dve(lambda: nc.vector.tensor_tensor(out=ang[:], in0=li[:], in1=dt[:], op=ALU.mult))
        for off, dst in ((0.0, sn), (0.5 * math.pi, cs)):
            dve(lambda off=off: nc.vector.tensor_scalar(out=tmp[:], in0=ang[:], scalar1=off, scalar2=1.0 / TWO_PI,
                                                        op0=ALU.add, op1=ALU.mult))
            dve(lambda: nc.vector.tensor_copy(out=ki[:], in_=tmp[:]))
            dve(lambda: nc.vector.tensor_copy(out=kf[:], in_=ki[:]))
            dve(lambda off=off: nc.vector.tensor_scalar(out=r[:], in0=ang[:], scalar1=off, scalar2=None, op0=ALU.add))
            dve(lambda: nc.vector.scalar_tensor_tensor(out=r[:], in0=kf[:], scalar=-TWO_PI, in1=r[:], op0=ALU.mult, op1=ALU.add))
            dve(lambda: nc.vector.tensor_scalar(out=msk[:], in0=r[:], scalar1=math.pi, scalar2=None, op0=ALU.is_gt))
            dve(lambda: nc.vector.scalar_tensor_tensor(out=r[:], in0=msk[:], scalar=-TWO_PI, in1=r[:], op0=ALU.mult, op1=ALU.add))
            dve(lambda: nc.vector.tensor_scalar(out=msk[:], in0=r[:], scalar1=-math.pi, scalar2=None, op0=ALU.is_lt))
            dve(lambda: nc.vector.scalar_tensor_tensor(out=r[:], in0=msk[:], scalar=TWO_PI, in1=r[:], op0=ALU.mult, op1=ALU.add))
            dve(lambda: nc.vector.tensor_scalar(out=r[:], in0=r[:], scalar1=math.pi, scalar2=-math.pi, op0=ALU.min, op1=ALU.max))
            act(lambda dst=dst: nc.scalar.activation(out=dst[:], in_=r[:], func=AF.Sin))
        dve(lambda: nc.vector.tensor_tensor(out=a_re[:], in0=mag[:], in1=cs[:], op=ALU.mult))
        dve(lambda: nc.vector.tensor_tensor(out=a_im[:], in0=mag[:], in1=sn[:], op=ALU.mult))
        dve(lambda: nc.vector.tensor_tensor(out=den[:], in0=lr[:], in1=lr[:], op=ALU.mult))
        dve(lambda: nc.vector.tensor_tensor(out=tmp[:], in0=li[:], in1=li[:], op=ALU.mult))
        dve(lambda: nc.vector.tensor_tensor(out=den[:], in0=den[:], in1=tmp[:], op=ALU.add))
        dve(lambda: nc.vector.reciprocal(out=den[:], in_=den[:]))
        dve(lambda: nc.vector.tensor_scalar(out=am1[:], in0=a_re[:], scalar1=-1.0, scalar2=None, op0=ALU.add))
        dve(lambda: nc.vector.tensor_tensor(out=f_re[:], in0=am1[:], in1=lr[:], op=ALU.mult))
        dve(lambda: nc.vector.tensor_tensor(out=tmp[:], in0=a_im[:], in1=li[:], op=ALU.mult))
        dve(lambda: nc.vector.tensor_tensor(out=f_re[:], in0=f_re[:], in1=tmp[:], op=ALU.add))
        dve(lambda: nc.vector.tensor_tensor(out=f_re[:], in0=f_re[:], in1=den[:], op=ALU.mult))
        dve(lambda: nc.vector.tensor_tensor(out=f_im[:], in0=a_im[:], in1=lr[:], op=ALU.mult))
        dve(lambda: nc.vector.tensor_tensor(out=tmp[:], in0=am1[:], in1=li[:], op=ALU.mult))
        dve(lambda: nc.vector.tensor_tensor(out=f_im[:], in0=f_im[:], in1=tmp[:], op=ALU.subtract))
        dve(lambda: nc.vector.tensor_tensor(out=f_im[:], in0=f_im[:], in1=den[:], op=ALU.mult))
        return a_re, a_im, f_re, f_im, key, (cs, sn, mag)

    def phase_ssm(self, w_in_l, sp, s):
        p, nc = self.p, self.nc
        NS = 11
        with ExitStack() as st:
            ufT = p.sb("sm_uf", [128, 6, SEQ], F32, st)
            ubT = p.sb("sm_ub", [128, 6, SEQ], BF16, st)
            BBre = p.sb("sm_BBre", [128, 6, 128], BF16, st)
            BBim = p.sb("sm_BBim", [128, 6, 128], BF16, st)
            Are = p.sb("sm_Are", [128, 16, NS], F32, st)
            Aim = p.sb("sm_Aim", [128, 16, NS], F32, st)
            nAim = p.sb("sm_nAim", [128, 16, NS], F32, st)
            rho = p.sb("sm_rho", [128, 16], F32, st)
            Cre = p.sb("sm_Cre", [128, 16, 32], BF16, st)
            nCim = p.sb("sm_nCim", [128, 16, 32], BF16, st)
            dsk = p.sb("sm_d", [128, 6], F32, st)
            p.dma("sp", dsk[:], sp["d"], writes=["sm_d"])
            p.dma("pool", Cre[:], sp["c_re"], writes=["sm_Cre"])
            with ExitStack() as st1:
                def ld(name, shape, src):
                    t = p.sb(name, shape, F32, st1)
                    p.dma("sp", t[:], src, writes=[name])
                    return t
                lrB = ld("sm_lrB", [128, 768], sp["lamB_re"]); liB = ld("sm_liB", [128, 768], sp["lamB_im"])
                ldB = ld("sm_ldB", [128, 768], sp["lamB_dt"])
                bre = ld("sm_bre", [128, 768], sp["b_re"]); bim = ld("sm_bim", [128, 768], sp["b_im"])
                lrS = ld("sm_lrS", [128, 16], sp["lamS_re"]); liS = ld("sm_liS", [128, 16], sp["lamS_im"])
                ldS = ld("sm_ldS", [128, 16], sp["lamS_dt"])
                cim = ld("sm_cim", [128, 512], sp["c_im"])
                p.barrier()
                _, _, f_re, f_im, kB, _ = self.ssm_discretize(st1, "smB", lrB, liB, ldB, (128, 768))
                t1 = p.sb("sm_pt1", [128, 768], F32, st1)
                t2 = p.sb("sm_pt2", [128, 768], F32, st1)
                kk = ["sm_prep", kB]
                seq = [
                    lambda: nc.vector.tensor_tensor(out=t1[:], in0=bre[:], in1=f_re[:], op=ALU.mult),
                    lambda: nc.vector.tensor_tensor(out=t2[:], in0=bim[:], in1=f_im[:], op=ALU.mult),
                    lambda: nc.vector.tensor_tensor(out=BBre[:].rearrange("p m j -> p (m j)"), in0=t1[:], in1=t2[:], op=ALU.subtract),
                    lambda: nc.vector.tensor_tensor(out=t1[:], in0=bre[:], in1=f_im[:], op=ALU.mult),
                    lambda: nc.vector.tensor_tensor(out=t2[:], in0=bim[:], in1=f_re[:], op=ALU.mult),
                    lambda: nc.vector.tensor_tensor(out=BBim[:].rearrange("p m j -> p (m j)"), in0=t1[:], in1=t2[:], op=ALU.add),
                    lambda: nc.vector.tensor_scalar(out=nCim[:].rearrange("p m j -> p (m j)"), in0=cim[:], scalar1=-1.0, scalar2=None, op0=ALU.mult),
                ]
                for fn in seq:
                    p.op("dve", fn, reads=kk, writes=kk)
                _, _, _, _, kS, (ucs, usn, umag) = self.ssm_discretize(st1, "smS", lrS, liS, ldS, (128, 16))
                kk = ["sm_prep", kS]
                sq1 = p.sb("sm_sq1", [128, 16], F32, st1)
                sq2 = p.sb("sm_sq2", [128, 16], F32, st1)
                p.op("dve", lambda: nc.vector.tensor_copy(out=Are[:, :, 0], in_=ucs[:]), reads=kk, writes=kk)
                p.op("dve", lambda: nc.vector.tensor_copy(out=Aim[:, :, 0], in_=usn[:]), reads=kk, writes=kk)
                p.op("dve", lambda: nc.vector.tensor_copy(out=rho[:], in_=umag[:]), reads=kk, writes=kk)
                for k in range(1, NS):
                    for fn in (
                        lambda k=k: nc.vector.tensor_tensor(out=sq1[:], in0=Are[:, :, k - 1], in1=Are[:, :, k - 1], op=ALU.mult),
                        lambda k=k: nc.vector.tensor_tensor(out=sq2[:], in0=Aim[:, :, k - 1], in1=Aim[:, :, k - 1], op=ALU.mult),
                        lambda k=k: nc.vector.tensor_tensor(out=Are[:, :, k], in0=sq1[:], in1=sq2[:], op=ALU.subtract),
                        lambda k=k: nc.vector.tensor_tensor(out=sq1[:], in0=Are[:, :, k - 1], in1=Aim[:, :, k - 1], op=ALU.mult),
                        lambda k=k: nc.vector.tensor_scalar(out=Aim[:, :, k], in0=sq1[:], scalar1=2.0, scalar2=None, op0=ALU.mult),
                    ):
                        p.op("dve", fn, reads=kk, writes=kk)
                p.op("dve", lambda: nc.vector.tensor_scalar(out=nAim[:], in0=Aim[:], scalar1=-1.0, scalar2=None, op0=ALU.mult),
                     reads=kk, writes=kk)
                p.barrier()
            with ExitStack() as st2:
                xT = p.sb("sm_x", [128, KC, SEQ], BF16, st2)
                W = p.sb("sm_w", [128, KC, 768], BF16, st2)
                acc = [p.ps(f"sm_acc{i}", [128, 512], F32, st2) for i in range(2)]
                self.load_xT(xT, s, "sm_x")
                self.load_w(w_in_l, "ssm_u", W, "sm_w")
                n = 0
                for m in range(6):
                    for tb in range(4):
                        a = acc[n % 2]
                        ak = ("P_smacc", n % 2)
                        n += 1
                        self.proj_fm(W, "sm_w", m * 128, 128, xT, "sm_x", tb * 512, 512, a, ak)
                        p.op("act", lambda a=a, m=m, tb=tb: nc.scalar.copy(out=ufT[:, m, tb * 512:(tb + 1) * 512], in_=a[:]),
                             reads=[ak], writes=[("sm_uf", m)])
                        p.op("dve", lambda a=a, m=m, tb=tb: nc.vector.tensor_copy(out=ubT[:, m, tb * 512:(tb + 1) * 512], in_=a[:]),
                             reads=[ak], writes=[("sm_ub", m)])
                p.barrier()
            glT = p.sb("sm_gl", [128, 6, SEQ], BF16, st)
            with ExitStack() as st3:
                Ere = [p.sb(f"sm_Ere{i}", [128, SEQ], F32, st3) for i in range(2)]
                Eim = [p.sb(f"sm_Eim{i}", [128, SEQ], F32, st3) for i in range(2)]
                Mre = p.sb("sm_Mre", [128, SEQ], F32, st3)
                Mim = p.sb("sm_Mim", [128, SEQ], F32, st3)
                Sre = p.sb("sm_Sre", [128, SEQ], F32, st3)
                Sim = p.sb("sm_Sim", [128, SEQ], F32, st3)
                rhoT = [p.sb("sm_rhoT", [128, SEQ], F32, st3)] * 2
                tm = [p.sb(f"sm_tm{i}", [128, 512], F32, st3) for i in range(2)]
                xbr = p.sb("sm_xbr", [128, SEQ], BF16, st3)
                xbi = p.sb("sm_xbi", [128, SEQ], BF16, st3)
                yvs = [p.sb(f"sm_yv{i}", [128, 512], F32, st3) for i in range(2)]
                y2s = [p.sb(f"sm_y2{i}", [128, 512], F32, st3) for i in range(2)]
                BUr = [p.ps(f"sm_BUr{i}", [128, 512], F32, st3) for i in range(2)]
                BUi = [p.ps(f"sm_BUi{i}", [128, 512], F32, st3) for i in range(2)]
                Yp = [p.ps(f"sm_Y{i}", [128, 512], F32, st3) for i in range(4)]

                def gen_table(sb_):
                    e = sb_ % 2
                    ek = ("sm_E", e)
                    steps = []

                    def init():
                        p.op("pool", lambda: nc.gpsimd.memset(Ere[e][:, 0:1], 1.0), writes=[ek])
                        p.op("pool", lambda: nc.gpsimd.memset(Eim[e][:, 0:1], 0.0), writes=[ek])
                    steps.append(init)
                    for k in range(NS):
                        steps.append(lambda k=k: gen_step(sb_, e, ek, k))
                    return steps

                def gen_step(sb_, e, ek, k):
                    if True:
                        d = 1 << k
                        ur, ui, nui = Are[:, sb_, k:k + 1], Aim[:, sb_, k:k + 1], nAim[:, sb_, k:k + 1]
                        p.op("act", lambda d=d, ur=ur: nc.scalar.activation(out=Ere[e][:, d:2 * d], in_=Ere[e][:, 0:d], func=AF.Copy, scale=ur),
                             reads=[ek, "sm_A"], writes=[ek])
                        p.op("act", lambda d=d, ur=ur: nc.scalar.activation(out=Eim[e][:, d:2 * d], in_=Eim[e][:, 0:d], func=AF.Copy, scale=ur),
                             reads=[ek, "sm_A"], writes=[ek])
                        p.op("dve", lambda d=d, nui=nui: nc.vector.scalar_tensor_tensor(
                            out=Ere[e][:, d:2 * d], in0=Eim[e][:, 0:d], scalar=nui, in1=Ere[e][:, d:2 * d], op0=ALU.mult, op1=ALU.add),
                            reads=[ek, "sm_A"], writes=[ek])
                        p.op("dve", lambda d=d, ui=ui: nc.vector.scalar_tensor_tensor(
                            out=Eim[e][:, d:2 * d], in0=Ere[e][:, 0:d], scalar=ui, in1=Eim[e][:, d:2 * d], op0=ALU.mult, op1=ALU.add),
                            reads=[ek, "sm_A"], writes=[ek])

                for th_ in gen_table(0):
                    th_()
                nbu = 0
                for sb_ in range(16):
                    m, q0 = sb_ // 3, (sb_ % 3) * 32
                    e = sb_ % 2
                    ek = ("sm_E", e)
                    nxt = gen_table(sb_ + 1) if sb_ + 1 < 16 else []
                    for tb in range(4):
                        ts_ = slice(tb * 512, (tb + 1) * 512)
                        bb = nbu % 2
                        nbu += 1
                        BUr_, BUi_ = [BUr[bb]], [BUi[bb]]
                        kr_, ki_ = ("P_smBUr", bb), ("P_smBUi", bb)
                        p.op("pe", lambda ts_=ts_: nc.tensor.matmul(BUr_[0][:], lhsT=BBre[q0:q0 + 32, m, :], rhs=ubT[q0:q0 + 32, m, ts_], start=True, stop=True),
                             reads=["sm_BB", ("sm_ub", m)], writes=[kr_])
                        p.op("pe", lambda ts_=ts_: nc.tensor.matmul(BUi_[0][:], lhsT=BBim[q0:q0 + 32, m, :], rhs=ubT[q0:q0 + 32, m, ts_], start=True, stop=True),
                             reads=["sm_BB", ("sm_ub", m)], writes=[ki_])
                        p.op("dve", lambda ts_=ts_: nc.vector.tensor_tensor(out=tm[0][:], in0=BUi_[0][:], in1=Eim[e][:, ts_], op=ALU.mult),
                             reads=[ki_, ek], writes=[("sm_tm", 0)])
                        p.op("dve", lambda ts_=ts_: nc.vector.tensor_tensor(out=Mre[:, ts_], in0=BUr_[0][:], in1=Ere[e][:, ts_], op=ALU.mult),
                             reads=[kr_, ek], writes=["sm_Mre"])
                        if nxt:
                            nxt.pop(0)()
                        p.op("dve", lambda ts_=ts_: nc.vector.tensor_tensor(out=tm[1][:], in0=BUr_[0][:], in1=Eim[e][:, ts_], op=ALU.mult),
                             reads=[kr_, ek], writes=[("sm_tm", 1)])
                        p.op("dve", lambda ts_=ts_: nc.vector.tensor_tensor(out=Mim[:, ts_], in0=BUi_[0][:], in1=Ere[e][:, ts_], op=ALU.mult),
                             reads=[ki_, ek], writes=["sm_Mim"])
                        if nxt:
                            nxt.pop(0)()
                        p.op("dve", lambda ts_=ts_: nc.vector.tensor_tensor(out=Mre[:, ts_], in0=Mre[:, ts_], in1=tm[0][:], op=ALU.add),
                             reads=["sm_Mre", ("sm_tm", 0)], writes=["sm_Mre"])
                        p.op("dve", lambda ts_=ts_: nc.vector.tensor_tensor(out=Mim[:, ts_], in0=Mim[:, ts_], in1=tm[1][:], op=ALU.subtract),
                             reads=["sm_Mim", ("sm_tm", 1)], writes=["sm_Mim"])
                        if nxt:
                            nxt.pop(0)()
                    p.op("act", lambda: nc.scalar.activation(out=rhoT[e][:], in_=Ere[e][:], func=AF.Identity, scale=0.0, bias=rho[:, sb_:sb_ + 1]),
                         reads=[ek, "sm_A", "sm_Sre", "sm_Sim"], writes=["sm_rhoT"])
                    p.op("dve", lambda: nc.vector.tensor_tensor_scan(out=Sre[:], data0=rhoT[e][:], data1=Mre[:], initial=0.0, op0=ALU.mult, op1=ALU.add),
                         reads=["sm_rhoT", "sm_Mre"], writes=["sm_Sre"])
                    p.op("dve", lambda: nc.vector.tensor_tensor_scan(out=Sim[:], data0=rhoT[e][:], data1=Mim[:], initial=0.0, op0=ALU.mult, op1=ALU.add),
                         reads=["sm_rhoT", "sm_Mim"], writes=["sm_Sim"])
                    p.op("dve", lambda: nc.vector.tensor_tensor(out=Mre[:], in0=Sim[:], in1=Eim[e][:], op=ALU.mult),
                         reads=["sm_Sim", ek, "sm_Mre"], writes=["sm_Mre"])
                    p.op("dve", lambda: nc.vector.tensor_tensor(out=Mim[:], in0=Sre[:], in1=Ere[e][:], op=ALU.mult),
                         reads=["sm_Sre", ek, "sm_Mim"], writes=["sm_Mim"])
                    p.op("dve", lambda: nc.vector.tensor_tensor(out=xbr[:], in0=Mim[:], in1=Mre[:], op=ALU.subtract),
                         reads=["sm_Mre", "sm_Mim"], writes=["sm_xbr"])
                    p.op("dve", lambda: nc.vector.tensor_tensor(out=Mre[:], in0=Sre[:], in1=Eim[e][:], op=ALU.mult),
                         reads=["sm_Sre", ek, "sm_Mre", "sm_xbr"], writes=["sm_Mre"])
                    p.op("dve", lambda: nc.vector.tensor_tensor(out=Mim[:], in0=Sim[:], in1=Ere[e][:], op=ALU.mult),
                         reads=["sm_Sim", ek, "sm_Mim", "sm_xbr"], writes=["sm_Mim"])
                    p.op("dve", lambda: nc.vector.tensor_tensor(out=xbi[:], in0=Mim[:], in1=Mre[:], op=ALU.add),
                         reads=["sm_Mre", "sm_Mim"], writes=["sm_xbi"])
                    while nxt:
                        nxt.pop(0)()
                    for tb in range(4):
                        ts_ = slice(tb * 512, (tb + 1) * 512)
                        p.op("pe", lambda tb=tb, ts_=ts_: nc.tensor.matmul(Yp[tb][q0:q0 + 32, :], lhsT=Cre[:, sb_, :], rhs=xbr[:, ts_],
                                                                           start=True, stop=False),
                             reads=["sm_Cre", "sm_xbr"], writes=[("P_smY", tb)])
                        p.op("pe", lambda tb=tb, ts_=ts_: nc.tensor.matmul(Yp[tb][q0:q0 + 32, :], lhsT=nCim[:, sb_, :], rhs=xbi[:, ts_],
                                                                           start=False, stop=True),
                             reads=["sm_Cre", "sm_xbi"], writes=[("P_smY", tb)])
                    if sb_ % 3 == 2 or sb_ == 15:
                        nr = 96 if m < 5 else 32

                        def gA(tb):
                            ts_ = slice(tb * 512, (tb + 1) * 512)
                            yv_, y2_ = yvs[tb % 2], y2s[tb % 2]
                            kv, k2 = ("sm_yv", tb % 2), ("sm_y2", tb % 2)
                            p.op("dve", lambda: nc.vector.scalar_tensor_tensor(
                                out=yv_[0:nr, :], in0=ufT[0:nr, m, ts_], scalar=dsk[0:nr, m:m + 1], in1=Yp[tb][0:nr, :], op0=ALU.mult, op1=ALU.add),
                                reads=[("sm_uf", m), "sm_d", ("P_smY", tb)], writes=[kv])
                            p.op("dve", lambda: nc.vector.tensor_tensor(out=y2_[0:nr, :], in0=yv_[0:nr, :], in1=yv_[0:nr, :], op=ALU.mult),
                                 reads=[kv], writes=[k2])
                            p.op("dve", lambda: nc.vector.tensor_scalar(out=y2_[0:nr, :], in0=y2_[0:nr, :], scalar1=0.044715, scalar2=1.0,
                                                                        op0=ALU.mult, op1=ALU.add),
                                 reads=[k2], writes=[k2])
                            p.op("dve", lambda: nc.vector.tensor_tensor(out=y2_[0:nr, :], in0=y2_[0:nr, :], in1=yv_[0:nr, :], op=ALU.mult),
                                 reads=[k2, kv], writes=[k2])
                            p.op("act", lambda: nc.scalar.activation(out=y2_[0:nr, :], in_=y2_[0:nr, :], func=AF.Sigmoid, scale=2.0 * math.sqrt(2.0 / math.pi)),
                                 reads=[k2], writes=[k2])

                        def gB(tb):
                            ts_ = slice(tb * 512, (tb + 1) * 512)
                            yv_, y2_ = yvs[tb % 2], y2s[tb % 2]
                            kv, k2 = ("sm_yv", tb % 2), ("sm_y2", tb % 2)
                            p.op("dve", lambda: nc.vector.tensor_tensor(out=glT[0:nr, m, ts_], in0=yv_[0:nr, :], in1=y2_[0:nr, :], op=ALU.mult),
                                 reads=[kv, k2], writes=[("sm_gl", m)])

                        gA(0); gA(1); gB(0); gA(2); gB(1); gA(3); gB(2); gB(3)
                p.barrier()
            Wg = p.sb("sm_Wg", [128, 6, 1024], BF16, st)
            osb = [p.sb(f"sm_o{i}", [128, 512], BF16, st) for i in range(2)]
            sgm = p.sb("sm_sg2", [128, 512], F32, st)
            GA = p.ps("sm_GA", [128, 512], F32, st)
            GB = p.ps("sm_GB", [128, 512], F32, st)
            p.dma("pool", Wg[:], sp["w_glu"], writes=["sm_Wg"], max_dma_last_dim=8192)
            no = 0
            for oc in range(4):
                for tb in range(4):
                    ts_ = slice(tb * 512, (tb + 1) * 512)
                    for acc_, c0, ak in ((GA, oc * 128, "P_smGA"), (GB, 512 + oc * 128, "P_smGB")):
                        for kc in range(6):
                            kr = 96 if kc < 5 else 32
                            p.op("pe", lambda acc_=acc_, c0=c0, kc=kc, ts_=ts_, kr=kr: nc.tensor.matmul(
                                acc_[:], lhsT=Wg[0:kr, kc, c0:c0 + 128], rhs=glT[0:kr, kc, ts_], start=(kc == 0), stop=(kc == 5)),
                                reads=["sm_Wg", ("sm_gl", kc)], writes=[ak])
                    oi = no % 2
                    no += 1
                    p.op("act", lambda: nc.scalar.activation(out=sgm[:], in_=GB[:], func=AF.Sigmoid), reads=["P_smGB"], writes=["sm_sg"])
                    p.op("dve", lambda oi=oi: nc.vector.tensor_tensor(out=osb[oi][:], in0=GA[:], in1=sgm[:], op=ALU.mult),
                         reads=["P_smGA", "sm_sg"], writes=[("sm_o", oi)])
                    tok0 = s * SEQ + tb * 512
                    p.dma("sp", self.YT[6 + oc, :, tok0:tok0 + 512], osb[oi][:], reads=[("sm_o", oi)], writes=[("YT", tok0 // TT)])
            p.barrier()


def t5_bucket_np(dist):
    dist = np.asarray(dist, np.int64)
    d = np.maximum(dist, 1).astype(np.float32)
    large = 16 + (np.log(d / np.float32(16)) / np.float32(math.log(128 / 16)) * np.float32(16)).astype(np.int32)
    large = np.minimum(large, 31)
    return np.where(dist < 16, dist, large)


def lay_dil_bias(rel_bias):
    out = np.full((128, 3, 4, 2, 128), -1e30, np.float32)
    sl = np.arange(128)[:, None]
    tl = np.arange(128)[None, :]
    for g, r in enumerate((1, 4, 16)):
        for kind in range(2):
            dloc = tl - sl + 128 * kind
            valid = (dloc >= 0) & (dloc <= 128)
            bk = t5_bucket_np(np.clip(dloc, 0, None) * r)
            for hh in range(4):
                vals = rel_bias[bk, g * 4 + hh]
                out[:, g, hh, kind, :] = np.where(valid, vals, np.float32(-1e30))
    return out.reshape(128, 24 * 128)


def lay_dsa_bias(rel_bias):
    out = np.full((128, 8, 2, 128), -1e30, np.float32)
    sl = np.arange(128)[:, None]
    tl = np.arange(128)[None, :]
    for kind in range(2):
        d = tl - sl + 128 * kind
        valid = d >= 0
        bk = t5_bucket_np(np.clip(d, 0, None))
        for h in range(8):
            out[:, h, kind, :] = np.where(valid, rel_bias[bk, 12 + h], np.float32(-1e30))
    b31 = np.ascontiguousarray(np.broadcast_to(rel_bias[31, 12:20][None, :], (128, 8))).astype(np.float32)
    return out.reshape(128, 16 * 128), b31


def lay_ssm(lam_re, lam_im, log_dt, b_re, b_im, c_re, c_im, d_skip, w_glu):
    out = {}
    q = np.arange(128)
    qd, gl, c = np.minimum(q // 32, 2), (q % 32) // 16, q % 16
    qvalid = q < 96
    mB = np.arange(6)
    j = np.arange(128)
    glp, pj = j // 64, j % 64
    sbq = 3 * mB[None, :, None] + qd[:, None, None]
    okB = (sbq < 16) & qvalid[:, None, None]
    sbq = np.minimum(sbq, 15)
    gB = 2 * sbq + glp[None, None, :]
    pB = np.broadcast_to(pj[None, None, :], gB.shape)
    out["lamB_re"] = lam_re[gB, pB].reshape(128, 768).astype(np.float32)
    out["lamB_im"] = lam_im[gB, pB].reshape(128, 768).astype(np.float32)
    out["lamB_dt"] = log_dt[gB].reshape(128, 768).astype(np.float32)
    same = (gl[:, None, None] == glp[None, None, :]) & okB
    cB = np.broadcast_to(c[:, None, None], gB.shape)
    out["b_re"] = np.where(same, b_re[gB, pB, cB], 0.0).reshape(128, 768).astype(np.float32)
    out["b_im"] = np.where(same, b_im[gB, pB, cB], 0.0).reshape(128, 768).astype(np.float32)
    sbS = np.arange(16)[None, :]
    gS = 2 * sbS + glp[:, None]
    pS = np.broadcast_to(pj[:, None], gS.shape)
    out["lamS_re"] = lam_re[gS, pS].astype(np.float32)
    out["lamS_im"] = lam_im[gS, pS].astype(np.float32)
    out["lamS_dt"] = log_dt[gS].astype(np.float32)
    ch = np.arange(32)
    glc, cc = ch // 16, ch % 16
    gC = 2 * np.arange(16)[None, :, None] + glp[:, None, None] + 0 * glc[None, None, :]
    sameC = (glp[:, None, None] == glc[None, None, :]) & np.ones((1, 16, 1), bool)
    cC = np.broadcast_to(cc[None, None, :], gC.shape)
    pC = np.broadcast_to(pj[:, None, None], gC.shape)
    out["c_re"] = np.where(sameC, c_re[gC, cC, pC], 0.0).astype(np.float32)
    out["c_im"] = np.where(sameC, c_im[gC, cC, pC], 0.0).reshape(128, 512).astype(np.float32)
    chan = 96 * mB[None, :] + q[:, None]
    okc = (chan < 512) & qvalid[:, None]
    chc = np.minimum(chan, 511)
    out["d"] = np.where(okc, d_skip[chc], 0.0).astype(np.float32)
    out["w_glu"] = np.where(okc[:, :, None], w_glu[chc, :], 0.0).astype(np.float32)
    return out


SSM_SHAPES = {"lamB_re": [128, 768], "lamB_im": [128, 768], "lamB_dt": [128, 768], "b_re": [128, 768], "b_im": [128, 768],
              "lamS_re": [128, 16], "lamS_im": [128, 16], "lamS_dt": [128, 16], "c_re": [128, 16, 32], "c_im": [128, 512],
              "d": [128, 6], "w_glu": [128, 6, 1024]}


def lay_up(w):
    return np.ascontiguousarray(w.reshape(16, 128, 2, 22, 256).transpose(3, 2, 1, 0, 4))


def lay_down(w):
    return np.ascontiguousarray(w.reshape(44, 128, 16, 128).transpose(2, 1, 0, 3))


def lay_br(ws):
    w = np.concatenate(ws, 0).reshape(NYC, 128, 16, 128)
    return np.ascontiguousarray(w.transpose(2, 1, 0, 3))


def lay_sq(w):
    return np.ascontiguousarray(w.reshape(16, 128, 16, 128).transpose(2, 1, 0, 3))


def lay_ln(v):
    return np.ascontiguousarray(v.reshape(6, 16, 128).transpose(2, 0, 1).reshape(128, 96))


N_CORES = 8
TTOT_CORE = 2 * SEQ


def build_full(ttot=TTOT_CORE, depth=DEPTH):
    b = Builder(ttot)
    nseq = ttot // SEQ
    x = b.inp("x", [ttot, D])
    y = b.nc.dram_tensor("y", [ttot, D], F32, kind="ExternalOutput").ap()
    L = []
    for l in range(depth):
        d = {}
        for f in ("ffn1", "ffn2"):
            d[f + "_up"] = b.inp(f"{f}_up{l}", [22, 2, 128, 16, 256])
            d[f + "_down"] = b.inp(f"{f}_down{l}", [16, 128, 44, 128])
        d["w_in"] = b.inp(f"w_in{l}", [128, WIN_TOT])
        d["w_br"] = b.inp(f"w_br{l}", [16, 128, NYC, 128])
        d["w_out"] = b.inp(f"w_out{l}", [16, 128, 16, 128])
        d["ssm"] = {k: b.inp(f"ssm{l}_{k}", SSM_SHAPES[k]) for k in SSM_SHAPES}
        L.append(d)
    dil_bias = b.inp("dil_bias", [128, 24 * 128])
    dsa_bias = b.inp("dsa_bias", [128, 16 * 128])
    dsa_b31 = b.inp("dsa_b31", [128, 8])
    negm = b.inp("cm_negmask", [128, 128])
    b.load_consts()
    b.load_mixer_consts()
    b.p.barrier()
    b.phase_in(x)
    for l in range(depth):
        d = L[l]
        b.phase_ffn(d["ffn1_up"], d["ffn1_down"], l * 3 + 0)
        for s in range(nseq):
            b.phase_sb(d["w_in"], s)
            b.phase_dil(d["w_in"], dil_bias, s)
            b.phase_ssm(d["w_in"], d["ssm"], s)
            b.phase_dsa(d["w_in"], dsa_bias, dsa_b31, negm, s)
        b.phase_merge(d["w_in"], d["w_br"], d["w_out"], l * 3 + 1)
        b.phase_ffn(d["ffn2_up"], d["ffn2_down"], l * 3 + 2)
    b.phase_out(y)
    b.p.finish()
    return b


def host_inputs(inp, depth=DEPTH):
    f = lambda a: np.asarray(a, np.float32)
    sh = {}
    for l in range(depth):
        sh[f"ffn1_up{l}"] = lay_up(f(inp["ffn1_w_up"][l]))
        sh[f"ffn1_down{l}"] = lay_down(f(inp["ffn1_w_down"][l]))
        sh[f"ffn2_up{l}"] = lay_up(f(inp["ffn2_w_up"][l]))
        sh[f"ffn2_down{l}"] = lay_down(f(inp["ffn2_w_down"][l]))
        sh[f"w_in{l}"] = lay_w_in(f(inp["w_in"][l]))
        sh[f"w_br{l}"] = lay_br([f(inp[k][l]) for k in ("w_br_sb", "w_br_dil", "w_br_ssm", "w_br_dsa")])
        sh[f"w_out{l}"] = lay_sq(f(inp["w_out"][l]))
        ss = lay_ssm(f(inp["ssm_lam_re"][l]), f(inp["ssm_lam_im"][l]), f(inp["ssm_log_dt"][l]), f(inp["ssm_b_re"][l]),
                     f(inp["ssm_b_im"][l]), f(inp["ssm_c_re"][l]), f(inp["ssm_c_im"][l]), f(inp["ssm_d"][l]),
                     f(inp["ssm_w_glu"][l]))
        for k, v in ss.items():
            sh[f"ssm{l}_{k}"] = np.ascontiguousarray(v)
    rb = f(inp["rel_bias"])
    sh["dil_bias"] = lay_dil_bias(rb)
    sh["dsa_bias"], sh["dsa_b31"] = lay_dsa_bias(rb)
    sh["ln_g_l"] = lay_ln(f(inp["ln_g"]))
    sh["ln_b_l"] = lay_ln(f(inp["ln_b"]))
    pp = np.arange(128)[:, None]
    sh["cm_ident"] = np.eye(128, dtype=np.float32)
    sh["cm_sbmask"] = (pp < np.arange(512)[None, :]).astype(np.float32)
    sh["cm_triu"] = (pp > np.arange(128)[None, :]).astype(np.float32)
    sh["cm_negmask"] = np.where(np.arange(128)[None, :] <= pp, 0.0, -1e30).astype(np.float32)
    return sh


def kernel(**inputs):
    x = np.asarray(inputs["x"], np.float32)
    bsz, L, _ = x.shape
    assert (bsz, L) == (16, SEQ)
    shared = host_inputs(inputs)
    b = build_full()
    xs = x.reshape(N_CORES, TTOT_CORE, D)
    in_maps = []
    for c in range(N_CORES):
        m = dict(shared)
        m["x"] = np.ascontiguousarray(xs[c])
        in_maps.append({k: v for k, v in m.items() if k in b.dram_in})
    res = run_bass_kernel_spmd(b.nc, in_maps, core_ids=list(range(N_CORES)))
    out = np.stack([np.asarray(r["y"]) for r in res.results], 0)
    return out.reshape(bsz, L, D).astype(np.float32)
```

```python
import math
from contextlib import ExitStack

import numpy as np
import concourse.bass as bass
import concourse.mybir as mybir
from concourse.bass_utils import run_bass_kernel_spmd

F32 = mybir.dt.float32
BF16 = mybir.dt.bfloat16
AF = mybir.ActivationFunctionType
ALU = mybir.AluOpType
AX = mybir.AxisListType


def _is_psum(key):
    name = key[0] if isinstance(key, tuple) else key
    return isinstance(name, str) and name.startswith("P_")


class Prog:
    NQ = 12

    def __init__(self):
        self.nc = bass.Bass("TRN2", target_bir_lowering=False)
        nc = self.nc
        self.es = ExitStack()
        self.eng = {"pe": nc.tensor, "act": nc.scalar, "dve": nc.vector, "pool": nc.gpsimd, "sp": nc.sync}
        self.sem = {}
        self.cnt = {}
        for e in ("pe", "act", "dve", "pool"):
            self.sem[e] = self.es.enter_context(nc.semaphore("s_" + e))
            self.cnt[e] = 0
        self.dq = {}
        for q in ("sp", "pool", "act"):
            sems = [self.es.enter_context(nc.semaphore(f"d_{q}{i}")) for i in range(self.NQ)]
            for i, s in enumerate(sems):
                self.sem[(q, i)] = s
                self.cnt[(q, i)] = 0
            self.dq[q] = 0
        self.seen = {e: {} for e in self.eng}
        self.last_w = {}
        self.readers = {}
        self.n_ins = 0
        self.n_wait = 0

    def _uniq(self, name):
        self.n_alloc = getattr(self, "n_alloc", 0) + 1
        return f"{name}__{self.n_alloc}"

    def sb(self, name, shape, dtype, stack=None):
        return (stack or self.es).enter_context(self.nc.sbuf_tensor(self._uniq(name), list(shape), dtype))

    def ps(self, name, shape, dtype=F32, stack=None):
        return (stack or self.es).enter_context(self.nc.psum_tensor(self._uniq(name), list(shape), dtype))

    def _wait(self, e, key, val):
        if val <= 0:
            return
        if self.seen[e].get(key, 0) >= val:
            return
        self.eng[e].wait_ge(self.sem[key], val)
        self.seen[e][key] = val
        self.n_wait += 1

    def _deps(self, e, reads, writes, strict_pe=False):
        deps = {}
        for r in reads:
            lw = self.last_w.get(r)
            if lw:
                deps[lw[0]] = max(deps.get(lw[0], 0), lw[1])
            if _is_psum(r):
                for k, v in self.readers.get(r, {}).items():
                    if k != e:
                        deps[k] = max(deps.get(k, 0), v)
        for w in writes:
            lw = self.last_w.get(w)
            if lw:
                deps[lw[0]] = max(deps.get(lw[0], 0), lw[1])
            for k, v in self.readers.get(w, {}).items():
                deps[k] = max(deps.get(k, 0), v)
        for k, v in deps.items():
            if k == "pe" and e == "pe" and not strict_pe:
                continue
            self._wait(e, k, v)

    def _mark(self, key, val, reads, writes):
        for w in writes:
            self.last_w[w] = (key, val)
            self.readers[w] = {}
        for r in reads:
            if r in writes:
                continue
            d = self.readers.setdefault(r, {})
            d[key] = max(d.get(key, 0), val)

    def op(self, e, fn, reads=(), writes=(), strict_pe=False):
        self._deps(e, reads, writes, strict_pe)
        ins = fn()
        self.cnt[e] += 1
        ins.then_inc(self.sem[e], 1)
        self._mark(e, self.cnt[e], reads, writes)
        self.n_ins += 1
        return ins

    def dma(self, q, out, in_, reads=(), writes=(), **kw):
        i = self.dq[q] % self.NQ
        self.dq[q] += 1
        key = (q, i)
        self._wait(q, key, self.cnt[key])
        self._deps(q, reads, writes)
        ins = self.eng[q].dma_start(out=out, in_=in_, **kw)
        self.cnt[key] += 16
        ins.then_inc(self.sem[key], 16)
        self._mark(key, self.cnt[key], reads, writes)
        self.n_ins += 1
        return ins

    def barrier(self):
        for e in self.eng:
            for k, v in self.cnt.items():
                if k == e and e == "pe":
                    continue
                self._wait(e, k, v)
        self.last_w.clear()
        self.readers.clear()

    def finish(self):
        for k, v in self.cnt.items():
            self._wait("sp", k, v)
        self.es.close()


D = 2048
KC = D // 128
DFF = 5632
NJ = DFF // 128
DEPTH = 2
DN_ALPHA = (2.0 * DEPTH) ** 0.25
LN_EPS = 1e-5
TT = 512
NYC = 14
SEQ = 2048
IN_COLS = 14664
GATE0 = 6472


def win_blocks():
    blks = {}
    blks["sb_q"] = list(range(0, 512))
    blks["sb_k"] = list(range(512, 1024))
    blks["sb_v"] = list(range(1024, 1536))
    for g in range(3):
        blks[f"dil_q{g}"] = list(range(1536 + g * 256, 1536 + (g + 1) * 256))
        blks[f"dil_k{g}"] = list(range(2304 + g * 256, 2304 + (g + 1) * 256))
        blks[f"dil_v{g}"] = list(range(3072 + g * 256, 3072 + (g + 1) * 256))
    blks["ssm_u"] = [3840 + min(96 * m + q, 511) for m in range(6) for q in range(128)]
    blks["dsa_q"] = list(range(4352, 4864))
    blks["dsa_k"] = list(range(4864, 5376))
    blks["dsa_v"] = list(range(5376, 5888))
    blks["idx_q"] = list(range(5888, 6400))
    blks["idx_k"] = list(range(6400, 6464)) * 2
    blks["idx_w"] = list(range(6464, 6472))
    for c in range(KC):
        cols = []
        for b in range(4):
            cols += list(range(GATE0 + b * D + c * 128, GATE0 + b * D + (c + 1) * 128))
        blks[f"gate{c}"] = cols
    return blks


WIN_BLOCKS = win_blocks()
WIN_OFF = {}
_o = 0
for _k, _v in WIN_BLOCKS.items():
    WIN_OFF[_k] = (_o, len(_v))
    _o += KC * len(_v)
WIN_TOT = _o


def lay_w_in(w):
    w3 = w.reshape(KC, 128, IN_COLS)
    out = np.empty((128, WIN_TOT), np.float32)
    for k, cols in WIN_BLOCKS.items():
        o, n = WIN_OFF[k]
        out[:, o:o + KC * n] = w3[:, :, cols].transpose(1, 0, 2).reshape(128, KC * n)
    return out


class Builder:
    def __init__(self, ttot):
        self.p = Prog()
        self.nc = self.p.nc
        self.ttot = ttot
        nc = self.nc
        self.dram_in = {}
        self.XTf = nc.dram_tensor("XTf", [KC, 128, ttot], F32, kind="Internal").ap()
        self.XTb = nc.dram_tensor("XTb", [KC, 128, ttot], BF16, kind="Internal").ap()
        self.YT = nc.dram_tensor("YT", [NYC, 128, ttot], BF16, kind="Internal").ap()
        p = self.p
        self.ident_f = p.sb("ident_f", [128, 128], F32)
        self.ones_mean = p.sb("ones_mean", [128, 128], F32)
        self.lng = p.sb("lng", [128, DEPTH * 3 * KC], F32)
        self.lnb = p.sb("lnb", [128, DEPTH * 3 * KC], F32)

    def inp(self, name, shape, dtype=F32):
        t = self.nc.dram_tensor(name, list(shape), dtype, kind="ExternalInput").ap()
        self.dram_in[name] = t
        return t

    def load_consts(self):
        p, nc = self.p, self.nc
        ident = self.inp("cm_ident", [128, 128])
        lng = self.inp("ln_g_l", [128, DEPTH * 3 * KC])
        lnb = self.inp("ln_b_l", [128, DEPTH * 3 * KC])
        p.dma("sp", self.ident_f[:], ident, writes=["ident_f"])
        p.dma("sp", self.lng[:], lng, writes=["lng"])
        p.dma("sp", self.lnb[:], lnb, writes=["lnb"])
        p.op("pool", lambda: nc.gpsimd.memset(self.ones_mean[:], 1.0 / D), writes=["ones_mean"])

    def phase_in(self, x):
        p, nc = self.p, self.nc
        with ExitStack() as st:
            xin = [p.sb(f"pi_x{i}", [128, D], F32, st) for i in range(2)]
            sf = [p.sb(f"pi_sf{i}", [128, KC, 512], F32, st) for i in range(2)]
            sb_ = [p.sb(f"pi_sb{i}", [128, KC, 512], BF16, st) for i in range(2)]
            pt = [p.ps(f"pi_pt{i}", [128, 512], F32, st) for i in range(4)]
            nblk = self.ttot // 128
            n = 0
            for g in range(self.ttot // 512):
                gi = g % 2
                for b4 in range(4):
                    blk = g * 4 + b4
                    xi = blk % 2
                    p.dma("sp", xin[xi][:], x[blk * 128:(blk + 1) * 128, :], writes=[("pi_x", xi)])
                    for c4 in range(4):
                        ps = pt[n % 4]
                        pk = ("P_pi", n % 4)
                        n += 1
                        for cc in range(4):
                            c = c4 * 4 + cc
                            p.op("pe", lambda ps=ps, cc=cc, c=c, xi=xi: nc.tensor.transpose(
                                ps[:, cc * 128:(cc + 1) * 128], xin[xi][:, c * 128:(c + 1) * 128], self.ident_f[:]),
                                reads=[("pi_x", xi), "ident_f"], writes=[pk])
                        src = ps[:].rearrange("p (c t) -> p c t", c=4)
                        p.op("dve", lambda src=src, c4=c4, gi=gi, b4=b4: nc.vector.tensor_copy(
                            sf[gi][:, c4 * 4:(c4 + 1) * 4, b4 * 128:(b4 + 1) * 128], src),
                            reads=[pk], writes=[("pi_sf", gi, b4, c4)])
                        p.op("act", lambda src=src, c4=c4, gi=gi, b4=b4: nc.scalar.copy(
                            sb_[gi][:, c4 * 4:(c4 + 1) * 4, b4 * 128:(b4 + 1) * 128], src),
                            reads=[pk], writes=[("pi_sb", gi, b4, c4)])
                rk = [("pi_sf", gi, b4, c4) for b4 in range(4) for c4 in range(4)]
                rk2 = [("pi_sb", gi, b4, c4) for b4 in range(4) for c4 in range(4)]
                for c in range(KC):
                    p.dma("sp", self.XTf[c, :, g * 512:(g + 1) * 512], sf[gi][:, c, :],
                          reads=[("pi_sf", gi, b4, c // 4) for b4 in range(4)], writes=[("XTf", g, c)])
                    p.dma("sp", self.XTb[c, :, g * 512:(g + 1) * 512], sb_[gi][:, c, :],
                          reads=[("pi_sb", gi, b4, c // 4) for b4 in range(4)], writes=[("XTb", g, c)])
            p.barrier()

    def phase_out(self, y):
        p, nc = self.p, self.nc
        with ExitStack() as st:
            sf = [p.sb(f"po_sf{i}", [128, KC, 512], F32, st) for i in range(2)]
            yo = [p.sb(f"po_y{i}", [128, D], F32, st) for i in range(2)]
            pt = [p.ps(f"po_pt{i}", [128, 512], F32, st) for i in range(4)]
            n = 0
            for g in range(self.ttot // 512):
                gi = g % 2
                for c in range(KC):
                    p.dma("sp", sf[gi][:, c, :], self.XTf[c, :, g * 512:(g + 1) * 512],
                          reads=[("XTf", g)], writes=[("po_sf", gi, c)])
                for b4 in range(4):
                    blk = g * 4 + b4
                    yi = blk % 2
                    for c4 in range(4):
                        ps = pt[n % 4]
                        pk = ("P_po", n % 4)
                        n += 1
                        for cc in range(4):
                            c = c4 * 4 + cc
                            p.op("pe", lambda ps=ps, cc=cc, c=c, gi=gi, b4=b4: nc.tensor.transpose(
                                ps[:, cc * 128:(cc + 1) * 128], sf[gi][:, c, b4 * 128:(b4 + 1) * 128], self.ident_f[:]),
                                reads=[("po_sf", gi, c), "ident_f"], writes=[pk])
                        eng = "dve" if c4 % 2 == 0 else "act"
                        if eng == "dve":
                            p.op("dve", lambda ps=ps, c4=c4, yi=yi: nc.vector.tensor_copy(
                                yo[yi][:, c4 * 512:(c4 + 1) * 512], ps[:]), reads=[pk], writes=[("po_y", yi, c4)])
                        else:
                            p.op("act", lambda ps=ps, c4=c4, yi=yi: nc.scalar.copy(
                                yo[yi][:, c4 * 512:(c4 + 1) * 512], ps[:]), reads=[pk], writes=[("po_y", yi, c4)])
                    p.dma("sp", y[blk * 128:(blk + 1) * 128, :], yo[yi][:],
                          reads=[("po_y", yi, c4) for c4 in range(4)])
            p.barrier()

    class _Epi:
        pass

    def epi_alloc(self, st, pre):
        p, nc = self.p, self.nc
        e = Builder._Epi()
        e.pre = pre
        e.rs = [p.sb(f"{pre}_rs{i}", [128, TT], F32, st) for i in range(2)]
        e.yT = p.sb(f"{pre}_y", [128, KC, TT], F32, st)
        e.ysq = [p.sb(f"{pre}_ysq{i}", [128, TT], F32, st) for i in range(2)]
        e.var = p.sb(f"{pre}_var", [128, TT], F32, st)
        e.rstd = p.sb(f"{pre}_rstd", [128, TT], F32, st)
        e.t1 = [p.sb(f"{pre}_t1{i}", [128, TT], F32, st) for i in range(2)]
        e.of = [p.sb(f"{pre}_of{i}", [128, TT], F32, st) for i in range(2)]
        e.ob = [p.sb(f"{pre}_ob{i}", [128, TT], BF16, st) for i in range(2)]
        e.epsb = p.sb(f"{pre}_eps", [128, 1], F32, st)
        e.S = [p.ps(f"{pre}_S{i}", [128, TT], F32, st) for i in range(2)]
        e.n = 0
        p.op("dve", lambda: nc.vector.memset(e.epsb[:], LN_EPS / (DN_ALPHA ** 2)), writes=[pre + "_eps"])
        return e

    def epi_load(self, e, c, t):
        p = self.p
        ci = e.n % 2
        tok = slice(t * TT, (t + 1) * TT)
        p.dma("sp", e.rs[ci][:], self.XTf[c, :, tok], reads=[("XTf", t)], writes=[(e.pre + "_rs", ci)])

    def epi_chunk(self, e, c, Yacc, Ykey, scale):
        p, nc = self.p, self.nc
        pre = e.pre
        ci = e.n % 2
        e.n += 1
        S = e.S
        p.op("dve", lambda: nc.vector.scalar_tensor_tensor(
            out=e.yT[:, c, :], in0=Yacc[:], scalar=scale, in1=e.rs[ci][:], op0=ALU.mult, op1=ALU.add),
            reads=[Ykey, (pre + "_rs", ci)], writes=[(pre + "_y", c)])
        p.op("act", lambda: nc.scalar.activation(out=e.ysq[ci][:], in_=e.yT[:, c, :], func=AF.Square),
             reads=[(pre + "_y", c)], writes=[(pre + "_ysq", ci)])
        p.op("pe", lambda: nc.tensor.matmul(S[0][:], lhsT=self.ones_mean[:], rhs=e.yT[:, c, :],
                                            start=(c == 0), stop=(c == KC - 1)),
             reads=["ones_mean", (pre + "_y", c)], writes=[("P_" + pre + "S", 0)])
        p.op("pe", lambda: nc.tensor.matmul(S[1][:], lhsT=self.ones_mean[:], rhs=e.ysq[ci][:],
                                            start=(c == 0), stop=(c == KC - 1)),
             reads=["ones_mean", (pre + "_ysq", ci)], writes=[("P_" + pre + "S", 1)])

    def epi_finalize(self, e, li, t):
        for piece in self.epi_finalize_pieces(e, li, t):
            piece()

    def epi_finalize_pieces(self, e, li, t):
        pieces = [lambda: self._epi_fin_head(e)]
        for c in range(KC):
            pieces.append(lambda c=c: self._epi_fin_chunk(e, li, t, c))
        return pieces

    def _epi_fin_head(self, e):
        p, nc = self.p, self.nc
        pre = e.pre
        S, var, rstd, epsb = e.S, e.var, e.rstd, e.epsb
        p.op("act", lambda: nc.scalar.activation(out=var[:], in_=S[0][:], func=AF.Square),
             reads=[("P_" + pre + "S", 0)], writes=[pre + "_var"])
        p.op("dve", lambda: nc.vector.tensor_tensor(out=var[:], in0=S[1][:], in1=var[:], op=ALU.subtract),
             reads=[("P_" + pre + "S", 1), pre + "_var"], writes=[pre + "_var"])
        p.op("act", lambda: nc.scalar.activation(out=var[:], in_=var[:], func=AF.Sqrt, bias=epsb[:], scale=1.0),
             reads=[pre + "_var", pre + "_eps"], writes=[pre + "_var"])
        p.op("dve", lambda: nc.vector.reciprocal(out=rstd[:], in_=var[:]),
             reads=[pre + "_var"], writes=[pre + "_rstd"])

    def _epi_fin_chunk(self, e, li, t, c):
        p, nc = self.p, self.nc
        pre = e.pre
        S, yT, var, rstd, t1, of, ob, epsb = e.S, e.yT, e.var, e.rstd, e.t1, e.of, e.ob, e.epsb
        tok = slice(t * TT, (t + 1) * TT)
        if True:
            ci = c % 2
            p.op("dve", lambda c=c, ci=ci: nc.vector.tensor_tensor(out=t1[ci][:], in0=yT[:, c, :], in1=S[0][:], op=ALU.subtract),
                 reads=[(pre + "_y", c), ("P_" + pre + "S", 0)], writes=[(pre + "_t1", ci)])
            p.op("dve", lambda ci=ci: nc.vector.tensor_tensor(out=t1[ci][:], in0=t1[ci][:], in1=rstd[:], op=ALU.mult),
                 reads=[(pre + "_t1", ci), pre + "_rstd"], writes=[(pre + "_t1", ci)])
            col = li * KC + c
            p.op("act", lambda c=c, ci=ci, col=col: nc.scalar.activation(
                out=of[ci][:], in_=t1[ci][:], func=AF.Identity, scale=self.lng[:, col:col + 1], bias=self.lnb[:, col:col + 1]),
                reads=[(pre + "_t1", ci), "lng", "lnb"], writes=[(pre + "_of", ci)])
            p.op("act", lambda c=c, ci=ci, col=col: nc.scalar.activation(
                out=ob[ci][:], in_=t1[ci][:], func=AF.Identity, scale=self.lng[:, col:col + 1], bias=self.lnb[:, col:col + 1]),
                reads=[(pre + "_t1", ci), "lng", "lnb"], writes=[(pre + "_ob", ci)])
            p.dma("sp", self.XTf[c, :, tok], of[ci][:], reads=[(pre + "_of", ci)], writes=[("XTf", t, c)])
            p.dma("sp", self.XTb[c, :, tok], ob[ci][:], reads=[(pre + "_ob", ci)], writes=[("XTb", t, c)])

    def phase_ffn(self, w_up_l, w_down_l, li):
        p, nc = self.p, self.nc
        with ExitStack() as st:
            xTb = [p.sb(f"ff_x{i}", [128, KC, TT], BF16, st) for i in range(2)]
            gT = p.sb("ff_g", [128, NJ, TT], BF16, st)
            wu = [[p.sb(f"ff_wu{i}{h}", [128, KC, 256], BF16, st) for h in range(2)] for i in range(2)]
            wd = [p.sb(f"ff_wd{i}", [128, NJ, 128], BF16, st) for i in range(2)]
            sa = [p.sb(f"ff_sa{i}", [128, TT], F32, st) for i in range(2)]
            e = self.epi_alloc(st, "ff")
            A = [p.ps(f"ff_A{i}", [128, TT], F32, st) for i in range(2)]
            B = [p.ps(f"ff_B{i}", [128, TT], F32, st) for i in range(2)]
            Y = [p.ps(f"ff_Y{i}", [128, TT], F32, st) for i in range(2)]
            ntile = self.ttot // TT
            nwu = 0
            nwd = 0
            nj = 0
            pending = []

            def load_x(t):
                for c in range(KC):
                    p.dma("sp", xTb[t % 2][:, c, :], self.XTb[c, :, t * TT:(t + 1) * TT],
                          reads=[("XTb", t)], writes=[("ff_x", t % 2)])

            load_x(0)
            for t in range(ntile):
                ti = t % 2
                tok = slice(t * TT, (t + 1) * TT)
                for g in range(NJ // 2):
                    for _ in range(2):
                        if pending:
                            pending.pop(0)()
                    wi = nwu % 2
                    nwu += 1
                    for h in range(2):
                        p.dma("pool", wu[wi][h][:], w_up_l[g, h], writes=[("ff_wu", wi, h)], max_dma_last_dim=8192)
                    for jj in range(2):
                        j = 2 * g + jj
                        ji = nj % 2
                        nj += 1
                        for h, acc, ak in ((0, A[ji], ("P_ffA", ji)), (1, B[ji], ("P_ffB", ji))):
                            for kc in range(KC):
                                p.op("pe", lambda acc=acc, h=h, kc=kc, jj=jj, wi=wi, ti=ti: nc.tensor.matmul(
                                    acc[:], lhsT=wu[wi][h][:, kc, jj * 128:(jj + 1) * 128], rhs=xTb[ti][:, kc, :],
                                    start=(kc == 0), stop=(kc == KC - 1)),
                                    reads=[("ff_wu", wi, h), ("ff_x", ti)], writes=[ak])
                        p.op("act", lambda ji=ji: nc.scalar.activation(out=sa[ji][:], in_=A[ji][:], func=AF.Silu),
                             reads=[("P_ffA", ji)], writes=[("ff_sa", ji)])
                        p.op("dve", lambda ji=ji, j=j: nc.vector.tensor_tensor(
                            out=gT[:, j, :], in0=B[ji][:], in1=sa[ji][:], op=ALU.mult),
                            reads=[("P_ffB", ji), ("ff_sa", ji)], writes=[("ff_g", j)])
                if t + 1 < ntile:
                    load_x(t + 1)
                for c in range(KC):
                    wi = nwd % 2
                    nwd += 1
                    yi = c % 2
                    p.dma("pool", wd[wi][:], w_down_l[c], writes=[("ff_wd", wi)], max_dma_last_dim=8192)
                    self.epi_load(e, c, t)
                    for j in range(NJ):
                        p.op("pe", lambda yi=yi, wi=wi, j=j: nc.tensor.matmul(
                            Y[yi][:], lhsT=wd[wi][:, j, :], rhs=gT[:, j, :], start=(j == 0), stop=(j == NJ - 1)),
                            reads=[("ff_wd", wi), ("ff_g", j)], writes=[("P_ffY", yi)])
                    self.epi_chunk(e, c, Y[yi], ("P_ffY", yi), 0.5 / DN_ALPHA)
                while pending:
                    pending.pop(0)()
                pending = self.epi_finalize_pieces(e, li, t)
            while pending:
                pending.pop(0)()
            p.barrier()

    def phase_merge(self, w_in_l, w_br_l, w_out_l, li):
        p, nc = self.p, self.nc
        YB = [(0, 4), (4, 2), (6, 4), (10, 4)]
        with ExitStack() as st:
            xTbs = [p.sb(f"mg_x{i}", [128, KC, TT], BF16, st) for i in range(2)]
            yTbs = [p.sb(f"mg_yb{i}", [128, NYC, TT], BF16, st) for i in range(2)]
            mT = p.sb("mg_m", [128, KC, TT], BF16, st)
            wg = [p.sb(f"mg_wg{i}", [128, KC, 512], BF16, st) for i in range(2)]
            wb = [p.sb(f"mg_wb{i}", [128, NYC, 128], BF16, st) for i in range(2)]
            wo = [p.sb(f"mg_wo{i}", [128, KC, 128], BF16, st) for i in range(2)]
            sg = [p.sb(f"mg_sg{i}", [128, TT], F32, st) for i in range(2)]
            macc = [p.sb(f"mg_ma{i}", [128, TT], F32, st) for i in range(2)]
            e = self.epi_alloc(st, "mg")
            G = [p.ps(f"mg_G{i}", [128, TT], F32, st) for i in range(2)]
            Pb = [p.ps(f"mg_P{i}", [128, TT], F32, st) for i in range(2)]
            Y = [p.ps(f"mg_Y{i}", [128, TT], F32, st) for i in range(2)]
            ntile = self.ttot // TT
            nw = 0
            nb = 0
            nwo = 0
            pending = []

            def load_xy(t):
                tk = slice(t * TT, (t + 1) * TT)
                for c in range(KC):
                    p.dma("sp", xTbs[t % 2][:, c, :], self.XTb[c, :, tk], reads=[("XTb", t)], writes=[("mg_x", t % 2)])
                for c in range(NYC):
                    p.dma("sp", yTbs[t % 2][:, c, :], self.YT[c, :, tk], reads=[("YT", t)], writes=[("mg_yb", t % 2)])

            load_xy(0)
            for t in range(ntile):
                tok = slice(t * TT, (t + 1) * TT)
                xTb, yTb = xTbs[t % 2], yTbs[t % 2]
                xk, yk = ("mg_x", t % 2), ("mg_yb", t % 2)
                for c in range(KC):
                    for _ in range(2):
                        if pending:
                            pending.pop(0)()
                    wi = nw % 2
                    nw += 1
                    o, n = WIN_OFF[f"gate{c}"]
                    p.dma("pool", wg[wi][:], w_in_l[:, o:o + KC * n].rearrange("p (k c) -> p k c", k=KC),
                          writes=[("mg_wg", wi)], max_dma_last_dim=8192)
                    p.dma("pool", wb[wi][:], w_br_l[c], writes=[("mg_wb", wi)], max_dma_last_dim=8192)
                    mi = c % 2
                    for b in range(4):
                        bi = nb % 2
                        nb += 1
                        for kc in range(KC):
                            p.op("pe", lambda bi=bi, wi=wi, kc=kc, b=b: nc.tensor.matmul(
                                G[bi][:], lhsT=wg[wi][:, kc, b * 128:(b + 1) * 128], rhs=xTb[:, kc, :],
                                start=(kc == 0), stop=(kc == KC - 1)),
                                reads=[("mg_wg", wi), xk], writes=[("P_mgG", bi)])
                        y0, ny = YB[b]
                        for k in range(ny):
                            p.op("pe", lambda bi=bi, wi=wi, k=k, y0=y0, ny=ny: nc.tensor.matmul(
                                Pb[bi][:], lhsT=wb[wi][:, y0 + k, :], rhs=yTb[:, y0 + k, :],
                                start=(k == 0), stop=(k == ny - 1)),
                                reads=[("mg_wb", wi), yk], writes=[("P_mgP", bi)])
                        p.op("act", lambda bi=bi: nc.scalar.activation(out=sg[bi][:], in_=G[bi][:], func=AF.Sigmoid),
                             reads=[("P_mgG", bi)], writes=[("mg_sg", bi)])
                        if b == 0:
                            p.op("dve", lambda bi=bi, mi=mi: nc.vector.tensor_tensor(
                                out=macc[mi][:], in0=Pb[bi][:], in1=sg[bi][:], op=ALU.mult),
                                reads=[("P_mgP", bi), ("mg_sg", bi)], writes=[("mg_ma", mi)])
                        else:
                            p.op("dve", lambda bi=bi: nc.vector.tensor_tensor(
                                out=sg[bi][:], in0=Pb[bi][:], in1=sg[bi][:], op=ALU.mult),
                                reads=[("P_mgP", bi), ("mg_sg", bi)], writes=[("mg_sg", bi)])
                            if b < 3:
                                p.op("dve", lambda bi=bi, mi=mi: nc.vector.tensor_tensor(
                                    out=macc[mi][:], in0=macc[mi][:], in1=sg[bi][:], op=ALU.add),
                                    reads=[("mg_ma", mi), ("mg_sg", bi)], writes=[("mg_ma", mi)])
                            else:
                                p.op("dve", lambda bi=bi, mi=mi, c=c: nc.vector.tensor_tensor(
                                    out=mT[:, c, :], in0=macc[mi][:], in1=sg[bi][:], op=ALU.add),
                                    reads=[("mg_ma", mi), ("mg_sg", bi)], writes=[("mg_m", c)])
                while pending:
                    pending.pop(0)()
                if t + 1 < ntile:
                    load_xy(t + 1)
                for c in range(KC):
                    wi = nwo % 2
                    nwo += 1
                    yi = c % 2
                    p.dma("pool", wo[wi][:], w_out_l[c], writes=[("mg_wo", wi)], max_dma_last_dim=8192)
                    self.epi_load(e, c, t)
                    for k in range(KC):
                        p.op("pe", lambda yi=yi, wi=wi, k=k: nc.tensor.matmul(
                            Y[yi][:], lhsT=wo[wi][:, k, :], rhs=mT[:, k, :], start=(k == 0), stop=(k == KC - 1)),
                            reads=[("mg_wo", wi), ("mg_m", k)], writes=[("P_mgY", yi)])
                    self.epi_chunk(e, c, Y[yi], ("P_mgY", yi), 1.0 / DN_ALPHA)
                pending = self.epi_finalize_pieces(e, li, t)
            while pending:
                pending.pop(0)()
            p.barrier()

    def load_mixer_consts(self):
        p, nc = self.p, self.nc
        self.c_sbmask = p.sb("c_sbmask", [128, 512], F32)
        self.c_triu = p.sb("c_triu", [128, 128], BF16)
        self.c_ones = p.sb("c_ones", [128, 128], BF16)
        self.ident_b = p.sb("ident_b", [128, 128], BF16)
        p.dma("sp", self.c_sbmask[:], self.inp("cm_sbmask", [128, 512]), writes=["c_sbmask"])
        p.dma("pool", self.c_triu[:], self.inp("cm_triu", [128, 128]), writes=["c_triu"])
        p.dma("pool", self.ident_b[:], self.dram_in["cm_ident"], writes=["ident_b"])
        p.op("pool", lambda: nc.gpsimd.memset(self.c_ones[:], 1.0), writes=["c_ones"])
        self.c_zeros = p.sb("c_zeros", [128, 128], BF16)
        p.op("pool", lambda: nc.gpsimd.memset(self.c_zeros[:], 0.0), writes=["c_zeros"])
        self.c_sbneg = p.sb("c_sbneg", [128, 512], BF16)
        p.op("pool", lambda: nc.gpsimd.tensor_scalar(out=self.c_sbneg[:], in0=self.c_sbmask[:], scalar1=-30000.0, scalar2=30000.0,
                                                     op0=ALU.mult, op1=ALU.add), reads=["c_sbmask"], writes=["c_sbneg"])

    def load_xT(self, xT, s, key):
        p = self.p
        for c in range(KC):
            for hh in range(2):
                p.dma("sp", xT[:, c, hh * 1024:(hh + 1) * 1024],
                      self.XTb[c, :, s * SEQ + hh * 1024: s * SEQ + (hh + 1) * 1024],
                      reads=[("XTb", (s * SEQ + hh * 1024) // TT + i) for i in range(2)], writes=[(key, c, hh)])

    def load_w(self, w_in_l, name, dst, key):
        o, n = WIN_OFF[name]
        self.p.dma("pool", dst[:, :, 0:n], w_in_l[:, o:o + KC * n].rearrange("p (k c) -> p k c", k=KC),
                   writes=[key], max_dma_last_dim=8192)

    def proj_fm(self, W, wkey, col0, ncols, xT, xkey, t0, nt, acc, acckey):
        p, nc = self.p, self.nc
        for kc in range(KC):
            p.op("pe", lambda kc=kc: nc.tensor.matmul(
                acc[0:ncols, 0:nt], lhsT=W[:, kc, col0:col0 + ncols], rhs=xT[:, kc, t0:t0 + nt],
                start=(kc == 0), stop=(kc == KC - 1)),
                reads=[wkey] + [(xkey, kc, hh) for hh in range(t0 // 1024, (t0 + nt - 1) // 1024 + 1)], writes=[acckey])

    def proj_tm(self, W, wkey, col0, ncols, xT, xkey, tok_ap_fn, acc, acckey):
        p, nc = self.p, self.nc
        for kc in range(KC):
            p.op("pe", lambda kc=kc: nc.tensor.matmul(
                acc[:, 0:ncols], lhsT=tok_ap_fn(kc), rhs=W[:, kc, col0:col0 + ncols],
                start=(kc == 0), stop=(kc == KC - 1)), reads=[wkey, (xkey, kc, 0), (xkey, kc, 1)], writes=[acckey])

    @staticmethod
    def pipeline(tiles, stages, skew=1, order=None, lags=None):
        n = len(tiles)
        ns = len(stages)
        lags = lags or [k * skew for k in range(ns)]
        for step in range(n + max(lags)):
            for k in (order or range(ns)):
                i = step - lags[k]
                if 0 <= i < n:
                    stages[k](i, tiles[i])

    def phase_sb(self, w_in_l, s):
        p, nc = self.p, self.nc
        with ExitStack() as st:
            qT = p.sb("sb_q", [128, 4, SEQ], BF16, st)
            kT = p.sb("sb_k", [128, 4, SEQ], BF16, st)
            V = p.sb("sb_v", [128, 16, 512], BF16, st)
            with ExitStack() as st2:
                xT = p.sb("sb_x", [128, KC, SEQ], BF16, st2)
                W = [p.sb(f"sb_w{i}", [128, KC, 512], BF16, st2) for i in range(2)]
                acc = [p.ps(f"sb_acc{i}", [128, 512], F32, st2) for i in range(4)]
                self.load_xT(xT, s, "sb_x")
                self.load_w(w_in_l, "sb_q", W[0], ("sb_w", 0))
                self.load_w(w_in_l, "sb_k", W[1], ("sb_w", 1))
                n = 0
                for wi, dst, sc, dk in ((0, qT, 0.125, "sb_q"), (1, kT, 1.0, "sb_k")):
                    for hp in range(4):
                        for tb in range(4):
                            a = acc[n % 4]
                            ak = ("P_sbacc", n % 4)
                            n += 1
                            self.proj_fm(W[wi], ("sb_w", wi), hp * 128, 128, xT, "sb_x", tb * 512, 512, a, ak)
                            if n % 2 == 0:
                                p.op("act", lambda a=a, dst=dst, hp=hp, tb=tb, sc=sc: nc.scalar.activation(
                                    out=dst[:, hp, tb * 512:(tb + 1) * 512], in_=a[:], func=AF.Copy, scale=sc),
                                    reads=[ak], writes=[(dk, hp)])
                            else:
                                p.op("dve", lambda a=a, dst=dst, hp=hp, tb=tb, sc=sc: nc.vector.tensor_scalar(
                                    out=dst[:, hp, tb * 512:(tb + 1) * 512], in0=a[:], scalar1=sc, scalar2=None, op0=ALU.mult),
                                    reads=[ak], writes=[(dk, hp)])
                self.load_w(w_in_l, "sb_v", W[0], ("sb_w", 0))
                for tkb in range(16):
                    a = acc[n % 4]
                    ak = ("P_sbacc", n % 4)
                    n += 1
                    self.proj_tm(W[0], ("sb_w", 0), 0, 512, xT, "sb_x",
                                 lambda kc, tkb=tkb: xT[:, kc, tkb * 128:(tkb + 1) * 128], a, ak)
                    if n % 2 == 0:
                        p.op("act", lambda a=a, tkb=tkb: nc.scalar.copy(out=V[:, tkb, :], in_=a[:]),
                             reads=[ak], writes=[("sb_v", tkb)])
                    else:
                        p.op("dve", lambda a=a, tkb=tkb: nc.vector.tensor_copy(out=V[:, tkb, :], in_=a[:]),
                             reads=[ak], writes=[("sb_v", tkb)])
                p.barrier()
            NB = 8
            E = [p.sb(f"sb_E{i}", [128, 512], F32, st) for i in range(NB)]
            L = [p.sb(f"sb_L{i}", [128, 512], F32, st) for i in range(NB)]
            Lh = [p.sb(f"sb_Lh{i}", [128, 512], BF16, st) for i in range(NB)]
            Wt = [p.sb(f"sb_W{i}", [128, 512], F32, st) for i in range(NB)]
            Ab = [p.sb(f"sb_Ab{i}", [128, 512], BF16, st) for i in range(NB)]
            osb = [p.sb(f"sb_o{i}", [128, 512], BF16, st) for i in range(2)]
            Z = [p.ps(f"sb_Z{i}", [128, 512], F32, st) for i in range(2)]
            SU = [p.ps(f"sb_SU{i}", [128, 512], F32, st) for i in range(2)]
            CS = [p.ps(f"sb_CS{i}", [128, 512], F32, st) for i in range(2)]
            O = [p.ps(f"sb_O{i}", [128, 512], F32, st) for i in range(2)]
            tiles = []
            for h in range(4):
                for qc in range(4):
                    kbs = list(range(4 * qc + 3, -1, -1))
                    for kb in kbs:
                        for slot, hh in ((0, h), (1, h + 4)):
                            tiles.append((hh, qc, kb, slot, kb == kbs[0], kb == 0))

            def geom(tl):
                h, qc, kb, slot, first, last = tl
                d = kb - 4 * qc
                c0 = max(d, 0) * 128
                return h, qc, kb, slot, first, last, d, slice(c0, 512), 512 - c0, h // 2, (h % 2) * 64

            def s0(i, tl):
                h, qc, kb, slot, first, last, d, cs, N, hp, po = geom(tl)
                zb = i % 2
                c0 = cs.start
                p.op("pe", lambda: nc.tensor.matmul(
                    Z[zb][:, cs], lhsT=kT[po:po + 64, hp, kb * 128:(kb + 1) * 128],
                    rhs=qT[po:po + 64, hp, qc * 512 + c0:(qc + 1) * 512], start=True, stop=True),
                    reads=[("sb_q", hp), ("sb_k", hp)], writes=[("P_sbZ", zb)])

            def s1(i, tl):
                h, qc, kb, slot, first, last, d, cs, N, hp, po = geom(tl)
                b, zb = i % NB, i % 2
                p.op("act", lambda: nc.scalar.activation(out=E[b][:, cs], in_=Z[zb][:, cs], func=AF.Exp),
                     reads=[("P_sbZ", zb)], writes=[("sb_E", b)])
                p.op("act", lambda: nc.scalar.activation(out=L[b][:, cs], in_=E[b][:, cs], func=AF.Ln, bias=1.0, scale=1.0),
                     reads=[("sb_E", b)], writes=[("sb_L", b)])
                if d >= 0:
                    p.op("pool", lambda: nc.gpsimd.tensor_tensor(out=Lh[b][:, cs], in0=L[b][:, cs],
                                                                 in1=self.c_sbmask[:, 0:N], op=ALU.mult),
                         reads=[("sb_L", b), "c_sbmask"], writes=[("sb_Lh", b)])
                else:
                    p.op("pool", lambda: nc.gpsimd.tensor_copy(out=Lh[b][:, cs], in_=L[b][:, cs]),
                         reads=[("sb_L", b)], writes=[("sb_Lh", b)])
                p.op("dve", lambda: nc.vector.tensor_tensor(out=Wt[b][:, cs], in0=Z[zb][:, cs], in1=L[b][:, cs], op=ALU.subtract),
                     reads=[("P_sbZ", zb), ("sb_L", b)], writes=[("sb_W", b)])

            def s2(i, tl):
                h, qc, kb, slot, first, last, d, cs, N, hp, po = geom(tl)
                b, zb = i % NB, i % 2
                p.op("pe", lambda: nc.tensor.matmul(SU[zb][:, cs], lhsT=self.c_triu[:], rhs=Lh[b][:, cs], start=True, stop=(d < 0)),
                     reads=["c_triu", ("sb_Lh", b)], writes=[("P_sbSU", zb)])
                if d >= 0:
                    p.op("pe", lambda: nc.tensor.matmul(SU[zb][:, cs], lhsT=self.ident_b[:], rhs=self.c_sbneg[:, 0:N], start=False, stop=True),
                         reads=["ident_b", "c_sbneg"], writes=[("P_sbSU", zb)])
                p.op("dve", lambda: nc.vector.tensor_tensor(out=Wt[b][:, cs], in0=Wt[b][:, cs], in1=SU[zb][:, cs], op=ALU.subtract),
                     reads=[("sb_W", b), ("P_sbSU", zb)], writes=[("sb_W", b)])
                if first:
                    p.op("pe", lambda: nc.tensor.matmul(CS[slot][:, :], lhsT=self.c_zeros[:], rhs=self.c_sbneg[:, :], start=True, stop=False),
                         reads=["c_zeros", "c_sbneg"], writes=[("P_sbCS", slot)])
                else:
                    p.op("dve", lambda: nc.vector.tensor_tensor(out=Wt[b][:, cs], in0=Wt[b][:, cs], in1=CS[slot][:, cs], op=ALU.subtract),
                         reads=[("sb_W", b), ("P_sbCS", slot)], writes=[("sb_W", b)])

            def s3(i, tl):
                h, qc, kb, slot, first, last, d, cs, N, hp, po = geom(tl)
                b = i % NB
                if not last:
                    p.op("pe", lambda: nc.tensor.matmul(CS[slot][:, cs], lhsT=self.c_ones[:], rhs=Lh[b][:, cs], start=False, stop=False),
                         reads=["c_ones", ("sb_Lh", b)], writes=[("P_sbCS", slot)])
                p.op("act", lambda: nc.scalar.activation(out=Ab[b][:, cs], in_=Wt[b][:, cs], func=AF.Exp),
                     reads=[("sb_W", b)], writes=[("sb_Ab", b)])
                p.op("pe", lambda: nc.tensor.matmul(O[slot][0:64, cs], lhsT=V[:, kb, h * 64:(h + 1) * 64], rhs=Ab[b][:, cs],
                                                    start=first, stop=last),
                     reads=[("sb_v", kb), ("sb_Ab", b)], writes=[("P_sbO", slot)])
                if last:
                    p.op("act", lambda: nc.scalar.copy(out=osb[slot][0:64, :], in_=O[slot][0:64, :]),
                         reads=[("P_sbO", slot)], writes=[("sb_o", slot)])
                    tok0 = s * SEQ + qc * 512
                    p.dma("sp", self.YT[hp, po:po + 64, tok0:tok0 + 512], osb[slot][0:64, :],
                          reads=[("sb_o", slot)], writes=[("YT", tok0 // TT)])

            self.pipeline(tiles, [s0, s1, s2, s3], order=(0, 3, 1, 2), lags=(0, 1, 3, 5))
            p.barrier()

    def phase_dil(self, w_in_l, dil_bias, s):
        p, nc = self.p, self.nc
        DILS = (1, 4, 16)
        with ExitStack() as st:
            qT = p.sb("dl_q", [128, 6, SEQ], BF16, st)
            kT = p.sb("dl_k", [128, 6, SEQ], BF16, st)
            V = p.sb("dl_v", [128, 48, 256], BF16, st)
            EB = p.sb("dl_EB", [128, 24 * 128], F32, st)
            p.dma("sp", EB[:], dil_bias, writes=["dl_EB"])
            p.op("act", lambda: nc.scalar.activation(out=EB[:], in_=EB[:], func=AF.Exp), reads=["dl_EB"], writes=["dl_EB"])
            with ExitStack() as st2:
                xT = p.sb("dl_x", [128, KC, SEQ], BF16, st2)
                W = [p.sb(f"dl_w{i}", [128, KC, 256], BF16, st2) for i in range(3)]
                acc = [p.ps(f"dl_acc{i}", [128, 512], F32, st2) for i in range(4)]
                self.load_xT(xT, s, "dl_x")
                n = 0
                for g, r in enumerate(DILS):
                    self.load_w(w_in_l, f"dil_q{g}", W[0], ("dl_w", 0))
                    self.load_w(w_in_l, f"dil_k{g}", W[1], ("dl_w", 1))
                    self.load_w(w_in_l, f"dil_v{g}", W[2], ("dl_w", 2))
                    for wi, dst, sc, dk in ((0, qT, 0.125, "dl_q"), (1, kT, 1.0, "dl_k")):
                        for hp in range(2):
                            for tb in range(4):
                                a = acc[n % 4]
                                ak = ("P_dlacc", n % 4)
                                n += 1
                                self.proj_fm(W[wi], ("dl_w", wi), hp * 128, 128, xT, "dl_x", tb * 512, 512, a, ak)
                                if n % 2 == 0:
                                    p.op("act", lambda a=a, dst=dst, hp=hp, tb=tb, sc=sc, g=g: nc.scalar.activation(
                                        out=dst[:, g * 2 + hp, tb * 512:(tb + 1) * 512], in_=a[:], func=AF.Copy, scale=sc),
                                        reads=[ak], writes=[(dk, g * 2 + hp)])
                                else:
                                    p.op("dve", lambda a=a, dst=dst, hp=hp, tb=tb, sc=sc, g=g: nc.vector.tensor_scalar(
                                        out=dst[:, g * 2 + hp, tb * 512:(tb + 1) * 512], in0=a[:], scalar1=sc, scalar2=None, op0=ALU.mult),
                                        reads=[ak], writes=[(dk, g * 2 + hp)])
                    nloc = 16 // r
                    for c in range(r):
                        for bl in range(nloc):
                            pi = c * nloc + bl
                            a = acc[n % 4]
                            ak = ("P_dlacc", n % 4)
                            n += 1
                            t0 = c + r * 128 * bl
                            self.proj_tm(W[2], ("dl_w", 2), 0, 256, xT, "dl_x",
                                         lambda kc, t0=t0, r=r: xT[:, kc, t0:t0 + 127 * r + 1:r], a, ak)
                            if n % 2 == 0:
                                p.op("act", lambda a=a, g=g, pi=pi: nc.scalar.copy(out=V[:, g * 16 + pi, :], in_=a[:, 0:256]),
                                     reads=[ak], writes=[("dl_v", g * 16 + pi)])
                            else:
                                p.op("dve", lambda a=a, g=g, pi=pi: nc.vector.tensor_copy(out=V[:, g * 16 + pi, :], in_=a[:, 0:256]),
                                     reads=[ak], writes=[("dl_v", g * 16 + pi)])
                p.barrier()
            Pe = [p.sb(f"dl_Pe{i}", [128, 256], F32, st) for i in range(2)]
            Pb = [p.sb(f"dl_Pb{i}", [128, 256], BF16, st) for i in range(2)]
            accU = p.sb("dl_aU", [64, SEQ], F32, st)
            accZ = p.sb("dl_aZ", [64, SEQ], F32, st)
            yb = p.sb("dl_yb", [64, SEQ], BF16, st)
            Sp = [p.ps(f"dl_S{i}", [128, 512], F32, st) for i in range(2)]
            Up = [p.ps(f"dl_U{i}", [128, 512], F32, st) for i in range(2)]
            Zp = [p.ps(f"dl_Z{i}", [128, 512], F32, st) for i in range(2)]
            for hh in range(4):
                po = (hh % 2) * 64
                tiles = []
                for g, r in enumerate(DILS):
                    nloc = 16 // r
                    for c in range(r):
                        for bl in range(nloc):
                            tiles.append((g, r, c, bl, nloc))

                def s1(i, tl):
                    g, r, c, bl, nloc = tl
                    b = i % 2
                    ch = g * 2 + hh // 2
                    tq = c + r * 128 * bl
                    nk = 2 if bl > 0 else 1
                    for kbi in range(nk):
                        tk = c + r * 128 * (bl - kbi)
                        p.op("pe", lambda kbi=kbi, tk=tk: nc.tensor.matmul(
                            Sp[b][:, kbi * 128:(kbi + 1) * 128], lhsT=kT[po:po + 64, ch, tk:tk + 127 * r + 1:r],
                            rhs=qT[po:po + 64, ch, tq:tq + 127 * r + 1:r], start=True, stop=True),
                            reads=[("dl_q", ch), ("dl_k", ch)], writes=[("P_dlS", b)])
                    p.op("act", lambda: nc.scalar.activation(out=Pe[b][:, 0:nk * 128], in_=Sp[b][:, 0:nk * 128], func=AF.Exp),
                         reads=[("P_dlS", b)], writes=[("dl_Pe", b)])
                    e0 = ((g * 4 + hh) * 2) * 128
                    p.op("dve", lambda: nc.vector.tensor_tensor(out=Pb[b][:, 0:nk * 128], in0=Pe[b][:, 0:nk * 128],
                                                                in1=EB[:, e0:e0 + nk * 128], op=ALU.mult),
                         reads=[("dl_Pe", b), "dl_EB"], writes=[("dl_Pb", b)])

                def s2(i, tl):
                    g, r, c, bl, nloc = tl
                    b = i % 2
                    tq = c + r * 128 * bl
                    nk = 2 if bl > 0 else 1
                    for kbi in range(nk):
                        vi = g * 16 + c * nloc + (bl - kbi)
                        p.op("pe", lambda kbi=kbi, vi=vi: nc.tensor.matmul(
                            Up[b][0:64, 0:128], lhsT=V[:, vi, hh * 64:(hh + 1) * 64], rhs=Pb[b][:, kbi * 128:(kbi + 1) * 128],
                            start=(kbi == 0), stop=(kbi == nk - 1)),
                            reads=[("dl_v", vi), ("dl_Pb", b)], writes=[("P_dlU", b)])
                    for kbi in range(nk):
                        p.op("pe", lambda kbi=kbi: nc.tensor.matmul(
                            Zp[b][0:64, 0:128], lhsT=self.c_ones[:, 0:64], rhs=Pb[b][:, kbi * 128:(kbi + 1) * 128],
                            start=(kbi == 0), stop=(kbi == nk - 1)),
                            reads=["c_ones", ("dl_Pb", b)], writes=[("P_dlZ", b)])
                    tsl = slice(tq, tq + 127 * r + 1, r)
                    if g == 0:
                        p.op("dve", lambda: nc.vector.tensor_copy(out=accU[:, tsl], in_=Up[b][0:64, 0:128]),
                             reads=[("P_dlU", b)], writes=["dl_aU"])
                        p.op("act", lambda: nc.scalar.copy(out=accZ[:, tsl], in_=Zp[b][0:64, 0:128]),
                             reads=[("P_dlZ", b)], writes=["dl_aZ"])
                    else:
                        p.op("dve", lambda: nc.vector.tensor_tensor(out=accU[:, tsl], in0=Up[b][0:64, 0:128], in1=accU[:, tsl], op=ALU.add),
                             reads=[("P_dlU", b), "dl_aU"], writes=["dl_aU"])
                        p.op("dve", lambda: nc.vector.tensor_tensor(out=accZ[:, tsl], in0=Zp[b][0:64, 0:128], in1=accZ[:, tsl], op=ALU.add),
                             reads=[("P_dlZ", b), "dl_aZ"], writes=["dl_aZ"])

                self.pipeline(tiles, [s1, s2])
                p.op("dve", lambda: nc.vector.reciprocal(out=accZ[:], in_=accZ[:]), reads=["dl_aZ"], writes=["dl_aZ"])
                p.op("dve", lambda: nc.vector.tensor_tensor(out=yb[:], in0=accU[:], in1=accZ[:], op=ALU.mult),
                     reads=["dl_aU", "dl_aZ"], writes=["dl_yb"])
                p.dma("sp", self.YT[4 + hh // 2, po:po + 64, s * SEQ:(s + 1) * SEQ], yb[:],
                      reads=["dl_yb"], writes=[("YT", (s * SEQ) // TT + i) for i in range(SEQ // TT)])
            p.barrier()

    def phase_dsa(self, w_in_l, dsa_bias, dsa_b31, c_negmask, s):
        p, nc = self.p, self.nc
        with ExitStack() as st:
            qT = p.sb("ds_q", [128, 4, SEQ], BF16, st)
            kT = p.sb("ds_k", [128, 4, SEQ], BF16, st)
            V = p.sb("ds_v", [128, 16, 512], BF16, st)
            qiT = p.sb("ds_qi", [128, 4, SEQ], BF16, st)
            kiT = p.sb("ds_ki", [128, SEQ], BF16, st)
            wabs = p.sb("ds_wabs", [128, 16, 8], F32, st)
            wsgn = p.sb("ds_wsgn", [128, 16, 8], F32, st)
            EB = p.sb("ds_EB", [128, 16 * 128], F32, st)
            b31 = p.sb("ds_b31", [128, 8], F32, st)
            nb31 = p.sb("ds_nb31", [128, 8], F32, st)
            negm = p.sb("ds_negm", [128, 128], F32, st)
            p.dma("sp", EB[:], dsa_bias, writes=["ds_EB"])
            p.dma("sp", b31[:], dsa_b31, writes=["ds_b31"])
            p.dma("sp", negm[:], c_negmask, writes=["ds_negm"])
            p.op("dve", lambda: nc.vector.tensor_scalar(out=nb31[:], in0=b31[:], scalar1=-1.0, scalar2=None, op0=ALU.mult),
                 reads=["ds_b31"], writes=["ds_nb31"])
            EBh = p.sb("ds_EBh", [128, 16 * 128], BF16, st)
            EBl = p.sb("ds_EBl", [128, 16 * 128], BF16, st)
            for h in range(8):
                hs = slice(h * 256, (h + 1) * 256)
                p.op("dve", lambda h=h, hs=hs: nc.vector.tensor_scalar(out=EB[:, hs], in0=EB[:, hs], scalar1=nb31[:, h:h + 1], scalar2=None, op0=ALU.add),
                     reads=["ds_EB", "ds_nb31"], writes=["ds_EB"])
            p.op("dve", lambda: nc.vector.tensor_copy(out=EBh[:], in_=EB[:]), reads=["ds_EB"], writes=["ds_EBh"])
            p.op("dve", lambda: nc.vector.tensor_tensor(out=EBl[:], in0=EB[:], in1=EBh[:], op=ALU.subtract),
                 reads=["ds_EB", "ds_EBh"], writes=["ds_EBl"])
            with ExitStack() as st2:
                xT = p.sb("ds_x", [128, KC, SEQ], BF16, st2)
                W = [p.sb(f"ds_w{i}", [128, KC, 512], BF16, st2) for i in range(2)]
                acc = [p.ps(f"ds_acc{i}", [128, 512], F32, st2) for i in range(4)]
                self.load_xT(xT, s, "ds_x")
                n = 0
                plan = (("dsa_q", qT, 0.125, "ds_q", 4), ("dsa_k", kT, 1.0, "ds_k", 4), ("idx_q", qiT, 1.0, "ds_qi", 4),
                        ("idx_k", kiT, 1.0, "ds_ki", 1))
                for pi, (wname, dst, sc, dk, nhp) in enumerate(plan):
                    wi = pi % 2
                    self.load_w(w_in_l, wname, W[wi], ("ds_w", wi))
                    for hp in range(nhp):
                        for tb in range(4):
                            a = acc[n % 4]
                            ak = ("P_dsacc", n % 4)
                            n += 1
                            self.proj_fm(W[wi], ("ds_w", wi), hp * 128, 128, xT, "ds_x", tb * 512, 512, a, ak)
                            o = dst[:, hp, tb * 512:(tb + 1) * 512] if nhp > 1 else dst[:, tb * 512:(tb + 1) * 512]
                            if n % 2 == 0:
                                p.op("act", lambda a=a, o=o, sc=sc: nc.scalar.activation(out=o, in_=a[:], func=AF.Copy, scale=sc),
                                     reads=[ak], writes=[(dk, hp)])
                            else:
                                p.op("dve", lambda a=a, o=o, sc=sc: nc.vector.tensor_scalar(
                                    out=o, in0=a[:], scalar1=sc, scalar2=None, op0=ALU.mult), reads=[ak], writes=[(dk, hp)])
                self.load_w(w_in_l, "dsa_v", W[0], ("ds_w", 0))
                self.load_w(w_in_l, "idx_w", W[1], ("ds_w", 1))
                for tkb in range(16):
                    a = acc[n % 4]
                    ak = ("P_dsacc", n % 4)
                    n += 1
                    self.proj_tm(W[0], ("ds_w", 0), 0, 512, xT, "ds_x",
                                 lambda kc, tkb=tkb: xT[:, kc, tkb * 128:(tkb + 1) * 128], a, ak)
                    p.op("act", lambda a=a, tkb=tkb: nc.scalar.copy(out=V[:, tkb, :], in_=a[:]),
                         reads=[ak], writes=[("ds_v", tkb)])
                    a = acc[n % 4]
                    ak = ("P_dsacc", n % 4)
                    n += 1
                    self.proj_tm(W[1], ("ds_w", 1), 0, 8, xT, "ds_x",
                                 lambda kc, tkb=tkb: xT[:, kc, tkb * 128:(tkb + 1) * 128], a, ak)
                    p.op("act", lambda a=a, tkb=tkb: nc.scalar.activation(out=wabs[:, tkb, :], in_=a[:, 0:8], func=AF.Abs),
                         reads=[ak], writes=["ds_wabs"])
                    p.op("act", lambda a=a, tkb=tkb: nc.scalar.activation(out=wsgn[:, tkb, :], in_=a[:, 0:8], func=AF.Sign),
                         reads=[ak], writes=["ds_wsgn"])
                p.barrier()
            I = [p.sb(f"ds_I{i}", [128, SEQ], F32, st) for i in range(2)]
            NIT = 24
            m8 = p.sb("ds_m8", [128, 8], F32, st)
            rmin = [p.sb(f"ds_rmin{i}", [128, 1], F32, st) for i in range(2)]
            blo = p.sb("ds_blo", [128, 1], F32, st)
            bmid = p.sb("ds_bmid", [128, 1], F32, st)
            bcnt = p.sb("ds_bcnt", [128, 1], F32, st)
            binc = p.sb("ds_binc", [128, 1], F32, st)
            bw = p.sb("ds_bw", [128, NIT], F32, st)
            pw2 = p.sb("ds_pw2", [128, NIT], F32, st)
            for k in range(NIT):
                p.op("pool", lambda k=k: nc.gpsimd.memset(pw2[:, k:k + 1], 0.5 ** (k + 1)), writes=["ds_pw2"])
            Mq = [p.sb(f"ds_Mq{i}", [128, SEQ], BF16, st) for i in range(2)]
            rl = [p.sb(f"ds_rl{i}", [128, 512], F32, st) for i in range(2)]
            Pb = [p.sb(f"ds_Pb{i}", [128, 512], BF16, st) for i in range(2)]
            rz = p.sb("ds_rz", [64, 128], F32, st)
            yfull = p.sb("ds_y", [64, 8, SEQ], BF16, st)
            DpI = [p.ps(f"ds_DI{i}", [128, 512], F32, st) for i in range(2)]
            DpA = [p.ps(f"ds_DA{i}", [128, 512], F32, st) for i in range(2)]
            Up = [p.ps(f"ds_U{i}", [128, 512], F32, st) for i in range(2)]
            Zp = [p.ps(f"ds_Z{i}", [128, 512], F32, st) for i in range(2)]
            cnt = {"d": 0, "t": 0, "a": 0, "u": 0}

            def idx_thunks(i):
                nk = 128 * (i + 1)
                qs = slice(i * 128, (i + 1) * 128)
                ib = i % 2
                Ii = I[ib]
                ik = ("ds_I", ib)
                th = []
                for h in range(8):
                    hp, po = h // 2, (h % 2) * 64
                    for k0 in range(0, nk, 512):
                        kn = min(512, nk - k0)

                        def f(h=h, hp=hp, po=po, k0=k0, kn=kn):
                            b = cnt["d"] % 2
                            cnt["d"] += 1
                            p.op("pe", lambda: nc.tensor.matmul(
                                DpI[b][:, 0:kn], lhsT=qiT[po:po + 64, hp, qs], rhs=kiT[po:po + 64, k0:k0 + kn], start=True, stop=True),
                                reads=[("ds_qi", hp), ("ds_ki", 0)], writes=[("P_dsDI", b)])
                            p.op("act", lambda: nc.scalar.activation(
                                out=rl[b][:, 0:kn], in_=DpI[b][:, 0:kn], func=AF.Relu, scale=wabs[:, i, h:h + 1]),
                                reads=[("P_dsDI", b), "ds_wabs"], writes=[("ds_rl", b)])
                            if h == 0:
                                p.op("dve", lambda: nc.vector.tensor_scalar(
                                    out=Ii[:, k0:k0 + kn], in0=rl[b][:, 0:kn], scalar1=wsgn[:, i, h:h + 1], scalar2=None, op0=ALU.mult),
                                    reads=[("ds_rl", b), "ds_wsgn"], writes=[ik])
                            else:
                                p.op("dve", lambda: nc.vector.scalar_tensor_tensor(
                                    out=Ii[:, k0:k0 + kn], in0=rl[b][:, 0:kn], scalar=wsgn[:, i, h:h + 1], in1=Ii[:, k0:k0 + kn],
                                    op0=ALU.mult, op1=ALU.add),
                                    reads=[("ds_rl", b), "ds_wsgn", ik], writes=[ik])
                        th.append(f)
                if i >= 2:
                    th.append(lambda: p.op("dve", lambda: nc.vector.tensor_reduce(
                        out=rmin[ib][:], in_=Ii[:, 0:nk], op=ALU.min, axis=AX.X), reads=[ik], writes=[("ds_rmin", ib)]))
                th.append(lambda: p.op("pool", lambda: nc.gpsimd.tensor_tensor(
                    out=Ii[:, i * 128:nk], in0=Ii[:, i * 128:nk], in1=negm[:], op=ALU.add),
                    reads=[ik, "ds_negm"], writes=[ik]))
                return th

            def topk_thunks(i):
                nk = 128 * (i + 1)
                ib = i % 2
                Ii, Mi = I[ib], Mq[ib]
                ik, mk = ("ds_I", ib), ("ds_Mq", ib)
                bk = "ds_bis"
                th = []
                if i >= 2:
                    def init():
                        p.op("dve", lambda: nc.vector.max(out=m8[:], in_=Ii[:, 0:nk]), reads=[ik], writes=["ds_m8"])
                        p.op("dve", lambda: nc.vector.tensor_copy(out=blo[:], in_=rmin[ib][:]), reads=[("ds_rmin", ib), bk], writes=[bk])
                        p.op("dve", lambda: nc.vector.tensor_tensor(out=binc[:], in0=m8[:, 0:1], in1=rmin[ib][:], op=ALU.subtract),
                             reads=["ds_m8", ("ds_rmin", ib), bk], writes=[bk])
                        p.op("dve", lambda: nc.vector.tensor_scalar(out=bw[:], in0=pw2[:], scalar1=binc[:, 0:1], scalar2=None, op0=ALU.mult),
                             reads=["ds_pw2", bk], writes=[bk])
                    th.append(init)
                    for k in range(NIT):
                        def f(k=k):
                            p.op("dve", lambda: nc.vector.tensor_tensor(out=bmid[:], in0=blo[:], in1=bw[:, k:k + 1], op=ALU.add),
                                 reads=[bk], writes=[bk])
                            p.op("dve", lambda: nc.vector.tensor_scalar(out=Mi[:, 0:nk], in0=Ii[:, 0:nk], scalar1=bmid[:, 0:1], scalar2=0.0,
                                                                        op0=ALU.is_ge, op1=ALU.add, accum_out=bcnt[:]),
                                 reads=[ik, bk], writes=[mk, bk])
                            p.op("dve", lambda: nc.vector.tensor_scalar(out=binc[:], in0=bcnt[:], scalar1=255.5, scalar2=bw[:, k:k + 1],
                                                                        op0=ALU.is_ge, op1=ALU.mult),
                                 reads=[bk], writes=[bk])
                            p.op("dve", lambda: nc.vector.tensor_tensor(out=blo[:], in0=blo[:], in1=binc[:], op=ALU.add),
                                 reads=[bk], writes=[bk])
                        th.append(f)
                    th.append(lambda: p.op("dve", lambda: nc.vector.tensor_scalar(
                        out=Mi[:, 0:nk], in0=Ii[:, 0:nk], scalar1=blo[:, 0:1], scalar2=-30000.0, op0=ALU.is_lt, op1=ALU.mult),
                        reads=[ik, bk], writes=[mk]))
                return th

            def att_thunks(i):
                qs = slice(i * 128, (i + 1) * 128)
                ib = i % 2
                Mi = Mq[ib]
                mk = ("ds_Mq", ib)
                th = []
                for h in range(8):
                    hp, po = h // 2, (h % 2) * 64
                    for k4 in range(0, i + 1, 4):
                        kn = min(4, i + 1 - k4)

                        def f(h=h, hp=hp, po=po, k4=k4, kn=kn):
                            b = cnt["a"] % 2
                            cnt["a"] += 1
                            if k4 == 0:
                                cnt["u"] += 1
                            ub = cnt["u"] % 2
                            for j in range(kn):
                                kb = k4 + j
                                kind = i - kb
                                osl = DpA[b][:, j * 128:(j + 1) * 128]
                                extra = (1 if i >= 2 else 0) + (2 if kind <= 1 else 0)
                                p.op("pe", lambda kb=kb, osl=osl, extra=extra: nc.tensor.matmul(
                                    osl, lhsT=kT[po:po + 64, hp, kb * 128:(kb + 1) * 128],
                                    rhs=qT[po:po + 64, hp, qs], start=True, stop=(extra == 0)),
                                    reads=[("ds_q", hp), ("ds_k", hp)], writes=[("P_dsDA", b)])
                                if i >= 2:
                                    extra -= 1
                                    p.op("pe", lambda kb=kb, osl=osl, extra=extra: nc.tensor.matmul(
                                        osl, lhsT=Mi[:, kb * 128:(kb + 1) * 128], rhs=self.ident_b[:], start=False, stop=(extra == 0)),
                                        reads=[mk, "ident_b"], writes=[("P_dsDA", b)], strict_pe=(po == 0))
                                if kind <= 1:
                                    e0 = (h * 2 + kind) * 128
                                    for EBx, nm in ((EBh, "ds_EBh"), (EBl, "ds_EBl")):
                                        extra -= 1
                                        p.op("pe", lambda osl=osl, EBx=EBx, e0=e0, extra=extra: nc.tensor.matmul(
                                            osl, lhsT=self.ident_b[:], rhs=EBx[:, e0:e0 + 128], start=False, stop=(extra == 0)),
                                            reads=[nm, "ident_b"], writes=[("P_dsDA", b)], strict_pe=(po == 0 and i < 2 and EBx is EBh))
                            p.op("act", lambda: nc.scalar.activation(
                                out=Pb[b][:, 0:kn * 128], in_=DpA[b][:, 0:kn * 128], func=AF.Exp, bias=b31[:, h:h + 1], scale=1.0),
                                reads=[("P_dsDA", b), "ds_b31"], writes=[("ds_Pb", b)])
                            for j in range(kn):
                                kb = k4 + j
                                p.op("pe", lambda j=j, kb=kb: nc.tensor.matmul(
                                    Up[ub][0:64, 0:128], lhsT=V[:, kb, h * 64:(h + 1) * 64], rhs=Pb[b][:, j * 128:(j + 1) * 128],
                                    start=(kb == 0), stop=(kb == i)),
                                    reads=[("ds_v", kb), ("ds_Pb", b)], writes=[("P_dsU", ub)])
                            for j in range(kn):
                                kb = k4 + j
                                p.op("pe", lambda j=j, kb=kb: nc.tensor.matmul(
                                    Zp[ub][0:64, 0:128], lhsT=self.c_ones[:, 0:64], rhs=Pb[b][:, j * 128:(j + 1) * 128],
                                    start=(kb == 0), stop=(kb == i)),
                                    reads=["c_ones", ("ds_Pb", b)], writes=[("P_dsZ", ub)])
                            if k4 + kn == i + 1:
                                p.op("dve", lambda: nc.vector.reciprocal(out=rz[:], in_=Zp[ub][0:64, 0:128]),
                                     reads=[("P_dsZ", ub)], writes=["ds_rz"])
                                p.op("dve", lambda: nc.vector.tensor_tensor(out=yfull[:, h, qs], in0=Up[ub][0:64, 0:128], in1=rz[:], op=ALU.mult),
                                     reads=[("P_dsU", ub), "ds_rz"], writes=[("ds_y", h)])
                        th.append(f)
                return th

            def interleave(lists):
                lists = [l for l in lists if l]
                pos = [0] * len(lists)
                total = sum(len(l) for l in lists)
                for _ in range(total):
                    j = min((jj for jj in range(len(lists)) if pos[jj] < len(lists[jj])),
                            key=lambda jj: (pos[jj] + 0.5) / len(lists[jj]))
                    lists[j][pos[j]]()
                    pos[j] += 1

            for step in range(16 + 2):
                ls = []
                if step - 1 >= 0 and step - 1 < 16:
                    ls.append(topk_thunks(step - 1))
                if step - 2 >= 0:
                    ls.append(att_thunks(step - 2))
                if step < 16:
                    ls.append(idx_thunks(step))
                interleave(ls)
            for h in range(8):
                hp, po = h // 2, (h % 2) * 64
                p.dma("sp", self.YT[10 + hp, po:po + 64, s * SEQ:(s + 1) * SEQ], yfull[:, h, :],
                      reads=[("ds_y", h)], writes=[("YT", (s * SEQ) // TT + j) for j in range(SEQ // TT)])
            p.barrier()

    def ssm_discretize(self, st, pre, lr, li, ldt, shape):
        p, nc = self.p, self.nc
        P_, F_ = shape
        TWO_PI = 2.0 * math.pi
        cnt = [0]

        def T(dtype=F32):
            cnt[0] += 1
            return p.sb(f"{pre}_t{cnt[0]}", [P_, F_], dtype, st)

        key = pre + "_k"

        def dve(fn):
            p.op("dve", fn, reads=[key], writes=[key])

        def act(fn):
            p.op("act", fn, reads=[key], writes=[key])

        dt = T(); mag = T(); ang = T(); tmp = T(); ki = T(mybir.dt.int32); kf = T(); r = T(); msk = T()
        sn = T(); cs = T(); a_re = T(); a_im = T(); den = T(); f_re = T(); f_im = T(); am1 = T()
        act(lambda: nc.scalar.activation(out=dt[:], in_=ldt[:], func=AF.Exp))
        dve(lambda: nc.vector.tensor_tensor(out=tmp[:], in0=lr[:], in1=dt[:], op=ALU.mult))
        act(lambda: nc.scalar.activation(out=mag[:], in_=tmp[:], func=AF.Exp))
        dve(lambda: nc.vector.tensor_tensor(out=ang[:], in0=li[:], in1=dt[:], op=ALU.mult))
        for off, dst in ((0.0, sn), (0.5 * math.pi, cs)):
            dve(lambda off=off: nc.vector.tensor_scalar(out=tmp[:], in0=ang[:], scalar1=off, scalar2=1.0 / TWO_PI,
                                                        op0=ALU.add, op1=ALU.mult))
            dve(lambda: nc.vector.tensor_copy(out=ki[:], in_=tmp[:]))
            dve(lambda: nc.vector.tensor_copy(out=kf[:], in_=ki[:]))
            dve(lambda off=off: nc.vector.tensor_scalar(out=r[:], in0=ang[:], scalar1=off, scalar2=None, op0=ALU.add))
            dve(lambda: nc.vector.scalar_tensor_tensor(out=r[:], in0=kf[:], scalar=-TWO_PI, in1=r[:], op0=ALU.mult, op1=ALU.add))
            dve(lambda: nc.vector.tensor_scalar(out=msk[:], in0=r[:], scalar1=math.pi, scalar2=None, op0=ALU.is_gt))
            dve(lambda: nc.vector.scalar_tensor_tensor(out=r[:], in0=msk[:], scalar=-TWO_PI, in1=r[:], op0=ALU.mult, op1=ALU.add))
            dve(lambda: nc.vector.tensor_scalar(out=msk[:], in0=r[:], scalar1=-math.pi, scalar2=None, op0=ALU.is_lt))
            dve(lambda: nc.vector.scalar_tensor_tensor(out=r[:], in0=msk[:], scalar=TWO_PI, in1=r[:], op0=ALU.mult, op1=ALU.add))
            dve(lambda: nc.vector.tensor_scalar(out=r[:], in0=r[:], scalar1=math.pi, scalar2=-math.pi, op0=ALU.min, op1=ALU.max))
            act(lambda dst=dst: nc.scalar.activation(out=dst[:], in_=r[:], func=AF.Sin))
        dve(lambda: nc.vector.tensor_tensor(out=a_re[:], in0=mag[:], in1=cs[:], op=ALU.mult))
        dve(lambda: nc.vector.tensor_tensor(out=a_im[:], in0=mag[:], in1=sn[:], op=ALU.mult))
        dve(lambda: nc.vector.tensor_tensor(out=den[:], in0=lr[:], in1=lr[:], op=ALU.mult))
        dve(lambda: nc.vector.tensor_tensor(out=tmp[:], in0=li[:], in1=li[:], op=ALU.mult))
        dve(lambda: nc.vector.tensor_tensor(out=den[:], in0=den[:], in1=tmp[:], op=ALU.add))
        dve(lambda: nc.vector.reciprocal(out=den[:], in_=den[:]))
        dve(lambda: nc.vector.tensor_scalar(out=am1[:], in0=a_re[:], scalar1=-1.0, scalar2=None, op0=ALU.add))
        dve(lambda: nc.vector.tensor_tensor(out=f_re[:], in0=am1[:], in1=lr[:], op=ALU.mult))
        dve(lambda: nc.vector.tensor_tensor(out=tmp[:], in0=a_im[:], in1=li[:], op=ALU.mult))
        dve(lambda: nc.vector.tensor_tensor(out=f_re[:], in0=f_re[:], in1=tmp[:], op=ALU.add))
        dve(lambda: nc.vector.tensor_tensor(out=f_re[:], in0=f_re[:], in1=den[:], op=ALU.mult))
        dve(lambda: nc.vector.tensor_tensor(out=f_im[:], in0=a_im[:], in1=lr[:], op=ALU.mult))
        dve(lambda: nc.vector.tensor_tensor(out=tmp[:], in0=am1[:], in1=li[:], op=ALU.mult))
        dve(lambda: nc.vector.tensor_tensor(out=f_im[:], in0=f_im[:], in1=tmp[:], op=ALU.subtract))
        dve(lambda: nc.vector.tensor_tensor(out=f_im[:], in0=f_im[:], in1=den[:], op=ALU.mult))
        return a_re, a_im, f_re, f_im, key, (cs, sn, mag)

    def phase_ssm(self, w_in_l, sp, s):
        p, nc = self.p, self.nc
        NS = 11
        with ExitStack() as st:
            ufT = p.sb("sm_uf", [128, 6, SEQ], F32, st)
            ubT = p.sb("sm_ub", [128, 6, SEQ], BF16, st)
            BBre = p.sb("sm_BBre", [128, 6, 128], BF16, st)
            BBim = p.sb("sm_BBim", [128, 6, 128], BF16, st)
            Are = p.sb("sm_Are", [128, 16, NS], F32, st)
            Aim = p.sb("sm_Aim", [128, 16, NS], F32, st)
            nAim = p.sb("sm_nAim", [128, 16, NS], F32, st)
            rho = p.sb("sm_rho", [128, 16], F32, st)
            Cre = p.sb("sm_Cre", [128, 16, 32], BF16, st)
            nCim = p.sb("sm_nCim", [128, 16, 32], BF16, st)
            dsk = p.sb("sm_d", [128, 6], F32, st)
            p.dma("sp", dsk[:], sp["d"], writes=["sm_d"])
            p.dma("pool", Cre[:], sp["c_re"], writes=["sm_Cre"])
            with ExitStack() as st1:
                def ld(name, shape, src):
                    t = p.sb(name, shape, F32, st1)
                    p.dma("sp", t[:], src, writes=[name])
                    return t
                lrB = ld("sm_lrB", [128, 768], sp["lamB_re"]); liB = ld("sm_liB", [128, 768], sp["lamB_im"])
                ldB = ld("sm_ldB", [128, 768], sp["lamB_dt"])
                bre = ld("sm_bre", [128, 768], sp["b_re"]); bim = ld("sm_bim", [128, 768], sp["b_im"])
                lrS = ld("sm_lrS", [128, 16], sp["lamS_re"]); liS = ld("sm_liS", [128, 16], sp["lamS_im"])
                ldS = ld("sm_ldS", [128, 16], sp["lamS_dt"])
                cim = ld("sm_cim", [128, 512], sp["c_im"])
                p.barrier()
                _, _, f_re, f_im, kB, _ = self.ssm_discretize(st1, "smB", lrB, liB, ldB, (128, 768))
                t1 = p.sb("sm_pt1", [128, 768], F32, st1)
                t2 = p.sb("sm_pt2", [128, 768], F32, st1)
                kk = ["sm_prep", kB]
                seq = [
                    lambda: nc.vector.tensor_tensor(out=t1[:], in0=bre[:], in1=f_re[:], op=ALU.mult),
                    lambda: nc.vector.tensor_tensor(out=t2[:], in0=bim[:], in1=f_im[:], op=ALU.mult),
                    lambda: nc.vector.tensor_tensor(out=BBre[:].rearrange("p m j -> p (m j)"), in0=t1[:], in1=t2[:], op=ALU.subtract),
                    lambda: nc.vector.tensor_tensor(out=t1[:], in0=bre[:], in1=f_im[:], op=ALU.mult),
                    lambda: nc.vector.tensor_tensor(out=t2[:], in0=bim[:], in1=f_re[:], op=ALU.mult),
                    lambda: nc.vector.tensor_tensor(out=BBim[:].rearrange("p m j -> p (m j)"), in0=t1[:], in1=t2[:], op=ALU.add),
                    lambda: nc.vector.tensor_scalar(out=nCim[:].rearrange("p m j -> p (m j)"), in0=cim[:], scalar1=-1.0, scalar2=None, op0=ALU.mult),
                ]
                for fn in seq:
                    p.op("dve", fn, reads=kk, writes=kk)
                _, _, _, _, kS, (ucs, usn, umag) = self.ssm_discretize(st1, "smS", lrS, liS, ldS, (128, 16))
                kk = ["sm_prep", kS]
                sq1 = p.sb("sm_sq1", [128, 16], F32, st1)
                sq2 = p.sb("sm_sq2", [128, 16], F32, st1)
                p.op("dve", lambda: nc.vector.tensor_copy(out=Are[:, :, 0], in_=ucs[:]), reads=kk, writes=kk)
                p.op("dve", lambda: nc.vector.tensor_copy(out=Aim[:, :, 0], in_=usn[:]), reads=kk, writes=kk)
                p.op("dve", lambda: nc.vector.tensor_copy(out=rho[:], in_=umag[:]), reads=kk, writes=kk)
                for k in range(1, NS):
                    for fn in (
                        lambda k=k: nc.vector.tensor_tensor(out=sq1[:], in0=Are[:, :, k - 1], in1=Are[:, :, k - 1], op=ALU.mult),
                        lambda k=k: nc.vector.tensor_tensor(out=sq2[:], in0=Aim[:, :, k - 1], in1=Aim[:, :, k - 1], op=ALU.mult),
                        lambda k=k: nc.vector.tensor_tensor(out=Are[:, :, k], in0=sq1[:], in1=sq2[:], op=ALU.subtract),
                        lambda k=k: nc.vector.tensor_tensor(out=sq1[:], in0=Are[:, :, k - 1], in1=Aim[:, :, k - 1], op=ALU.mult),
                        lambda k=k: nc.vector.tensor_scalar(out=Aim[:, :, k], in0=sq1[:], scalar1=2.0, scalar2=None, op0=ALU.mult),
                    ):
                        p.op("dve", fn, reads=kk, writes=kk)
                p.op("dve", lambda: nc.vector.tensor_scalar(out=nAim[:], in0=Aim[:], scalar1=-1.0, scalar2=None, op0=ALU.mult),
                     reads=kk, writes=kk)
                p.barrier()
            with ExitStack() as st2:
                xT = p.sb("sm_x", [128, KC, SEQ], BF16, st2)
                W = p.sb("sm_w", [128, KC, 768], BF16, st2)
                acc = [p.ps(f"sm_acc{i}", [128, 512], F32, st2) for i in range(2)]
                self.load_xT(xT, s, "sm_x")
                self.load_w(w_in_l, "ssm_u", W, "sm_w")
                n = 0
                for m in range(6):
                    for tb in range(4):
                        a = acc[n % 2]
                        ak = ("P_smacc", n % 2)
                        n += 1
                        self.proj_fm(W, "sm_w", m * 128, 128, xT, "sm_x", tb * 512, 512, a, ak)
                        p.op("act", lambda a=a, m=m, tb=tb: nc.scalar.copy(out=ufT[:, m, tb * 512:(tb + 1) * 512], in_=a[:]),
                             reads=[ak], writes=[("sm_uf", m)])
                        p.op("dve", lambda a=a, m=m, tb=tb: nc.vector.tensor_copy(out=ubT[:, m, tb * 512:(tb + 1) * 512], in_=a[:]),
                             reads=[ak], writes=[("sm_ub", m)])
                p.barrier()
            glT = p.sb("sm_gl", [128, 6, SEQ], BF16, st)
            with ExitStack() as st3:
                Ere = [p.sb(f"sm_Ere{i}", [128, SEQ], F32, st3) for i in range(2)]
                Eim = [p.sb(f"sm_Eim{i}", [128, SEQ], F32, st3) for i in range(2)]
                Mre = p.sb("sm_Mre", [128, SEQ], F32, st3)
                Mim = p.sb("sm_Mim", [128, SEQ], F32, st3)
                Sre = p.sb("sm_Sre", [128, SEQ], F32, st3)
                Sim = p.sb("sm_Sim", [128, SEQ], F32, st3)
                rhoT = [p.sb("sm_rhoT", [128, SEQ], F32, st3)] * 2
                tm = [p.sb(f"sm_tm{i}", [128, 512], F32, st3) for i in range(2)]
                xbr = p.sb("sm_xbr", [128, SEQ], BF16, st3)
                xbi = p.sb("sm_xbi", [128, SEQ], BF16, st3)
                yvs = [p.sb(f"sm_yv{i}", [128, 512], F32, st3) for i in range(2)]
                y2s = [p.sb(f"sm_y2{i}", [128, 512], F32, st3) for i in range(2)]
                BUr = [p.ps(f"sm_BUr{i}", [128, 512], F32, st3) for i in range(2)]
                BUi = [p.ps(f"sm_BUi{i}", [128, 512], F32, st3) for i in range(2)]
                Yp = [p.ps(f"sm_Y{i}", [128, 512], F32, st3) for i in range(4)]

                def gen_table(sb_):
                    e = sb_ % 2
                    ek = ("sm_E", e)
                    steps = []

                    def init():
                        p.op("pool", lambda: nc.gpsimd.memset(Ere[e][:, 0:1], 1.0), writes=[ek])
                        p.op("pool", lambda: nc.gpsimd.memset(Eim[e][:, 0:1], 0.0), writes=[ek])
                    steps.append(init)
                    for k in range(NS):
                        steps.append(lambda k=k: gen_step(sb_, e, ek, k))
                    return steps

                def gen_step(sb_, e, ek, k):
                    if True:
                        d = 1 << k
                        ur, ui, nui = Are[:, sb_, k:k + 1], Aim[:, sb_, k:k + 1], nAim[:, sb_, k:k + 1]
                        p.op("act", lambda d=d, ur=ur: nc.scalar.activation(out=Ere[e][:, d:2 * d], in_=Ere[e][:, 0:d], func=AF.Copy, scale=ur),
                             reads=[ek, "sm_A"], writes=[ek])
                        p.op("act", lambda d=d, ur=ur: nc.scalar.activation(out=Eim[e][:, d:2 * d], in_=Eim[e][:, 0:d], func=AF.Copy, scale=ur),
                             reads=[ek, "sm_A"], writes=[ek])
                        p.op("dve", lambda d=d, nui=nui: nc.vector.scalar_tensor_tensor(
                            out=Ere[e][:, d:2 * d], in0=Eim[e][:, 0:d], scalar=nui, in1=Ere[e][:, d:2 * d], op0=ALU.mult, op1=ALU.add),
                            reads=[ek, "sm_A"], writes=[ek])
                        p.op("dve", lambda d=d, ui=ui: nc.vector.scalar_tensor_tensor(
                            out=Eim[e][:, d:2 * d], in0=Ere[e][:, 0:d], scalar=ui, in1=Eim[e][:, d:2 * d], op0=ALU.mult, op1=ALU.add),
                            reads=[ek, "sm_A"], writes=[ek])

                for th_ in gen_table(0):
                    th_()
                nbu = 0
                for sb_ in range(16):
                    m, q0 = sb_ // 3, (sb_ % 3) * 32
                    e = sb_ % 2
                    ek = ("sm_E", e)
                    nxt = gen_table(sb_ + 1) if sb_ + 1 < 16 else []
                    for tb in range(4):
                        ts_ = slice(tb * 512, (tb + 1) * 512)
                        bb = nbu % 2
                        nbu += 1
                        BUr_, BUi_ = [BUr[bb]], [BUi[bb]]
                        kr_, ki_ = ("P_smBUr", bb), ("P_smBUi", bb)
                        p.op("pe", lambda ts_=ts_: nc.tensor.matmul(BUr_[0][:], lhsT=BBre[q0:q0 + 32, m, :], rhs=ubT[q0:q0 + 32, m, ts_], start=True, stop=True),
                             reads=["sm_BB", ("sm_ub", m)], writes=[kr_])
                        p.op("pe", lambda ts_=ts_: nc.tensor.matmul(BUi_[0][:], lhsT=BBim[q0:q0 + 32, m, :], rhs=ubT[q0:q0 + 32, m, ts_], start=True, stop=True),
                             reads=["sm_BB", ("sm_ub", m)], writes=[ki_])
                        p.op("dve", lambda ts_=ts_: nc.vector.tensor_tensor(out=tm[0][:], in0=BUi_[0][:], in1=Eim[e][:, ts_], op=ALU.mult),
                             reads=[ki_, ek], writes=[("sm_tm", 0)])
                        p.op("dve", lambda ts_=ts_: nc.vector.tensor_tensor(out=Mre[:, ts_], in0=BUr_[0][:], in1=Ere[e][:, ts_], op=ALU.mult),
                             reads=[kr_, ek], writes=["sm_Mre"])
                        if nxt:
                            nxt.pop(0)()
                        p.op("dve", lambda ts_=ts_: nc.vector.tensor_tensor(out=tm[1][:], in0=BUr_[0][:], in1=Eim[e][:, ts_], op=ALU.mult),
                             reads=[kr_, ek], writes=[("sm_tm", 1)])
                        p.op("dve", lambda ts_=ts_: nc.vector.tensor_tensor(out=Mim[:, ts_], in0=BUi_[0][:], in1=Ere[e][:, ts_], op=ALU.mult),
                             reads=[ki_, ek], writes=["sm_Mim"])
                        if nxt:
                            nxt.pop(0)()
                        p.op("dve", lambda ts_=ts_: nc.vector.tensor_tensor(out=Mre[:, ts_], in0=Mre[:, ts_], in1=tm[0][:], op=ALU.add),
                             reads=["sm_Mre", ("sm_tm", 0)], writes=["sm_Mre"])
                        p.op("dve", lambda ts_=ts_: nc.vector.tensor_tensor(out=Mim[:, ts_], in0=Mim[:, ts_], in1=tm[1][:], op=ALU.subtract),
                             reads=["sm_Mim", ("sm_tm", 1)], writes=["sm_Mim"])
                        if nxt:
                            nxt.pop(0)()
                    p.op("act", lambda: nc.scalar.activation(out=rhoT[e][:], in_=Ere[e][:], func=AF.Identity, scale=0.0, bias=rho[:, sb_:sb_ + 1]),
                         reads=[ek, "sm_A", "sm_Sre", "sm_Sim"], writes=["sm_rhoT"])
                    p.op("dve", lambda: nc.vector.tensor_tensor_scan(out=Sre[:], data0=rhoT[e][:], data1=Mre[:], initial=0.0, op0=ALU.mult, op1=ALU.add),
                         reads=["sm_rhoT", "sm_Mre"], writes=["sm_Sre"])
                    p.op("dve", lambda: nc.vector.tensor_tensor_scan(out=Sim[:], data0=rhoT[e][:], data1=Mim[:], initial=0.0, op0=ALU.mult, op1=ALU.add),
                         reads=["sm_rhoT", "sm_Mim"], writes=["sm_Sim"])
                    p.op("dve", lambda: nc.vector.tensor_tensor(out=Mre[:], in0=Sim[:], in1=Eim[e][:], op=ALU.mult),
                         reads=["sm_Sim", ek, "sm_Mre"], writes=["sm_Mre"])
                    p.op("dve", lambda: nc.vector.tensor_tensor(out=Mim[:], in0=Sre[:], in1=Ere[e][:], op=ALU.mult),
                         reads=["sm_Sre", ek, "sm_Mim"], writes=["sm_Mim"])
                    p.op("dve", lambda: nc.vector.tensor_tensor(out=xbr[:], in0=Mim[:], in1=Mre[:], op=ALU.subtract),
                         reads=["sm_Mre", "sm_Mim"], writes=["sm_xbr"])
                    p.op("dve", lambda: nc.vector.tensor_tensor(out=Mre[:], in0=Sre[:], in1=Eim[e][:], op=ALU.mult),
                         reads=["sm_Sre", ek, "sm_Mre", "sm_xbr"], writes=["sm_Mre"])
                    p.op("dve", lambda: nc.vector.tensor_tensor(out=Mim[:], in0=Sim[:], in1=Ere[e][:], op=ALU.mult),
                         reads=["sm_Sim", ek, "sm_Mim", "sm_xbr"], writes=["sm_Mim"])
                    p.op("dve", lambda: nc.vector.tensor_tensor(out=xbi[:], in0=Mim[:], in1=Mre[:], op=ALU.add),
                         reads=["sm_Mre", "sm_Mim"], writes=["sm_xbi"])
                    while nxt:
                        nxt.pop(0)()
                    for tb in range(4):
                        ts_ = slice(tb * 512, (tb + 1) * 512)
                        p.op("pe", lambda tb=tb, ts_=ts_: nc.tensor.matmul(Yp[tb][q0:q0 + 32, :], lhsT=Cre[:, sb_, :], rhs=xbr[:, ts_],
                                                                           start=True, stop=False),
                             reads=["sm_Cre", "sm_xbr"], writes=[("P_smY", tb)])
                        p.op("pe", lambda tb=tb, ts_=ts_: nc.tensor.matmul(Yp[tb][q0:q0 + 32, :], lhsT=nCim[:, sb_, :], rhs=xbi[:, ts_],
                                                                           start=False, stop=True),
                             reads=["sm_Cre", "sm_xbi"], writes=[("P_smY", tb)])
                    if sb_ % 3 == 2 or sb_ == 15:
                        nr = 96 if m < 5 else 32

                        def gA(tb):
                            ts_ = slice(tb * 512, (tb + 1) * 512)
                            yv_, y2_ = yvs[tb % 2], y2s[tb % 2]
                            kv, k2 = ("sm_yv", tb % 2), ("sm_y2", tb % 2)
                            p.op("dve", lambda: nc.vector.scalar_tensor_tensor(
                                out=yv_[0:nr, :], in0=ufT[0:nr, m, ts_], scalar=dsk[0:nr, m:m + 1], in1=Yp[tb][0:nr, :], op0=ALU.mult, op1=ALU.add),
                                reads=[("sm_uf", m), "sm_d", ("P_smY", tb)], writes=[kv])
                            p.op("dve", lambda: nc.vector.tensor_tensor(out=y2_[0:nr, :], in0=yv_[0:nr, :], in1=yv_[0:nr, :], op=ALU.mult),
                                 reads=[kv], writes=[k2])
                            p.op("dve", lambda: nc.vector.tensor_scalar(out=y2_[0:nr, :], in0=y2_[0:nr, :], scalar1=0.044715, scalar2=1.0,
                                                                        op0=ALU.mult, op1=ALU.add),
                                 reads=[k2], writes=[k2])
                            p.op("dve", lambda: nc.vector.tensor_tensor(out=y2_[0:nr, :], in0=y2_[0:nr, :], in1=yv_[0:nr, :], op=ALU.mult),
                                 reads=[k2, kv], writes=[k2])
                            p.op("act", lambda: nc.scalar.activation(out=y2_[0:nr, :], in_=y2_[0:nr, :], func=AF.Sigmoid, scale=2.0 * math.sqrt(2.0 / math.pi)),
                                 reads=[k2], writes=[k2])

                        def gB(tb):
                            ts_ = slice(tb * 512, (tb + 1) * 512)
                            yv_, y2_ = yvs[tb % 2], y2s[tb % 2]
                            kv, k2 = ("sm_yv", tb % 2), ("sm_y2", tb % 2)
                            p.op("dve", lambda: nc.vector.tensor_tensor(out=glT[0:nr, m, ts_], in0=yv_[0:nr, :], in1=y2_[0:nr, :], op=ALU.mult),
                                 reads=[kv, k2], writes=[("sm_gl", m)])

                        gA(0); gA(1); gB(0); gA(2); gB(1); gA(3); gB(2); gB(3)
                p.barrier()
            Wg = p.sb("sm_Wg", [128, 6, 1024], BF16, st)
            osb = [p.sb(f"sm_o{i}", [128, 512], BF16, st) for i in range(2)]
            sgm = p.sb("sm_sg2", [128, 512], F32, st)
            GA = p.ps("sm_GA", [128, 512], F32, st)
            GB = p.ps("sm_GB", [128, 512], F32, st)
            p.dma("pool", Wg[:], sp["w_glu"], writes=["sm_Wg"], max_dma_last_dim=8192)
            no = 0
            for oc in range(4):
                for tb in range(4):
                    ts_ = slice(tb * 512, (tb + 1) * 512)
                    for acc_, c0, ak in ((GA, oc * 128, "P_smGA"), (GB, 512 + oc * 128, "P_smGB")):
                        for kc in range(6):
                            kr = 96 if kc < 5 else 32
                            p.op("pe", lambda acc_=acc_, c0=c0, kc=kc, ts_=ts_, kr=kr: nc.tensor.matmul(
                                acc_[:], lhsT=Wg[0:kr, kc, c0:c0 + 128], rhs=glT[0:kr, kc, ts_], start=(kc == 0), stop=(kc == 5)),
                                reads=["sm_Wg", ("sm_gl", kc)], writes=[ak])
                    oi = no % 2
                    no += 1
                    p.op("act", lambda: nc.scalar.activation(out=sgm[:], in_=GB[:], func=AF.Sigmoid), reads=["P_smGB"], writes=["sm_sg"])
                    p.op("dve", lambda oi=oi: nc.vector.tensor_tensor(out=osb[oi][:], in0=GA[:], in1=sgm[:], op=ALU.mult),
                         reads=["P_smGA", "sm_sg"], writes=[("sm_o", oi)])
                    tok0 = s * SEQ + tb * 512
                    p.dma("sp", self.YT[6 + oc, :, tok0:tok0 + 512], osb[oi][:], reads=[("sm_o", oi)], writes=[("YT", tok0 // TT)])
            p.barrier()


def t5_bucket_np(dist):
    dist = np.asarray(dist, np.int64)
    d = np.maximum(dist, 1).astype(np.float32)
    large = 16 + (np.log(d / np.float32(16)) / np.float32(math.log(128 / 16)) * np.float32(16)).astype(np.int32)
    large = np.minimum(large, 31)
    return np.where(dist < 16, dist, large)


def lay_dil_bias(rel_bias):
    out = np.full((128, 3, 4, 2, 128), -1e30, np.float32)
    sl = np.arange(128)[:, None]
    tl = np.arange(128)[None, :]
    for g, r in enumerate((1, 4, 16)):
        for kind in range(2):
            dloc = tl - sl + 128 * kind
            valid = (dloc >= 0) & (dloc <= 128)
            bk = t5_bucket_np(np.clip(dloc, 0, None) * r)
            for hh in range(4):
                vals = rel_bias[bk, g * 4 + hh]
                out[:, g, hh, kind, :] = np.where(valid, vals, np.float32(-1e30))
    return out.reshape(128, 24 * 128)


def lay_dsa_bias(rel_bias):
    out = np.full((128, 8, 2, 128), -1e30, np.float32)
    sl = np.arange(128)[:, None]
    tl = np.arange(128)[None, :]
    for kind in range(2):
        d = tl - sl + 128 * kind
        valid = d >= 0
        bk = t5_bucket_np(np.clip(d, 0, None))
        for h in range(8):
            out[:, h, kind, :] = np.where(valid, rel_bias[bk, 12 + h], np.float32(-1e30))
    b31 = np.ascontiguousarray(np.broadcast_to(rel_bias[31, 12:20][None, :], (128, 8))).astype(np.float32)
    return out.reshape(128, 16 * 128), b31


def lay_ssm(lam_re, lam_im, log_dt, b_re, b_im, c_re, c_im, d_skip, w_glu):
    out = {}
    q = np.arange(128)
    qd, gl, c = np.minimum(q // 32, 2), (q % 32) // 16, q % 16
    qvalid = q < 96
    mB = np.arange(6)
    j = np.arange(128)
    glp, pj = j // 64, j % 64
    sbq = 3 * mB[None, :, None] + qd[:, None, None]
    okB = (sbq < 16) & qvalid[:, None, None]
    sbq = np.minimum(sbq, 15)
    gB = 2 * sbq + glp[None, None, :]
    pB = np.broadcast_to(pj[None, None, :], gB.shape)
    out["lamB_re"] = lam_re[gB, pB].reshape(128, 768).astype(np.float32)
    out["lamB_im"] = lam_im[gB, pB].reshape(128, 768).astype(np.float32)
    out["lamB_dt"] = log_dt[gB].reshape(128, 768).astype(np.float32)
    same = (gl[:, None, None] == glp[None, None, :]) & okB
    cB = np.broadcast_to(c[:, None, None], gB.shape)
    out["b_re"] = np.where(same, b_re[gB, pB, cB], 0.0).reshape(128, 768).astype(np.float32)
    out["b_im"] = np.where(same, b_im[gB, pB, cB], 0.0).reshape(128, 768).astype(np.float32)
    sbS = np.arange(16)[None, :]
    gS = 2 * sbS + glp[:, None]
    pS = np.broadcast_to(pj[:, None], gS.shape)
    out["lamS_re"] = lam_re[gS, pS].astype(np.float32)
    out["lamS_im"] = lam_im[gS, pS].astype(np.float32)
    out["lamS_dt"] = log_dt[gS].astype(np.float32)
    ch = np.arange(32)
    glc, cc = ch // 16, ch % 16
    gC = 2 * np.arange(16)[None, :, None] + glp[:, None, None] + 0 * glc[None, None, :]
    sameC = (glp[:, None, None] == glc[None, None, :]) & np.ones((1, 16, 1), bool)
    cC = np.broadcast_to(cc[None, None, :], gC.shape)
    pC = np.broadcast_to(pj[:, None, None], gC.shape)
    out["c_re"] = np.where(sameC, c_re[gC, cC, pC], 0.0).astype(np.float32)
    out["c_im"] = np.where(sameC, c_im[gC, cC, pC], 0.0).reshape(128, 512).astype(np.float32)
    chan = 96 * mB[None, :] + q[:, None]
    okc = (chan < 512) & qvalid[:, None]
    chc = np.minimum(chan, 511)
    out["d"] = np.where(okc, d_skip[chc], 0.0).astype(np.float32)
    out["w_glu"] = np.where(okc[:, :, None], w_glu[chc, :], 0.0).astype(np.float32)
    return out


SSM_SHAPES = {"lamB_re": [128, 768], "lamB_im": [128, 768], "lamB_dt": [128, 768], "b_re": [128, 768], "b_im": [128, 768],
              "lamS_re": [128, 16], "lamS_im": [128, 16], "lamS_dt": [128, 16], "c_re": [128, 16, 32], "c_im": [128, 512],
              "d": [128, 6], "w_glu": [128, 6, 1024]}


def lay_up(w):
    return np.ascontiguousarray(w.reshape(16, 128, 2, 22, 256).transpose(3, 2, 1, 0, 4))


def lay_down(w):
    return np.ascontiguousarray(w.reshape(44, 128, 16, 128).transpose(2, 1, 0, 3))


def lay_br(ws):
    w = np.concatenate(ws, 0).reshape(NYC, 128, 16, 128)
    return np.ascontiguousarray(w.transpose(2, 1, 0, 3))


def lay_sq(w):
    return np.ascontiguousarray(w.reshape(16, 128, 16, 128).transpose(2, 1, 0, 3))


def lay_ln(v):
    return np.ascontiguousarray(v.reshape(6, 16, 128).transpose(2, 0, 1).reshape(128, 96))


N_CORES = 8
TTOT_CORE = 2 * SEQ


def build_full(ttot=TTOT_CORE, depth=DEPTH):
    b = Builder(ttot)
    nseq = ttot // SEQ
    x = b.inp("x", [ttot, D])
    y = b.nc.dram_tensor("y", [ttot, D], F32, kind="ExternalOutput").ap()
    L = []
    for l in range(depth):
        d = {}
        for f in ("ffn1", "ffn2"):
            d[f + "_up"] = b.inp(f"{f}_up{l}", [22, 2, 128, 16, 256])
            d[f + "_down"] = b.inp(f"{f}_down{l}", [16, 128, 44, 128])
        d["w_in"] = b.inp(f"w_in{l}", [128, WIN_TOT])
        d["w_br"] = b.inp(f"w_br{l}", [16, 128, NYC, 128])
        d["w_out"] = b.inp(f"w_out{l}", [16, 128, 16, 128])
        d["ssm"] = {k: b.inp(f"ssm{l}_{k}", SSM_SHAPES[k]) for k in SSM_SHAPES}
        L.append(d)
    dil_bias = b.inp("dil_bias", [128, 24 * 128])
    dsa_bias = b.inp("dsa_bias", [128, 16 * 128])
    dsa_b31 = b.inp("dsa_b31", [128, 8])
    negm = b.inp("cm_negmask", [128, 128])
    b.load_consts()
    b.load_mixer_consts()
    b.p.barrier()
    b.phase_in(x)
    for l in range(depth):
        d = L[l]
        b.phase_ffn(d["ffn1_up"], d["ffn1_down"], l * 3 + 0)
        for s in range(nseq):
            b.phase_sb(d["w_in"], s)
            b.phase_dil(d["w_in"], dil_bias, s)
            b.phase_ssm(d["w_in"], d["ssm"], s)
            b.phase_dsa(d["w_in"], dsa_bias, dsa_b31, negm, s)
        b.phase_merge(d["w_in"], d["w_br"], d["w_out"], l * 3 + 1)
        b.phase_ffn(d["ffn2_up"], d["ffn2_down"], l * 3 + 2)
    b.phase_out(y)
    b.p.finish()
    return b


def host_inputs(inp, depth=DEPTH):
    f = lambda a: np.asarray(a, np.float32)
    sh = {}
    for l in range(depth):
        sh[f"ffn1_up{l}"] = lay_up(f(inp["ffn1_w_up"][l]))
        sh[f"ffn1_down{l}"] = lay_down(f(inp["ffn1_w_down"][l]))
        sh[f"ffn2_up{l}"] = lay_up(f(inp["ffn2_w_up"][l]))
        sh[f"ffn2_down{l}"] = lay_down(f(inp["ffn2_w_down"][l]))
        sh[f"w_in{l}"] = lay_w_in(f(inp["w_in"][l]))
        sh[f"w_br{l}"] = lay_br([f(inp[k][l]) for k in ("w_br_sb", "w_br_dil", "w_br_ssm", "w_br_dsa")])
        sh[f"w_out{l}"] = lay_sq(f(inp["w_out"][l]))
        ss = lay_ssm(f(inp["ssm_lam_re"][l]), f(inp["ssm_lam_im"][l]), f(inp["ssm_log_dt"][l]), f(inp["ssm_b_re"][l]),
                     f(inp["ssm_b_im"][l]), f(inp["ssm_c_re"][l]), f(inp["ssm_c_im"][l]), f(inp["ssm_d"][l]),
                     f(inp["ssm_w_glu"][l]))
        for k, v in ss.items():
            sh[f"ssm{l}_{k}"] = np.ascontiguousarray(v)
    rb = f(inp["rel_bias"])
    sh["dil_bias"] = lay_dil_bias(rb)
    sh["dsa_bias"], sh["dsa_b31"] = lay_dsa_bias(rb)
    sh["ln_g_l"] = lay_ln(f(inp["ln_g"]))
    sh["ln_b_l"] = lay_ln(f(inp["ln_b"]))
    pp = np.arange(128)[:, None]
    sh["cm_ident"] = np.eye(128, dtype=np.float32)
    sh["cm_sbmask"] = (pp < np.arange(512)[None, :]).astype(np.float32)
    sh["cm_triu"] = (pp > np.arange(128)[None, :]).astype(np.float32)
    sh["cm_negmask"] = np.where(np.arange(128)[None, :] <= pp, 0.0, -1e30).astype(np.float32)
    return sh


def kernel(**inputs):
    x = np.asarray(inputs["x"], np.float32)
    bsz, L, _ = x.shape
    assert (bsz, L) == (16, SEQ)
    shared = host_inputs(inputs)
    b = build_full()
    xs = x.reshape(N_CORES, TTOT_CORE, D)
    in_maps = []
    for c in range(N_CORES):
        m = dict(shared)
        m["x"] = np.ascontiguousarray(xs[c])
        in_maps.append({k: v for k, v in m.items() if k in b.dram_in})
    res = run_bass_kernel_spmd(b.nc, in_maps, core_ids=list(range(N_CORES)))
    out = np.stack([np.asarray(r["y"]) for r in res.results], 0)
    return out.reshape(bsz, L, D).astype(np.float32)
```
